# Optimizing a Trainium2 kernel written in Bass

```python
import jax
import jax.numpy as jnp
from jax import lax
import numpy as np

D_MODEL = 1024
BATCH = 16
SEQ = 2048
DEPTH = 1

HEAD_DIM = 64
ATTN_Q_HEADS = 8
ATTN_KV_HEADS = 2
ATTN_GROUP = ATTN_Q_HEADS // ATTN_KV_HEADS
WINDOW = 128
BLOCK = 128
RWKV_HEADS = 8
RWKV_HEAD_SIZE = 64
LORA_DECAY = 64
LORA_AAA = 64
LORA_GATE = 128
D_FF = 4 * D_MODEL
ATTN_Q_WIDTH = ATTN_Q_HEADS * HEAD_DIM
ATTN_KV_WIDTH = ATTN_KV_HEADS * HEAD_DIM
RWKV_WIDTH = RWKV_HEADS * RWKV_HEAD_SIZE
SPLIT_SIZES = (ATTN_Q_WIDTH, ATTN_KV_WIDTH, ATTN_KV_WIDTH, RWKV_WIDTH, RWKV_WIDTH, RWKV_WIDTH, LORA_DECAY, LORA_AAA, LORA_GATE, D_MODEL, D_MODEL)
IN_WIDTH = ATTN_Q_WIDTH + 2 * ATTN_KV_WIDTH + 3 * RWKV_WIDTH + LORA_DECAY + LORA_AAA + LORA_GATE + 2 * D_MODEL
RMS_EPS = 1e-6
GN_EPS = 64e-5
L2_EPS = 1e-12

kernel_name = "hybrid_swa_sink_rwkv7_gated_block"


def rmsnorm(z, gain):
    zf = z.astype(jnp.float32)
    return zf * lax.rsqrt(jnp.mean(zf * zf, axis=-1, keepdims=True) + RMS_EPS) * gain.astype(jnp.float32)


def token_shift(z, mu):
    prev = jnp.pad(z, ((0, 0), (1, 0), (0, 0)))[:, :-1]
    return z + (prev - z) * mu


def sliding_window_attention(q, k, v, q_gain, k_gain, sinks):
    b, t, _ = q.shape
    nb = t // BLOCK
    q = rmsnorm(q.reshape(b, nb, BLOCK, ATTN_KV_HEADS, ATTN_GROUP, HEAD_DIM), q_gain)
    k = rmsnorm(k.reshape(b, nb, BLOCK, ATTN_KV_HEADS, HEAD_DIM), k_gain)
    v = v.astype(jnp.float32).reshape(b, nb, BLOCK, ATTN_KV_HEADS, HEAD_DIM)

    def with_prev(z):
        prev = jnp.concatenate([jnp.zeros_like(z[:, :1]), z[:, :-1]], axis=1)
        return jnp.concatenate([prev, z], axis=2)

    kw, vw = with_prev(k), with_prev(v)
    scores = jnp.einsum('bnqkgd,bnskd->bnkgqs', q, kw) * (HEAD_DIM ** -0.5)
    qpos = jnp.arange(BLOCK)[:, None] + BLOCK
    spos = jnp.arange(2 * BLOCK)[None, :]
    rel = qpos - spos
    band = (rel >= 0) & (rel < WINDOW)
    exists = (spos >= BLOCK) | (jnp.arange(nb)[:, None, None] > 0)
    valid = (band[None] & exists)[None, :, None, None]
    scores = jnp.where(valid, scores, -jnp.inf)
    sink = sinks.astype(jnp.float32).reshape(1, 1, ATTN_KV_HEADS, ATTN_GROUP, 1, 1)
    m = jnp.maximum(jnp.max(scores, axis=-1, keepdims=True), sink)
    p = jnp.exp(scores - m)
    probs = p / (jnp.sum(p, axis=-1, keepdims=True) + jnp.exp(sink - m))
    out = jnp.einsum('bnkgqs,bnskd->bnqkgd', probs, vw)
    return out.reshape(b, t, ATTN_Q_WIDTH)


def wkv7_scan(r, w, k, v, a, b):
    def step(S, inp):
        r_t, w_t, k_t, v_t, a_t, b_t = inp
        sa = jnp.einsum('bhij,bhj->bhi', S, a_t)
        S = S * w_t[:, :, None, :] + sa[..., None] * b_t[:, :, None, :] + v_t[..., None] * k_t[:, :, None, :]
        return S, jnp.einsum('bhij,bhj->bhi', S, r_t)

    bsz, _, h, n = r.shape
    xs = tuple(jnp.swapaxes(z, 0, 1) for z in (r, w, k, v, a, b))
    S0 = jnp.zeros((bsz, h, n, n), jnp.float32)
    _, ys = lax.scan(step, S0, xs)
    return jnp.swapaxes(ys, 0, 1)


def heads(z):
    return z.reshape(*z.shape[:-1], RWKV_HEADS, RWKV_HEAD_SIZE)


def rwkv7_time_mix(r_p, k_p, v_p, w_p, a_p, g_p, mu_r, mu_k, mu_v, mu_w, mu_a, mu_g,
                   decay_bias, decay_up, aaa_bias, aaa_up, gate_up, k_k, k_a, r_k,
                   ln_x_gain, ln_x_bias):
    b, t, _ = r_p.shape
    r = token_shift(r_p, mu_r)
    k = token_shift(k_p, mu_k)
    v = token_shift(v_p, mu_v)
    w_log = -jax.nn.softplus(-(decay_bias + jnp.tanh(token_shift(w_p, mu_w)) @ decay_up)) - 0.5
    decay = jnp.exp(-jnp.exp(w_log.astype(jnp.float32)))
    a = jax.nn.sigmoid(aaa_bias + token_shift(a_p, mu_a) @ aaa_up)
    g = jax.nn.sigmoid(token_shift(g_p, mu_g)) @ gate_up
    kk = heads(k * k_k).astype(jnp.float32)
    kk = kk / jnp.maximum(jnp.sqrt(jnp.sum(kk * kk, axis=-1, keepdims=True)), L2_EPS)
    k = k * (1.0 + (a - 1.0) * k_a)
    r_h = heads(r).astype(jnp.float32)
    k_h = heads(k).astype(jnp.float32)
    v_h = heads(v).astype(jnp.float32)
    a_h = heads(a).astype(jnp.float32)
    y = wkv7_scan(r_h, heads(decay), k_h, v_h, -kk, kk * a_h)
    mean = jnp.mean(y, axis=-1, keepdims=True)
    var = jnp.mean(jnp.square(y - mean), axis=-1, keepdims=True)
    y = ((y - mean) * lax.rsqrt(var + GN_EPS)).reshape(b, t, RWKV_WIDTH) * ln_x_gain + ln_x_bias
    bonus = jnp.sum(r_h * k_h * r_k.reshape(RWKV_HEADS, RWKV_HEAD_SIZE), axis=-1, keepdims=True) * v_h
    return (y + bonus.reshape(b, t, RWKV_WIDTH)) * g


def hybrid_layer(h, norm1_gain, w_in, q_norm_gain, k_norm_gain, attn_sinks,
                 mu_r, mu_k, mu_v, mu_w, mu_a, mu_g, decay_bias, decay_up, aaa_bias, aaa_up,
                 gate_up, k_k, k_a, r_k, ln_x_gain, ln_x_bias, w_branch_attn, w_branch_rwkv,
                 w_out, norm2_gain, w_ff_in, w_ff_out):
    u = rmsnorm(h, norm1_gain)
    proj = u @ w_in
    split_at = np.cumsum(SPLIT_SIZES)[:-1].tolist()
    q, k, v, r_p, k_p, v_p, w_p, a_p, g_p, gate_a, gate_b = jnp.split(proj, split_at, axis=-1)
    y_attn = sliding_window_attention(q, k, v, q_norm_gain, k_norm_gain, attn_sinks) @ w_branch_attn
    y_rwkv = rwkv7_time_mix(r_p, k_p, v_p, w_p, a_p, g_p, mu_r, mu_k, mu_v, mu_w, mu_a, mu_g,
                            decay_bias, decay_up, aaa_bias, aaa_up, gate_up, k_k, k_a, r_k,
                            ln_x_gain, ln_x_bias) @ w_branch_rwkv
    mixed = jax.nn.sigmoid(gate_a) * y_attn + jax.nn.sigmoid(gate_b) * y_rwkv
    h = h + mixed @ w_out
    hidden = jnp.square(jax.nn.relu(rmsnorm(h, norm2_gain) @ w_ff_in))
    return h + hidden @ w_ff_out


def setup_inputs(seed: int = 0) -> dict:
    key = jax.random.key(seed)
    ks = jax.random.split(key, 28)
    f32 = jnp.float32

    def dense(k, fan_in, fan_out, scale=1.0):
        return jax.random.normal(k, (DEPTH, fan_in, fan_out), f32) * (scale * fan_in ** -0.5)

    def vec(k, n, mean, std):
        return mean + std * jax.random.normal(k, (DEPTH, n), f32)

    def unif(k, n):
        return jax.random.uniform(k, (DEPTH, n), f32)

    return {
        "x": jax.random.normal(ks[0], (BATCH, SEQ, D_MODEL), f32),
        "norm1_gain": vec(ks[1], D_MODEL, 1.0, 0.02),
        "w_in": dense(ks[2], D_MODEL, IN_WIDTH),
        "q_norm_gain": vec(ks[3], HEAD_DIM, 1.0, 0.02),
        "k_norm_gain": vec(ks[4], HEAD_DIM, 1.0, 0.02),
        "attn_sinks": vec(ks[5], ATTN_Q_HEADS, 0.0, 0.5),
        "mu_r": unif(ks[6], RWKV_WIDTH),
        "mu_k": unif(ks[7], RWKV_WIDTH),
        "mu_v": unif(ks[8], RWKV_WIDTH),
        "mu_w": unif(ks[9], LORA_DECAY),
        "mu_a": unif(ks[10], LORA_AAA),
        "mu_g": unif(ks[11], LORA_GATE),
        "decay_bias": vec(ks[12], RWKV_WIDTH, -1.0, 0.5),
        "decay_up": dense(ks[13], LORA_DECAY, RWKV_WIDTH, 0.5),
        "aaa_bias": vec(ks[14], RWKV_WIDTH, 0.0, 0.1),
        "aaa_up": dense(ks[15], LORA_AAA, RWKV_WIDTH),
        "gate_up": dense(ks[16], LORA_GATE, RWKV_WIDTH),
        "k_k": vec(ks[17], RWKV_WIDTH, 0.85, 0.05),
        "k_a": vec(ks[18], RWKV_WIDTH, 1.0, 0.05),
        "r_k": vec(ks[19], RWKV_WIDTH, 0.0, 0.1),
        "ln_x_gain": vec(ks[20], RWKV_WIDTH, 1.0, 0.02),
        "ln_x_bias": vec(ks[21], RWKV_WIDTH, 0.0, 0.02),
        "w_branch_attn": dense(ks[22], ATTN_Q_WIDTH, D_MODEL),
        "w_branch_rwkv": dense(ks[23], RWKV_WIDTH, D_MODEL),
        "w_out": dense(ks[24], D_MODEL, D_MODEL),
        "norm2_gain": vec(ks[25], D_MODEL, 1.0, 0.02),
        "w_ff_in": dense(ks[26], D_MODEL, D_FF),
        "w_ff_out": dense(ks[27], D_FF, D_MODEL),
    }


def reference(x, norm1_gain, w_in, q_norm_gain, k_norm_gain, attn_sinks,
              mu_r, mu_k, mu_v, mu_w, mu_a, mu_g, decay_bias, decay_up, aaa_bias, aaa_up,
              gate_up, k_k, k_a, r_k, ln_x_gain, ln_x_bias, w_branch_attn, w_branch_rwkv,
              w_out, norm2_gain, w_ff_in, w_ff_out):
    h = x.astype(jnp.float32)
    for l in range(DEPTH):
        h = hybrid_layer(h, norm1_gain[l], w_in[l], q_norm_gain[l], k_norm_gain[l], attn_sinks[l],
                         mu_r[l], mu_k[l], mu_v[l], mu_w[l], mu_a[l], mu_g[l],
                         decay_bias[l], decay_up[l], aaa_bias[l], aaa_up[l], gate_up[l],
                         k_k[l], k_a[l], r_k[l], ln_x_gain[l], ln_x_bias[l],
                         w_branch_attn[l], w_branch_rwkv[l], w_out[l],
                         norm2_gain[l], w_ff_in[l], w_ff_out[l])
    return h.astype(x.dtype)
```

```python
import numpy as np
from contextlib import ExitStack, suppress
import concourse.bass as bass
import concourse.mybir as mybir
from concourse.bass_utils import run_bass_kernel_spmd

F32 = mybir.dt.float32
BF16 = mybir.dt.bfloat16
ALU = mybir.AluOpType
AF = mybir.ActivationFunctionType

SEM_ROT = 6000
NCORES = 8
SEQ = 2048
DM = 1024
NB = 2
IN_W = 4608
DFF = 4096
RMS_EPS = 1e-6
GN_EPS = 64e-5
DECAY_C = -0.6065306597126334

PC = {}
_o = 0
for _n, _w in [("g1", 8), ("g2", 8), ("mu_r", 4), ("mu_k", 4), ("mu_v", 4), ("mu2", 2),
               ("db", 4), ("ab", 4), ("k_k", 4), ("k_a", 4), ("r_k", 4), ("lng", 4), ("lnb", 4),
               ("qg", 1), ("kg", 1)]:
    PC[_n] = (_o, _w)
    _o += _w
NPRM = _o


class Buf:
    __slots__ = ("name", "w", "r")

    def __init__(self, name):
        self.name = name
        self.w = None
        self.r = {}


class Counter:
    def __init__(self, prog, name, step):
        self.prog = prog
        self.name = name
        self.step = step
        self.sems = []
        self.idx = -1
        self.cnt = 0
        self._new_sem()
        prog.counters.append(self)

    def _new_sem(self):
        s = self.prog.stack.enter_context(self.prog.nc.semaphore(f"{self.name}_{len(self.sems)}"))
        self.sems.append(s)
        self.idx += 1
        self.cnt = 0

    def next_token(self):
        if self.cnt >= SEM_ROT * self.step:
            self._new_sem()
        self.cnt += self.step
        return (self, self.idx, self.cnt)

    def peek_token(self):
        if self.cnt >= SEM_ROT * self.step:
            self._new_sem()
        return (self, self.idx, self.cnt + self.step)


class Eng:
    def __init__(self, prog, name, handle, compute=True):
        self.name = name
        self.h = handle
        self.ctr = Counter(prog, name, 1) if compute else None
        self.known = {}
        self.nwait = 0
        self.ninst = 0


class Prog:
    def __init__(self, nc, stack):
        self.nc = nc
        self.stack = stack
        self.counters = []
        self.pe = Eng(self, "pe", nc.tensor)
        self.act = Eng(self, "act", nc.scalar)
        self.dve = Eng(self, "dve", nc.vector)
        self.pool = Eng(self, "pool", nc.gpsimd)
        self.sp = Eng(self, "sp", nc.sync, compute=False)
        self.engs = [self.pe, self.act, self.dve, self.pool, self.sp]
        self.pe_pending = False

    def dsem(self, name):
        return Counter(self, name, 16)

    def _wait(self, eng, tok):
        ctr, idx, cnt = tok
        key = (id(ctr), idx)
        if eng.known.get(key, 0) >= cnt:
            return
        if ctr is self.pe.ctr and self.pe_pending and (idx, cnt) == (ctr.idx, ctr.cnt + 1):
            raise RuntimeError("waiting on a pending (unsignalled) PE token")
        eng.h.wait_ge(ctr.sems[idx], cnt)
        eng.known[key] = cnt
        eng.nwait += 1

    def _deps(self, eng, reads, writes):
        toks = []
        for b in reads:
            if b.w is not None:
                toks.append(b.w)
        for b in writes:
            if b.w is not None:
                toks.append(b.w)
            toks.extend(b.r.values())
        for t in toks:
            if eng is self.pe and t[0] is self.pe.ctr:
                continue
            self._wait(eng, t)

    def op(self, eng, fn, reads=(), writes=(), sig=True):
        self._deps(eng, reads, writes)
        ins = fn(eng.h)
        eng.ninst += 1
        if sig:
            tok = eng.ctr.next_token()
            ins.then_inc(tok[0].sems[tok[1]], 1)
            if eng is self.pe:
                self.pe_pending = False
        else:
            assert eng is self.pe
            tok = eng.ctr.peek_token()
            self.pe_pending = True
        for b in reads:
            b.r[eng.name] = tok
        for b in writes:
            b.w = tok
            b.r = {}
        return tok

    def dma(self, eng, out, in_, ds, reads=(), writes=(), **kw):
        self._deps(eng, reads, writes)
        tok = ds.next_token()
        eng.h.dma_start(out=out, in_=in_, **kw).then_inc(tok[0].sems[tok[1]], 16)
        eng.ninst += 1
        for b in reads:
            b.r[id(ds)] = tok
        for b in writes:
            b.w = tok
            b.r = {}
        return tok

    def mm(self, out, lhsT, rhs, start, stop, reads, writes, sig=None):
        if sig is None:
            sig = stop
        return self.op(self.pe, lambda h: h.matmul(out, lhsT, rhs, start=start, stop=stop),
                       reads, writes, sig=sig)

    def tr(self, out, in_, ident, reads, writes, sig=True):
        return self.op(self.pe, lambda h: h.transpose(out, in_, ident), reads, writes, sig=sig)

    def barrier(self):
        assert not self.pe_pending
        for e in self.engs:
            for c in self.counters:
                if e.ctr is c and e is self.pe:
                    continue
                if c.cnt > 0:
                    self._wait(e, (c, c.idx, c.cnt))

    def stats(self):
        return {e.name: (e.ninst, e.nwait) for e in self.engs}


class T:
    def __init__(self, h, name):
        self.h = h
        self.b = Buf(name)
        self.bs = [self.b]

    def __getitem__(self, k):
        return self.h[k]


def bc(ap, shape):
    return ap.broadcast_to(list(shape))


class _Stop(Exception):
    pass


def build_program(nsteps=SEQ // 128, dbg=False, upto=99):
    nc = bass.Bass("TRN2", target_bir_lowering=False)
    dt_in = lambda name, shape: nc.dram_tensor(name, list(shape), F32, kind="ExternalInput").ap()
    x_d = dt_in("x", [NB, SEQ, DM])
    prm_d = dt_in("prm", [128, NPRM])
    sinks_d = dt_in("attn_sinks", [1, 8])
    w_in_d = dt_in("w_in", [DM, IN_W])
    decay_up_d = dt_in("decay_up", [64, 512])
    aaa_up_d = dt_in("aaa_up", [64, 512])
    gate_up_d = dt_in("gate_up", [128, 512])
    w_ba_d = dt_in("w_branch_attn", [512, DM])
    w_br_d = dt_in("w_branch_rwkv", [512, DM])
    w_out_d = dt_in("w_out", [DM, DM])
    w1_d = dt_in("w_ff_in", [DM, DFF])
    w2_d = dt_in("w_ff_out", [DFF, DM])
    y_d = nc.dram_tensor("y", [NB, SEQ, DM], F32, kind="ExternalOutput").ap()
    h_d = nc.dram_tensor("hbuf", [NB, SEQ, DM], F32, kind="Internal").ap()
    dbg_d = {}

    with ExitStack() as st0:
        P = Prog(nc, st0)
        pe, act, dve, pool, sp = P.pe, P.act, P.dve, P.pool, P.sp

        def chk(stage):
            if stage > upto:
                raise _Stop()

        def mk(st, name, shape, dt=F32):
            return T(st.enter_context(nc.sbuf_tensor("s_" + name, list(shape), dt)), name)

        banks = [T(st0.enter_context(nc.psum_tensor(f"bank{i}", [128, 512], F32)), f"bank{i}") for i in range(8)]
        free_banks = list(banks)

        def bank():
            if not free_banks:
                raise RuntimeError("out of PSUM banks")
            return free_banks.pop(0)

        def rel(b):
            assert b not in free_banks
            free_banks.append(b)

        def bf(bk):
            return bk.h[:].bitcast(BF16)

        prm = mk(st0, "prm", [128, NPRM])
        der = mk(st0, "der", [128, 16])
        esink = mk(st0, "esink", [128, 8])
        ident = mk(st0, "ident", [128, 128], BF16)
        ones64 = mk(st0, "ones64", [128, 128], BF16)
        ones1 = mk(st0, "ones1", [128, 128], BF16)
        onesf = mk(st0, "onesf", [128, 128], F32)
        m_lt = mk(st0, "m_lt", [128, 128], BF16)
        m_le = mk(st0, "m_le", [128, 128], BF16)
        m_gt = mk(st0, "m_gt", [128, 128], BF16)
        mask4 = mk(st0, "mask4", [128, 4, 128], BF16)
        stage = [mk(st0, f"stage{i}", [128, 512]) for i in range(1)]
        d_prm = P.dsem("d_prm")
        d_stage = [P.dsem(f"d_stage{i}") for i in range(1)]
        d_x = [P.dsem(f"d_x{i}") for i in range(2)]
        d_st = [P.dsem(f"d_st{i}") for i in range(2)]
        sstate = {"i": 0}

        def pcol(name, j=0, n=1):
            o, w = PC[name]
            return prm[:, o + j:o + j + n]

        P.dma(sp, prm[:], prm_d[:, :], d_prm, writes=[prm.b])
        d_snk = P.dsem("d_snk")
        P.dma(sp, esink[:], bc(sinks_d[0:1, :], [128, 8]), d_snk, writes=[esink.b])
        V = P.op
        V(pool, lambda h: h.memset(onesf[:], 1.0), writes=[onesf.b])
        for t_, cmp_ in ((ident, ALU.is_equal), (m_le, ALU.is_ge), (m_lt, ALU.is_gt)):
            V(pool, lambda h, t_=t_: h.memset(t_[:], 1.0), writes=[t_.b])
            V(pool, lambda h, t_=t_, cmp_=cmp_: h.affine_select(out=t_[:], in_=t_[:], pattern=[[1, 128]],
                                                                compare_op=cmp_, fill=0.0, base=0,
                                                                channel_multiplier=-1),
              reads=[t_.b], writes=[t_.b])
        V(pool, lambda h: h.memset(m_gt[:], 1.0), writes=[m_gt.b])
        V(pool, lambda h: h.affine_select(out=m_gt[:], in_=m_gt[:], pattern=[[-1, 128]], compare_op=ALU.is_gt,
                                          fill=0.0, base=0, channel_multiplier=1),
          reads=[m_gt.b], writes=[m_gt.b])
        for t_, val in ((ones64, 1.0 / 64), (ones1, 1.0)):
            V(pool, lambda h, t_=t_: h.memset(t_[:], 0.0), writes=[t_.b])
            V(pool, lambda h, t_=t_, val=val: h.memset(t_[0:64, 0:64], val), reads=[t_.b], writes=[t_.b])
            V(pool, lambda h, t_=t_, val=val: h.memset(t_[64:128, 64:128], val), reads=[t_.b], writes=[t_.b])
        for j in range(4):
            src = m_lt if j % 2 == 0 else m_le
            V(pool, lambda h, j=j, src=src: h.tensor_copy(out=mask4[:, j, :], in_=src[:]),
              reads=[src.b], writes=[mask4.b])
        V(dve, lambda h: h.tensor_scalar(out=der[:, 0:4], in0=pcol("k_a", 0, 4), scalar1=-1.0, scalar2=1.0,
                                         op0=ALU.mult, op1=ALU.add), reads=[prm.b], writes=[der.b])
        V(dve, lambda h: h.scalar_tensor_tensor(out=der[:, 4:5], in0=pcol("qg"), scalar=0.125, in1=pcol("kg"),
                                                op0=ALU.mult, op1=ALU.mult), reads=[prm.b, der.b], writes=[der.b])
        V(act, lambda h: h.activation(out=esink[:], in_=esink[:], func=AF.Exp), reads=[esink.b], writes=[esink.b])

        cast_rr = {"i": 0}

        wsem = {}
        stg_state = {"i": 0, "e": 0}

        def load_cast_sp(dst_buf, dst_ap, src_ap, ncols, stgs, dsems):
            c0 = 0
            while c0 < ncols:
                i = stg_state["i"] % len(stgs)
                stg_state["i"] += 1
                sg = stgs[i]
                w = min(int(sg.h.shape[-1]), ncols - c0)
                P.dma(sp, sg[:, 0:w], src_ap[:, c0:c0 + w], dsems[i], writes=[sg.b])
                if stg_state["e"] % 2 == 0:
                    V(dve, lambda h, sg=sg, c0=c0, w=w: h.tensor_copy(out=dst_ap[:, c0:c0 + w], in_=sg[:, 0:w]),
                      reads=[sg.b], writes=[dst_buf])
                else:
                    V(act, lambda h, sg=sg, c0=c0, w=w: h.copy(out=dst_ap[:, c0:c0 + w], in_=sg[:, 0:w]),
                      reads=[sg.b], writes=[dst_buf])
                stg_state["e"] += 1
                c0 += w

        def load_cast(dst_t, dst_ap, src_ap, ncols):
            if dst_t.b.name not in wsem:
                wsem[dst_t.b.name] = P.dsem("dw_" + dst_t.b.name)
            P.dma(pool, dst_ap, src_ap, wsem[dst_t.b.name], writes=[dst_t.b])

        with ExitStack() as st1, suppress(_Stop):
            w_in_sb = mk(st1, "w_in_sb", [128, 8, IN_W], BF16)
            wkd = mk(st1, "wkd", [128, 8, 256], BF16)
            w_ba = mk(st1, "w_ba", [128, 4, DM], BF16)
            w_br = mk(st1, "w_br", [128, 4, DM], BF16)
            w_out_sb = mk(st1, "w_out_sb", [128, 8, DM], BF16)
            lora_wa = mk(st1, "lora_wa", [128, 512], BF16)
            lora_g = mk(st1, "lora_g", [128, 512], BF16)

            W_IN_SPLIT = True
            for kc in range(8):
                if not (W_IN_SPLIT and kc % 2 == 1):
                    load_cast(w_in_sb, w_in_sb[:, kc, :], w_in_d[kc * 128:(kc + 1) * 128, :], IN_W)
            for kc in range(4):
                load_cast(w_ba, w_ba[:, kc, :], w_ba_d[kc * 128:(kc + 1) * 128, :], DM)
                load_cast(w_br, w_br[:, kc, :], w_br_d[kc * 128:(kc + 1) * 128, :], DM)
            for kc in range(8):
                load_cast(w_out_sb, w_out_sb[:, kc, :], w_out_d[kc * 128:(kc + 1) * 128, :], DM)
            i = 0
            P.dma(sp, stage[i][0:64, 0:512], decay_up_d[:, :], d_stage[i], writes=[stage[i].b])
            P.dma(sp, stage[i][64:128, 0:512], aaa_up_d[:, :], d_stage[i], writes=[stage[i].b])
            V(dve, lambda h, i=i: h.tensor_copy(out=lora_wa[:], in_=stage[i][:, 0:512]),
              reads=[stage[i].b], writes=[lora_wa.b])
            load_cast(lora_g, lora_g[:], gate_up_d[:, :], 512)

            xt = [mk(st1, f"xt{i}", [128, DM]) for i in range(2)]
            xn = mk(st1, "xn", [128, DM], BF16)
            sm = mk(st1, "sm", [128, 32])
            uT = [mk(st1, f"uT{i}", [128, 8, 128], BF16) for i in range(2)]
            qsq = mk(st1, "qsq", [128, 6, 128], BF16)
            rq = mk(st1, "rq", [128, 6, 128])
            qn = mk(st1, "qn", [128, 4, 128], BF16)
            kn = [mk(st1, f"kn{i}", [128, 2, 2, 128], BF16) for i in range(2)]
            vext = [mk(st1, f"vext{i}", [128, 2, 65], BF16) for i in range(2)]
            pt = [mk(st1, f"pt{i}", [128, 4, 128], BF16) for i in range(2)]
            ao = mk(st1, "ao", [128, 8, 64], BF16)
            aoT = [mk(st1, f"aoT{i}", [128, 4, 128], BF16) for i in range(2)]
            rkvp = mk(st1, "rkvp", [128, 12, 129])
            lorap = mk(st1, "lorap", [128, 2, 129])
            FS = [mk(st1, f"fs{i}", [128, 4, 128]) for i in range(10)]
            r_, k_, v_, lw, a_, g_, kk, Lc, t1, t2 = FS
            kmod = k_
            e1 = t2
            sgt = mk(st1, "sgt", [128, 4, 128])
            zs = mk(st1, "zs", [128, 2, 128])
            zt = mk(st1, "zt", [128, 2, 128])
            twz = mk(st1, "twz", [128, 2, 128], BF16)
            sgb = mk(st1, "sgb", [128, 128], BF16)
            arz = mk(st1, "arz", [128, 4, 2, 2, 128], BF16)
            bk = mk(st1, "bk", [128, 4, 2, 128], BF16)
            hat = mk(st1, "hat", [128, 4, 2, 128], BF16)
            kbh = mk(st1, "kbh", [128, 4, 2, 128], BF16)
            vtok = mk(st1, "vtok", [128, 4, 128], BF16)
            hb16 = mk(st1, "hb16", [128, 4, 128], BF16)
            ssb = mk(st1, "ssb", [128, 8, 4, 128], BF16)
            Xs = mk(st1, "Xs", [128, 8, 128], BF16)
            Ys = mk(st1, "Ys", [128, 8, 128], BF16)
            TTs = mk(st1, "TTs", [128, 8, 128], BF16)
            gbf = mk(st1, "gbf", [128, 8, 64], BF16)
            ubf = mk(st1, "ubf", [128, 8, 64], BF16)
            Hs = mk(st1, "Hs", [128, 4, 64])
            Hb = mk(st1, "Hb", [128, 4, 64], BF16)
            wcs = mk(st1, "wcs", [128, 4, 1])
            ywT = mk(st1, "ywT", [128, 4, 128], BF16)
            mixT = mk(st1, "mixT", [128, 8, 128], BF16)
            print("sbuf remaining after phase1 alloc:", nc.sbuf_bytes_remaining)

            if W_IN_SPLIT:
                w_in_sb.b2 = Buf("w_in_odd")
                w_in_sb.bs = [w_in_sb.b, w_in_sb.b2]
                for kc in range(1, 8, 2):
                    load_cast_sp(w_in_sb.b2, w_in_sb[:, kc, :], w_in_d[kc * 128:(kc + 1) * 128, :], IN_W, xt, d_x)
            for kc in range(8):
                for g in range(2):
                    for dup in range(2):
                        V(pool, lambda h, kc=kc, g=g, dup=dup: h.tensor_copy(
                            out=wkd[:, kc, g * 128 + dup * 64:g * 128 + dup * 64 + 64],
                            in_=w_in_sb[:, kc, 512 + g * 64:512 + g * 64 + 64]),
                          reads=[*w_in_sb.bs], writes=[wkd.b])
            for i in range(2):
                V(pool, lambda h, i=i: h.memset(vext[i][:], 1.0), writes=[vext[i].b])
                V(pool, lambda h, i=i: h.memset(kn[i][:], 0.0), writes=[kn[i].b])
            V(pool, lambda h: h.memset(twz[:], 0.0), writes=[twz.b])
            V(pool, lambda h: h.memset(arz[:], 0.0), writes=[arz.b])

            def load_x(b, n, slot):
                P.dma(sp, xt[slot][:], x_d[b, n * 128:(n + 1) * 128, :], d_x[slot], writes=[xt[slot].b])

            steps = [(b, n) for b in range(NB) for n in range(nsteps)]

            def rsqrt_inplace(t, ap_fn, src_ap_fn, src_reads, scale, bias):
                V(act, lambda h: h.activation(out=ap_fn(), in_=src_ap_fn(), func=AF.Ln, bias=bias, scale=scale),
                  reads=src_reads, writes=[t.b])
                V(act, lambda h: h.activation(out=ap_fn(), in_=ap_fn(), func=AF.Exp, scale=-0.5),
                  reads=[t.b], writes=[t.b])

            def sigmoid_inplace(t, ap_fn, src_ap_fn, src_reads, scale=-1.0):
                V(act, lambda h: h.activation(out=ap_fn(), in_=src_ap_fn(), func=AF.Exp, scale=scale),
                  reads=src_reads, writes=[t.b])
                V(act, lambda h: h.activation(out=ap_fn(), in_=ap_fn(), func=AF.Ln, bias=1.0),
                  reads=[t.b], writes=[t.b])
                V(act, lambda h: h.activation(out=ap_fn(), in_=ap_fn(), func=AF.Exp, scale=-1.0),
                  reads=[t.b], writes=[t.b])

            def v4(bk_):
                return bk_.h[:].rearrange("p (c t) -> p c t", c=4)

            def mucol(name):
                o, w = PC[name]
                return bc(prm[:, o:o + w].unsqueeze(2), [128, w, 128])

            def gen_front(si):
                b, n = steps[si]
                slot = si % 2
                par = n % 2
                first = (n == 0)
                xcur = xt[slot]
                uTc = uT[slot]
                load_x(b, n, slot)
                yield
                V(dve, lambda h: h.memset(sm[:, 0:1], 0.0), writes=[sm.b])
                V(act, lambda h: h.activation(out=xn[:], in_=xcur[:], func=AF.Square, accum_out=sm[:, 0:1]),
                  reads=[xcur.b, sm.b], writes=[xn.b, sm.b])
                rsqrt_inplace(sm, lambda: sm[:, 0:1], lambda: sm[:, 0:1], [sm.b], 1.0 / DM, RMS_EPS)
                V(dve, lambda h: h.tensor_scalar(out=xn[:], in0=xcur[:], scalar1=sm[:, 0:1], scalar2=None,
                                                 op0=ALU.mult), reads=[xcur.b, sm.b], writes=[xn.b])
                yield
                yield
                bk_ = bank()
                for c in range(8):
                    P.tr(bf(bk_)[:, c * 128:(c + 1) * 128], xn[:, c * 128:(c + 1) * 128], ident[:],
                         reads=[xn.b, ident.b], writes=[bk_.b], sig=(c == 7))
                o, _ = PC["g1"]
                V(dve, lambda h: h.tensor_tensor(out=uTc[:], in0=bf(bk_).rearrange("p (c t) -> p c t", c=8),
                                                 in1=bc(prm[:, o:o + 8].unsqueeze(2), [128, 8, 128]), op=ALU.mult),
                  reads=[bk_.b, prm.b], writes=[uTc.b])
                rel(bk_)
                yield

                def proj(out_ap, bkx, wt, col0, sig):
                    for kc in range(8):
                        P.mm(out_ap, wt[:, kc, col0:col0 + 128], uTc[:, kc, :], kc == 0, kc == 7,
                             reads=[*wt.bs, uTc.b], writes=[bkx.b], sig=(sig and kc == 7))

                for qi in range(3):
                    br_ = bank()
                    for c in range(4):
                        proj(v4(br_)[:, c, :], br_, w_in_sb, 768 + (qi * 4 + c) * 128, c == 3)
                        if c == 1:
                            yield
                    V(act, lambda h, qi=qi, br_=br_: h.copy(out=rkvp[:, qi * 4:(qi + 1) * 4, 1:129], in_=v4(br_)),
                      reads=[br_.b], writes=[rkvp.b])
                    rel(br_)
                    yield
                bkk = bank()
                for g in range(2):
                    proj(v4(bkk)[:, g, :], bkk, wkd, g * 128, False)
                proj(v4(bkk)[:, 2, :], bkk, w_in_sb, 2304, False)
                proj(v4(bkk)[:, 3, :], bkk, w_in_sb, 2432, True)
                V(act, lambda h: h.copy(out=lorap[:, :, 1:129], in_=v4(bkk)[:, 2:4, :]), reads=[bkk.b],
                  writes=[lorap.b])
                V(act, lambda h: h.activation(out=qsq[:, 4:6, :], in_=v4(bkk)[:, 0:2, :], func=AF.Square),
                  reads=[bkk.b], writes=[qsq.b])
                yield
                bq = bank()
                for c in range(4):
                    proj(v4(bq)[:, c, :], bq, w_in_sb, c * 128, c == 3)
                    if c == 1:
                        yield
                V(act, lambda h: h.activation(out=qsq[:, 0:4, :], in_=v4(bq), func=AF.Square),
                  reads=[bq.b], writes=[qsq.b])
                yield
                bv = bank()
                for kc in range(8):
                    P.mm(bv[:, 0:128], uTc[:, kc, :], w_in_sb[:, kc, 640:768], kc == 0, kc == 7,
                         reads=[uTc.b, *w_in_sb.bs], writes=[bv.b])
                V(dve, lambda h: h.tensor_copy(out=vext[par][:, :, 0:64],
                                               in_=bv[:, 0:128].rearrange("p (g d) -> p g d", g=2)),
                  reads=[bv.b], writes=[vext[par].b])
                rel(bv)
                yield
                bs1 = bank()
                for c in range(4):
                    P.mm(v4(bs1)[:, c, :], ones64[:], qsq[:, c, :], True, True, reads=[ones64.b, qsq.b],
                         writes=[bs1.b], sig=(c == 3))
                bs2 = bank()
                for c in range(2):
                    P.mm(v4(bs2)[:, c, :], ones64[:], qsq[:, 4 + c, :], True, True, reads=[ones64.b, qsq.b],
                         writes=[bs2.b], sig=(c == 1))
                rsqrt_inplace(rq, lambda: rq[:, 0:4, :], lambda: v4(bs1), [bs1.b], 1.0, RMS_EPS)
                rsqrt_inplace(rq, lambda: rq[:, 4:6, :], lambda: v4(bs2)[:, 0:2, :], [bs2.b], 1.0, RMS_EPS)
                rel(bs1)
                rel(bs2)
                yield
                V(dve, lambda h: h.tensor_tensor(out=qn[:], in0=v4(bq), in1=rq[:, 0:4, :], op=ALU.mult),
                  reads=[bq.b, rq.b], writes=[qn.b])
                for hf in range(2):
                    ps_ = slice(hf * 64, (hf + 1) * 64)
                    V(dve, lambda h, hf=hf, ps_=ps_: h.scalar_tensor_tensor(
                        out=kn[par][ps_, :, hf, :], in0=v4(bkk)[ps_, 0:2, :], scalar=der[ps_, 4:5],
                        in1=rq[ps_, 4:6, :], op0=ALU.mult, op1=ALU.mult),
                      reads=[bkk.b, rq.b, der.b], writes=[kn[par].b])
                rel(bq)
                rel(bkk)
                yield
                for g in range(2):
                    blocks = ([] if first else [(1 - par, m_gt, pt[0])]) + [(par, m_le, pt[1])]
                    for (kp, msk, ptt) in blocks:
                        bs = bank()
                        for j in range(4):
                            hd = 4 * g + j
                            c, hf = hd // 2, hd % 2
                            P.mm(v4(bs)[:, j, :], kn[kp][:, g, hf, :], qn[:, c, :], True, True,
                                 reads=[kn[kp].b, qn.b], writes=[bs.b], sig=(j == 3))
                        V(act, lambda h, bs=bs, ptt=ptt: h.activation(out=ptt[:], in_=v4(bs), func=AF.Exp),
                          reads=[bs.b], writes=[ptt.b])
                        rel(bs)
                        V(pool, lambda h, ptt=ptt, msk=msk: h.tensor_tensor(
                            out=ptt[:], in0=ptt[:], in1=bc(msk[:].unsqueeze(1), [128, 4, 128]), op=ALU.mult),
                          reads=[ptt.b, msk.b], writes=[ptt.b])
                        yield
                    bo = bank()
                    bov = bo.h[:, 0:260].rearrange("p (j d) -> p j d", j=4)
                    for j in range(4):
                        for bi, (kp, msk, ptt) in enumerate(blocks):
                            P.mm(bov[:, j, :], ptt[:, j, :], vext[kp][:, g, :], bi == 0, bi == len(blocks) - 1,
                                 reads=[ptt.b, vext[kp].b], writes=[bo.b],
                                 sig=(j == 3 and bi == len(blocks) - 1))
                    V(dve, lambda h, g=g, bov=bov: h.tensor_tensor(
                        out=sm[:, 8 + 4 * g:12 + 4 * g].unsqueeze(2), in0=bov[:, :, 64:65],
                        in1=esink[:, 4 * g:4 * g + 4].unsqueeze(2), op=ALU.add),
                      reads=[bo.b, esink.b, sm.b], writes=[sm.b])
                    V(act, lambda h, g=g: h.activation(out=sm[:, 8 + 4 * g:12 + 4 * g],
                                                       in_=sm[:, 8 + 4 * g:12 + 4 * g], func=AF.Ln),
                      reads=[sm.b], writes=[sm.b])
                    V(act, lambda h, g=g: h.activation(out=sm[:, 8 + 4 * g:12 + 4 * g],
                                                       in_=sm[:, 8 + 4 * g:12 + 4 * g], func=AF.Exp, scale=-1.0),
                      reads=[sm.b], writes=[sm.b])
                    V(dve, lambda h, g=g, bov=bov: h.tensor_tensor(
                        out=ao[:, 4 * g:4 * g + 4, :], in0=bov[:, :, 0:64],
                        in1=bc(sm[:, 8 + 4 * g:12 + 4 * g].unsqueeze(2), [128, 4, 64]), op=ALU.mult),
                      reads=[bo.b, sm.b], writes=[ao.b])
                    rel(bo)
                    yield
                bt = bank()
                aov = ao[:].rearrange("p h d -> p (h d)")
                for c in range(4):
                    P.tr(bf(bt)[:, c * 128:(c + 1) * 128], aov[:, c * 128:(c + 1) * 128], ident[:],
                         reads=[ao.b, ident.b], writes=[bt.b], sig=(c == 3))
                V(act, lambda h: h.copy(out=aoT[slot][:], in_=bf(bt)[:, 0:512].rearrange("p (c t) -> p c t", c=4)),
                  reads=[bt.b], writes=[aoT[slot].b])
                rel(bt)
                yield

            def gen_rw(si):
                b, n = steps[si]
                first = (n == 0)
                if first:
                    V(pool, lambda h: h.memset(rkvp[:, :, 0:1], 0.0), reads=[rkvp.b], writes=[rkvp.b])
                    V(pool, lambda h: h.memset(lorap[:, :, 0:1], 0.0), reads=[lorap.b], writes=[lorap.b])
                    V(pool, lambda h: h.memset(Hs[:], 0.0), writes=[Hs.b])
                    V(pool, lambda h: h.memset(Hb[:], 0.0), writes=[Hb.b])
                V(dve, lambda h: h.tensor_tensor(out=zs[:], in0=lorap[:, :, 0:128], in1=lorap[:, :, 1:129],
                                                 op=ALU.subtract), reads=[lorap.b], writes=[zs.b])
                V(pool, lambda h: h.tensor_tensor(out=zs[:], in0=zs[:], in1=mucol("mu2"), op=ALU.mult),
                  reads=[zs.b, prm.b], writes=[zs.b])
                V(dve, lambda h: h.tensor_tensor(out=zs[:], in0=zs[:], in1=lorap[:, :, 1:129], op=ALU.add),
                  reads=[zs.b, lorap.b], writes=[zs.b])
                V(pool, lambda h: h.tensor_copy(out=lorap[:, :, 0:1], in_=lorap[:, :, 128:129]),
                  reads=[lorap.b], writes=[lorap.b])
                yield
                sigmoid_inplace(zt, lambda: zt[0:64, 0, :], lambda: zs[0:64, 0, :], [zs.b], scale=2.0)
                sigmoid_inplace(zt, lambda: zt[:, 1, :], lambda: zs[:, 1, :], [zs.b], scale=-1.0)
                V(dve, lambda h: h.tensor_scalar(out=twz[0:64, 0, :], in0=zt[0:64, 0, :], scalar1=-2.0, scalar2=1.0,
                                                 op0=ALU.mult, op1=ALU.add), reads=[zt.b], writes=[twz.b])
                V(dve, lambda h: h.tensor_copy(out=twz[64:128, 1, :], in_=zs[64:128, 0, :]), reads=[zs.b],
                  writes=[twz.b])
                V(dve, lambda h: h.tensor_copy(out=sgb[:], in_=zt[:, 1, :]), reads=[zt.b], writes=[sgb.b])
                yield
                for qi, (dst, mun) in enumerate(((r_, "mu_r"), (k_, "mu_k"), (v_, "mu_v"))):
                    V(dve, lambda h, qi=qi, dst=dst: h.tensor_tensor(out=dst[:], in0=rkvp[:, qi * 4:qi * 4 + 4, 0:128],
                                                                     in1=rkvp[:, qi * 4:qi * 4 + 4, 1:129],
                                                                     op=ALU.subtract),
                      reads=[rkvp.b], writes=[dst.b])
                    V(pool, lambda h, mun=mun, dst=dst: h.tensor_tensor(out=dst[:], in0=dst[:], in1=mucol(mun),
                                                                        op=ALU.mult),
                      reads=[dst.b, prm.b], writes=[dst.b])
                    V(pool, lambda h, qi=qi, dst=dst: h.tensor_tensor(out=dst[:], in0=dst[:],
                                                                      in1=rkvp[:, qi * 4:qi * 4 + 4, 1:129],
                                                                      op=ALU.add),
                      reads=[dst.b, rkvp.b], writes=[dst.b])
                    yield
                V(pool, lambda h: h.tensor_copy(out=rkvp[:, :, 0:1], in_=rkvp[:, :, 128:129]),
                  reads=[rkvp.b], writes=[rkvp.b])
                V(pool, lambda h: h.tensor_tensor(out=kk[:], in0=k_[:], in1=mucol("k_k"), op=ALU.mult),
                  reads=[k_.b, prm.b], writes=[kk.b])
                V(act, lambda h: h.activation(out=hb16[:], in_=kk[:], func=AF.Square),
                  reads=[kk.b], writes=[hb16.b])
                bz1, bz2 = bank(), bank()
                for c in range(4):
                    P.mm(v4(bz1)[:, c, :], lora_wa[:, c * 128:(c + 1) * 128], twz[:, 0, :], True, True,
                         reads=[lora_wa.b, twz.b], writes=[bz1.b], sig=(c == 3))
                for c in range(4):
                    P.mm(v4(bz2)[:, c, :], lora_wa[:, c * 128:(c + 1) * 128], twz[:, 1, :], True, True,
                         reads=[lora_wa.b, twz.b], writes=[bz2.b], sig=(c == 3))
                V(dve, lambda h: h.tensor_tensor(out=lw[:], in0=v4(bz1), in1=mucol("db"), op=ALU.add),
                  reads=[bz1.b, prm.b], writes=[lw.b])
                V(dve, lambda h: h.tensor_tensor(out=a_[:], in0=v4(bz2), in1=mucol("ab"), op=ALU.add),
                  reads=[bz2.b, prm.b], writes=[a_.b])
                rel(bz1)
                rel(bz2)
                yield
                bs3 = bank()
                for c in range(4):
                    P.mm(v4(bs3)[:, c, :], ones1[:], hb16[:, c, :], True, True, reads=[ones1.b, hb16.b],
                         writes=[bs3.b], sig=(c == 3))
                sigmoid_inplace(lw, lambda: lw[:], lambda: lw[:], [lw.b])
                V(pool, lambda h: h.tensor_scalar(out=lw[:], in0=lw[:], scalar1=DECAY_C, scalar2=None, op0=ALU.mult),
                  reads=[lw.b], writes=[lw.b])
                yield
                sigmoid_inplace(a_, lambda: a_[:], lambda: a_[:], [a_.b])
                rsqrt_inplace(t1, lambda: t1[:], lambda: v4(bs3), [bs3.b], 1.0, 1e-30)
                rel(bs3)
                yield
                for c in range(4):
                    V(dve, lambda h, c=c: h.tensor_tensor_scan(out=Lc[:, c, :], data0=onesf[:], data1=lw[:, c, :],
                                                               initial=0.0, op0=ALU.mult, op1=ALU.add),
                      reads=[onesf.b, lw.b], writes=[Lc.b])
                V(dve, lambda h: h.tensor_tensor(out=kk[:], in0=kk[:], in1=t1[:], op=ALU.mult),
                  reads=[kk.b, t1.b], writes=[kk.b])
                yield
                V(pool, lambda h: h.tensor_tensor(out=t2[:], in0=a_[:], in1=mucol("k_a"), op=ALU.mult),
                  reads=[a_.b, prm.b], writes=[t2.b])
                V(pool, lambda h: h.tensor_tensor(out=t2[:], in0=t2[:], in1=bc(der[:, 0:4].unsqueeze(2), [128, 4, 128]),
                                                  op=ALU.add), reads=[t2.b, der.b], writes=[t2.b])
                V(dve, lambda h: h.tensor_tensor(out=kmod[:], in0=k_[:], in1=t2[:], op=ALU.mult),
                  reads=[k_.b, t2.b], writes=[kmod.b])
                yield
                V(act, lambda h: h.activation(out=e1[:], in_=Lc[:], func=AF.Exp), reads=[Lc.b], writes=[e1.b])
                V(pool, lambda h: h.tensor_tensor(out=t1[:], in0=Lc[:], in1=lw[:], op=ALU.subtract),
                  reads=[Lc.b, lw.b], writes=[t1.b])
                for hf in range(2):
                    ps_ = slice(hf * 64, (hf + 1) * 64)
                    V(dve, lambda h, hf=hf, ps_=ps_: h.tensor_tensor(out=arz[ps_, :, hf, 1, :], in0=r_[ps_, :, :],
                                                                     in1=e1[ps_, :, :], op=ALU.mult),
                      reads=[r_.b, e1.b], writes=[arz.b])
                V(pool, lambda h: h.tensor_copy(out=wcs[:], in_=e1[:, :, 127:128]), reads=[e1.b], writes=[wcs.b])
                V(act, lambda h: h.activation(out=t1[:], in_=t1[:], func=AF.Exp), reads=[t1.b], writes=[t1.b])
                yield
                for hf in range(2):
                    ps_ = slice(hf * 64, (hf + 1) * 64)
                    V(dve, lambda h, hf=hf, ps_=ps_: h.scalar_tensor_tensor(
                        out=arz[ps_, :, hf, 0, :], in0=kk[ps_, :, :], scalar=-1.0, in1=t1[ps_, :, :],
                        op0=ALU.mult, op1=ALU.mult), reads=[kk.b, t1.b], writes=[arz.b])
                V(act, lambda h: h.activation(out=t2[:], in_=Lc[:], func=AF.Exp, scale=-1.0),
                  reads=[Lc.b], writes=[t2.b])
                V(pool, lambda h: h.tensor_tensor(out=t1[:], in0=kk[:], in1=a_[:], op=ALU.mult),
                  reads=[kk.b, a_.b], writes=[t1.b])
                yield
                V(dve, lambda h: h.tensor_tensor(out=bk[:, :, 1, :], in0=kmod[:], in1=t2[:], op=ALU.mult),
                  reads=[kmod.b, t2.b], writes=[bk.b])
                V(dve, lambda h: h.tensor_tensor(out=bk[:, :, 0, :], in0=t1[:], in1=t2[:], op=ALU.mult),
                  reads=[t1.b, t2.b], writes=[bk.b])
                V(pool, lambda h: h.tensor_tensor(out=hat[:], in0=bk[:],
                                                  in1=bc(wcs[:].unsqueeze(2), [128, 4, 2, 128]), op=ALU.mult),
                  reads=[bk.b, wcs.b], writes=[hat.b])
                yield
                arv = arz[:].rearrange("p c h w t -> p c h (w t)")
                for hd in range(8):
                    c, hf = hd // 2, hd % 2
                    bsx = bank()
                    P.mm(bsx[:, 0:256], bk[:, c, 0, :], arv[:, c, hf, :], True, True,
                         reads=[bk.b, arz.b], writes=[bsx.b], sig=False)
                    P.mm(bsx[:, 256:512], bk[:, c, 1, :], arv[:, c, hf, :], True, True,
                         reads=[bk.b, arz.b], writes=[bsx.b], sig=True)
                    V(dve, lambda h, hd=hd, bsx=bsx: h.tensor_tensor(out=ssb[:, hd, :, :], in0=v4(bsx), in1=mask4[:],
                                                                     op=ALU.mult),
                      reads=[bsx.b, mask4.b], writes=[ssb.b])
                    rel(bsx)
                    if hd % 2 == 1:
                        yield
                for q in range(2):
                    by = bank()
                    for j in range(4):
                        hd = q * 4 + j
                        c, hf = hd // 2, hd % 2
                        P.mm(v4(by)[:, j, :], arz[:, c, hf, 0, :], bk[:, c, 0, :], True, True,
                             reads=[arz.b, bk.b], writes=[by.b], sig=(j == 3))
                    V(dve, lambda h, q=q, by=by: h.tensor_tensor(out=Ys[:, 4 * q:4 * q + 4, :], in0=v4(by),
                                                                 in1=bc(m_gt[:].unsqueeze(1), [128, 4, 128]),
                                                                 op=ALU.mult),
                      reads=[by.b, m_gt.b], writes=[Ys.b])
                    rel(by)
                yield
                bh = bank()
                for c in range(4):
                    for w in range(2):
                        o = (c * 2 + w) * 128
                        P.tr(bf(bh)[:, o:o + 128], hat[:, c, w, :], ident[:], reads=[hat.b, ident.b],
                             writes=[bh.b], sig=(c == 3 and w == 1))
                V(act, lambda h: h.copy(out=kbh[:].rearrange("p c w t -> p (c w t)"), in_=bf(bh)),
                  reads=[bh.b], writes=[kbh.b])
                rel(bh)
                V(act, lambda h: h.copy(out=hb16[:], in_=v_[:]), reads=[v_.b], writes=[hb16.b])
                V(pool, lambda h: h.tensor_tensor(out=TTs[:], in0=ssb[:, :, 0, :],
                                                  in1=bc(ident[:].unsqueeze(1), [128, 8, 128]), op=ALU.add),
                  reads=[ssb.b, ident.b], writes=[TTs.b])
                yield
                bvt = bank()
                for c in range(4):
                    P.tr(bf(bvt)[:, c * 128:(c + 1) * 128], hb16[:, c, :], ident[:], reads=[hb16.b, ident.b],
                         writes=[bvt.b], sig=(c == 3))
                V(act, lambda h: h.copy(out=vtok[:].rearrange("p c t -> p (c t)"), in_=bf(bvt)[:, 0:512]),
                  reads=[bvt.b], writes=[vtok.b])
                rel(bvt)
                for kr in range(6):
                    Xin = (lambda hd: ssb[:, hd, 0, :]) if kr == 0 else (lambda hd: Xs[:, hd, :])
                    xb_ = ssb.b if kr == 0 else Xs.b
                    bxs = [bank(), bank()] if kr < 5 else None
                    bys = [bank(), bank()]
                    for q in range(2):
                        for j in range(4):
                            hd = 4 * q + j
                            if kr < 5:
                                P.mm(v4(bxs[q])[:, j, :], Ys[:, hd, :], Xin(hd), True, True,
                                     reads=[Ys.b, xb_], writes=[bxs[q].b], sig=(j == 3))
                        for j in range(4):
                            hd = 4 * q + j
                            P.mm(v4(bys[q])[:, j, :], Xin(hd), Ys[:, hd, :], True, True,
                                 reads=[Ys.b, xb_], writes=[bys[q].b], sig=(j == 3))
                    yield
                    for q in range(2):
                        if kr < 5:
                            V(act, lambda h, q=q, bxs=bxs: h.copy(out=Xs[:, 4 * q:4 * q + 4, :], in_=v4(bxs[q])),
                              reads=[bxs[q].b], writes=[Xs.b])
                            rel(bxs[q])
                        if q == 0:
                            V(dve, lambda h, q=q, bys=bys: h.tensor_copy(out=Ys[:, 4 * q:4 * q + 4, :],
                                                                         in_=v4(bys[q])),
                              reads=[bys[q].b], writes=[Ys.b])
                        else:
                            V(act, lambda h, q=q, bys=bys: h.copy(out=Ys[:, 4 * q:4 * q + 4, :], in_=v4(bys[q])),
                              reads=[bys[q].b], writes=[Ys.b])
                        rel(bys[q])
                    yield
                    bts = [bank(), bank()]
                    for hd in range(8):
                        P.mm(v4(bts[hd // 4])[:, hd % 4, :], Ys[:, hd, :], TTs[:, hd, :], True, True,
                             reads=[Ys.b, TTs.b], writes=[bts[hd // 4].b], sig=(hd % 4 == 3))
                    for q in range(2):
                        V(dve, lambda h, q=q, bts=bts: h.tensor_tensor(out=TTs[:, 4 * q:4 * q + 4, :], in0=v4(bts[q]),
                                                                       in1=TTs[:, 4 * q:4 * q + 4, :], op=ALU.add),
                          reads=[bts[q].b, TTs.b], writes=[TTs.b])
                        rel(bts[q])
                    yield
                V(pool, lambda h: h.tensor_tensor(out=t1[:], in0=r_[:], in1=mucol("r_k"), op=ALU.mult),
                  reads=[r_.b, prm.b], writes=[t1.b])
                V(pool, lambda h: h.tensor_tensor(out=hb16[:], in0=t1[:], in1=kmod[:], op=ALU.mult),
                  reads=[t1.b, kmod.b], writes=[hb16.b])
                bg = bank()
                bgv = bg.h[:].rearrange("p (h d) -> p h d", h=8)
                for hd in range(8):
                    c, hf = hd // 2, hd % 2
                    p0 = hf * 64
                    if not first:
                        P.mm(bgv[:, hd, :], arz[:, c, hf, 0, :], Hb[:, c, :], True, False,
                             reads=[arz.b, Hb.b], writes=[bg.b])
                    P.mm(bgv[:, hd, :], ssb[:, hd, 2, :], vtok[:, c, p0:p0 + 64], first, True,
                         reads=[ssb.b, vtok.b], writes=[bg.b], sig=(hd == 7))
                V(act, lambda h: h.copy(out=gbf[:], in_=bgv), reads=[bg.b], writes=[gbf.b])
                rel(bg)
                yield
                bbs = bank()
                for c in range(4):
                    P.mm(v4(bbs)[:, c, :], ones1[:], hb16[:, c, :], True, True, reads=[ones1.b, hb16.b],
                         writes=[bbs.b], sig=(c == 3))
                bonus = kmod
                V(dve, lambda h: h.tensor_tensor(out=bonus[:], in0=v4(bbs), in1=v_[:], op=ALU.mult),
                  reads=[bbs.b, v_.b, kmod.b], writes=[bonus.b])
                rel(bbs)
                V(pool, lambda h: h.tensor_tensor(out=r_[:], in0=g_[:], in1=mucol("lng"), op=ALU.mult),
                  reads=[g_.b, prm.b], writes=[r_.b])
                V(pool, lambda h: h.tensor_tensor(out=v_[:], in0=bonus[:], in1=mucol("lnb"), op=ALU.add),
                  reads=[bonus.b, prm.b], writes=[v_.b])
                V(pool, lambda h: h.tensor_tensor(out=v_[:], in0=v_[:], in1=g_[:], op=ALU.mult),
                  reads=[v_.b, g_.b], writes=[v_.b])
                bu = bank()
                buv = bu.h[:].rearrange("p (h d) -> p h d", h=8)
                for hd in range(8):
                    P.mm(buv[:, hd, :], TTs[:, hd, :], gbf[:, hd, :], True, True, reads=[TTs.b, gbf.b],
                         writes=[bu.b], sig=(hd == 7))
                V(dve, lambda h: h.tensor_copy(out=ubf[:], in_=buv), reads=[bu.b], writes=[ubf.b])
                rel(bu)
                yield
                bhn = bank()
                bhv = bhn.h[:, 0:256].rearrange("p (c d) -> p c d", c=4)
                byp = bank()
                for hd in range(8):
                    c, hf = hd // 2, hd % 2
                    p0 = hf * 64
                    if not first:
                        P.mm(v4(byp)[p0:p0 + 64, c, :], Hb[:, c, :], arz[:, c, hf, 1, :], True, False,
                             reads=[Hb.b, arz.b], writes=[byp.b])
                    P.mm(v4(byp)[p0:p0 + 64, c, :], ubf[:, hd, :], ssb[:, hd, 1, :], first, False,
                         reads=[ubf.b, ssb.b], writes=[byp.b])
                    P.mm(v4(byp)[p0:p0 + 64, c, :], vtok[:, c, p0:p0 + 64], ssb[:, hd, 3, :], False, True,
                         reads=[vtok.b, ssb.b], writes=[byp.b], sig=(hd == 7))
                for hd in range(8):
                    c, hf = hd // 2, hd % 2
                    p0 = hf * 64
                    P.mm(bhv[p0:p0 + 64, c, :], kbh[:, c, 0, p0:p0 + 64], ubf[:, hd, :], True, False,
                         reads=[kbh.b, ubf.b], writes=[bhn.b])
                    P.mm(bhv[p0:p0 + 64, c, :], kbh[:, c, 1, p0:p0 + 64], vtok[:, c, p0:p0 + 64], False, True,
                         reads=[kbh.b, vtok.b], writes=[bhn.b], sig=(hd == 7))
                V(pool, lambda h: h.tensor_tensor(out=Hs[:], in0=Hs[:], in1=bc(wcs[:], [128, 4, 64]), op=ALU.mult),
                  reads=[Hs.b, wcs.b], writes=[Hs.b])
                V(dve, lambda h: h.tensor_tensor(out=Hs[:], in0=bhv, in1=Hs[:], op=ALU.add),
                  reads=[bhn.b, Hs.b], writes=[Hs.b])
                rel(bhn)
                V(pool, lambda h: h.tensor_copy(out=Hb[:], in_=Hs[:]), reads=[Hs.b], writes=[Hb.b])
                yield
                ysb, yc = lw, a_
                V(act, lambda h: h.copy(out=ysb[:], in_=v4(byp)), reads=[byp.b], writes=[ysb.b])
                rel(byp)
                V(pool, lambda h: h.tensor_copy(out=hb16[:], in_=ysb[:]), reads=[ysb.b], writes=[hb16.b])
                bm = bank()
                for c in range(4):
                    P.mm(v4(bm)[:, c, :], ones64[:], hb16[:, c, :], True, True, reads=[ones64.b, hb16.b],
                         writes=[bm.b], sig=(c == 3))
                V(dve, lambda h: h.tensor_tensor(out=yc[:], in0=ysb[:], in1=v4(bm), op=ALU.subtract),
                  reads=[ysb.b, bm.b], writes=[yc.b])
                rel(bm)
                yield
                V(act, lambda h: h.activation(out=hb16[:], in_=yc[:], func=AF.Square),
                  reads=[yc.b], writes=[hb16.b])
                bvar = bank()
                for c in range(4):
                    P.mm(v4(bvar)[:, c, :], ones64[:], hb16[:, c, :], True, True, reads=[ones64.b, hb16.b],
                         writes=[bvar.b], sig=(c == 3))
                rsqrt_inplace(t1, lambda: t1[:], lambda: v4(bvar), [bvar.b], 1.0, GN_EPS)
                rel(bvar)
                yield
                V(dve, lambda h: h.tensor_tensor(out=yc[:], in0=yc[:], in1=t1[:], op=ALU.mult),
                  reads=[yc.b, t1.b], writes=[yc.b])
                V(dve, lambda h: h.tensor_tensor(out=yc[:], in0=yc[:], in1=r_[:], op=ALU.mult),
                  reads=[yc.b, r_.b], writes=[yc.b])
                yield

            def rw_gate_lora():
                bz3 = bank()
                for c in range(4):
                    P.mm(v4(bz3)[:, c, :], lora_g[:, c * 128:(c + 1) * 128], sgb[:], True, True,
                         reads=[lora_g.b, sgb.b], writes=[bz3.b], sig=(c == 3))
                V(act, lambda h: h.copy(out=g_[:], in_=v4(bz3)), reads=[bz3.b], writes=[g_.b])
                rel(bz3)

            def rw_finish():
                yc = a_
                V(dve, lambda h: h.tensor_tensor(out=ywT[:], in0=yc[:], in1=v_[:], op=ALU.add),
                  reads=[yc.b, v_.b], writes=[ywT.b])

            def gen_back(si):
                b, n = steps[si]
                slot = si % 2
                xcur = xt[slot]
                uTc = uT[slot]
                aoTc = aoT[slot]
                for fp in range(4):
                    bgt = bank()
                    for j in range(4):
                        col0 = (2560 if j < 2 else 3584) + (2 * fp + (j % 2)) * 128
                        for kc in range(8):
                            P.mm(v4(bgt)[:, j, :], w_in_sb[:, kc, col0:col0 + 128], uTc[:, kc, :], kc == 0, kc == 7,
                                 reads=[*w_in_sb.bs, uTc.b], writes=[bgt.b], sig=(j == 3 and kc == 7))
                        if j == 1:
                            yield
                    sigmoid_inplace(sgt, lambda: sgt[:], lambda bgt=bgt: v4(bgt), [bgt.b])
                    rel(bgt)
                    yield
                    byy = bank()
                    for j in range(2):
                        for kc in range(4):
                            P.mm(v4(byy)[:, j, :], w_ba[:, kc, (2 * fp + j) * 128:(2 * fp + j + 1) * 128], aoTc[:, kc, :],
                                 kc == 0, kc == 3, reads=[w_ba.b, aoTc.b], writes=[byy.b], sig=False)
                    for j in range(2):
                        for kc in range(4):
                            P.mm(v4(byy)[:, 2 + j, :], w_br[:, kc, (2 * fp + j) * 128:(2 * fp + j + 1) * 128],
                                 ywT[:, kc, :], kc == 0, kc == 3, reads=[w_br.b, ywT.b], writes=[byy.b],
                                 sig=(j == 1 and kc == 3))
                    V(dve, lambda h, byy=byy: h.tensor_tensor(out=sgt[:], in0=sgt[:], in1=v4(byy), op=ALU.mult),
                      reads=[sgt.b, byy.b], writes=[sgt.b])
                    rel(byy)
                    V(pool, lambda h, fp=fp: h.tensor_tensor(out=mixT[:, 2 * fp:2 * fp + 2, :], in0=sgt[:, 0:2, :],
                                                             in1=sgt[:, 2:4, :], op=ALU.add),
                      reads=[sgt.b], writes=[mixT.b])
                    yield
                for hh in range(2):
                    bw = bank()
                    for kc in range(8):
                        P.mm(bw[:], mixT[:, kc, :], w_out_sb[:, kc, hh * 512:(hh + 1) * 512], kc == 0, kc == 7,
                             reads=[mixT.b, w_out_sb.b], writes=[bw.b])
                    V(dve, lambda h, hh=hh, bw=bw: h.tensor_tensor(out=xcur[:, hh * 512:(hh + 1) * 512], in0=bw[:],
                                                                   in1=xcur[:, hh * 512:(hh + 1) * 512], op=ALU.add),
                      reads=[bw.b, xcur.b], writes=[xcur.b])
                    rel(bw)
                    yield
                P.dma(sp, h_d[b, n * 128:(n + 1) * 128, :], xcur[:], d_st[slot], reads=[xcur.b])
                yield

            def gen_rw_full(si):
                g = gen_rw(si)
                cnt = 0
                for _ in g:
                    cnt += 1
                    if cnt == 6:
                        rw_gate_lora()
                    yield
                rw_finish()
                yield

            def chain_gens(*gs):
                for g in gs:
                    if g is not None:
                        yield from g

            def interleave(g1, g2, r1=1, r2=1):
                a1 = a2 = True
                while a1 or a2:
                    for _ in range(r1):
                        if a1:
                            try:
                                next(g1)
                            except StopIteration:
                                a1 = False
                    for _ in range(r2):
                        if a2:
                            try:
                                next(g2)
                            except StopIteration:
                                a2 = False

            def run_all(g):
                for _ in g:
                    pass

            NS = len(steps)
            run_all(gen_front(0))
            for si in range(NS):
                g2 = chain_gens(gen_back(si - 1) if si > 0 else None,
                                gen_front(si + 1) if si + 1 < NS else None)
                interleave(gen_rw_full(si), g2)
            run_all(gen_back(NS - 1))
            P.barrier()
            print("phase1 stats", P.stats())

        with ExitStack() as st2, suppress(_Stop):
            w1_sb = mk(st2, "w1_sb", [128, 8, DFF], BF16)
            w2_sb = mk(st2, "w2_sb", [128, 32, DM], BF16)
            stg2 = [mk(st2, f"stg2_{i}", [128, 2048]) for i in range(2)]
            d_stg2 = [P.dsem(f"d_stg2_{i}") for i in range(2)]
            w1_sb.b2 = Buf("w1_odd")
            w1_sb.bs = [w1_sb.b, w1_sb.b2]
            w2_sb.b2 = Buf("w2_odd")
            w2_sb.bs = [w2_sb.b, w2_sb.b2]
            for kc in range(8):
                if kc % 2 == 0:
                    load_cast(w1_sb, w1_sb[:, kc, :], w1_d[kc * 128:(kc + 1) * 128, :], DFF)
                else:
                    load_cast_sp(w1_sb.b2, w1_sb[:, kc, :], w1_d[kc * 128:(kc + 1) * 128, :], DFF, stg2, d_stg2)
            for fc in range(32):
                if fc % 2 == 0:
                    load_cast(w2_sb, w2_sb[:, fc, :], w2_d[fc * 128:(fc + 1) * 128, :], DM)
                else:
                    load_cast_sp(w2_sb.b2, w2_sb[:, fc, :], w2_d[fc * 128:(fc + 1) * 128, :], DM, stg2, d_stg2)
            NT = 2
            ht = [mk(st2, f"ht{i}", [128, NT, DM]) for i in range(2)]
            xn2 = mk(st2, "xn2", [128, DM], BF16)
            sm2 = mk(st2, "sm2", [128, 4])
            u2T = mk(st2, "u2T", [128, 8, NT * 128], BF16)
            hT = mk(st2, "hT", [128, 32, NT * 128], BF16)
            print("sbuf remaining after phase2 alloc:", nc.sbuf_bytes_remaining)
            d_h = [P.dsem(f"d_h{i}") for i in range(2)]
            d_y = [P.dsem(f"d_y{i}") for i in range(2)]
            steps2 = [(b, n2) for b in range(NB) for n2 in range(nsteps // NT)]

            def load_h(b, n2, slot):
                for j in range(NT):
                    r0 = (n2 * NT + j) * 128
                    P.dma(sp, ht[slot][:, j, :], h_d[b, r0:r0 + 128, :], d_h[slot], writes=[ht[slot].b])

            load_h(steps2[0][0], steps2[0][1], 0)
            for si, (b, n2) in enumerate(steps2):
                slot = si % 2
                hcur = ht[slot]
                if si + 1 < len(steps2):
                    load_h(steps2[si + 1][0], steps2[si + 1][1], 1 - slot)
                for j in range(NT):
                    V(dve, lambda h: h.memset(sm2[:, 0:1], 0.0), writes=[sm2.b])
                    V(act, lambda h, j=j: h.activation(out=xn2[:], in_=hcur[:, j, :], func=AF.Square,
                                                       accum_out=sm2[:, 0:1]),
                      reads=[hcur.b, sm2.b], writes=[xn2.b, sm2.b])
                    V(act, lambda h: h.activation(out=sm2[:, 0:1], in_=sm2[:, 0:1], func=AF.Ln, bias=RMS_EPS,
                                                  scale=1.0 / DM), reads=[sm2.b], writes=[sm2.b])
                    V(act, lambda h: h.activation(out=sm2[:, 0:1], in_=sm2[:, 0:1], func=AF.Exp, scale=-0.5),
                      reads=[sm2.b], writes=[sm2.b])
                    V(dve, lambda h, j=j: h.tensor_scalar(out=xn2[:], in0=hcur[:, j, :], scalar1=sm2[:, 0:1],
                                                          scalar2=None, op0=ALU.mult),
                      reads=[hcur.b, sm2.b], writes=[xn2.b])
                    bk_ = bank()
                    for c in range(8):
                        P.tr(bf(bk_)[:, c * 128:(c + 1) * 128], xn2[:, c * 128:(c + 1) * 128], ident[:],
                             reads=[xn2.b, ident.b], writes=[bk_.b], sig=(c == 7))
                    o, _ = PC["g2"]
                    V(dve, lambda h, j=j, bk_=bk_: h.tensor_tensor(
                        out=u2T[:, :, j * 128:(j + 1) * 128], in0=bf(bk_).rearrange("p (c t) -> p c t", c=8),
                        in1=bc(prm[:, o:o + 8].unsqueeze(2), [128, 8, 128]), op=ALU.mult),
                      reads=[bk_.b, prm.b], writes=[u2T.b])
                    rel(bk_)
                W = NT * 128
                for fq in range(16):
                    bff = bank()
                    for j in range(2):
                        fc = 2 * fq + j
                        for kc in range(8):
                            P.mm(bff[:, j * W:(j + 1) * W], w1_sb[:, kc, fc * 128:(fc + 1) * 128], u2T[:, kc, :],
                                 kc == 0, kc == 7, reads=[*w1_sb.bs, u2T.b], writes=[bff.b],
                                 sig=(j == 1 and kc == 7))
                    V(act, lambda h, fq=fq, bff=bff: h.activation(
                        out=hT[:, 2 * fq:2 * fq + 2, :].rearrange("p a t -> p (a t)"), in_=bff[:, 0:2 * W],
                        func=AF.Relu), reads=[bff.b], writes=[hT.b])
                    rel(bff)
                    V(pool, lambda h, fq=fq: h.tensor_tensor(out=hT[:, 2 * fq:2 * fq + 2, :],
                                                             in0=hT[:, 2 * fq:2 * fq + 2, :],
                                                             in1=hT[:, 2 * fq:2 * fq + 2, :], op=ALU.mult),
                      reads=[hT.b], writes=[hT.b])
                for j in range(NT):
                    for hh in range(2):
                        bo2 = bank()
                        for fc in range(32):
                            P.mm(bo2[:], hT[:, fc, j * 128:(j + 1) * 128], w2_sb[:, fc, hh * 512:(hh + 1) * 512],
                                 fc == 0, fc == 31, reads=[hT.b, *w2_sb.bs], writes=[bo2.b])
                        V(dve, lambda h, j=j, hh=hh, bo2=bo2: h.tensor_tensor(
                            out=hcur[:, j, hh * 512:(hh + 1) * 512], in0=bo2[:],
                            in1=hcur[:, j, hh * 512:(hh + 1) * 512], op=ALU.add),
                          reads=[bo2.b, hcur.b], writes=[hcur.b])
                        rel(bo2)
                    r0 = (n2 * NT + j) * 128
                    P.dma(sp, y_d[b, r0:r0 + 128, :], hcur[:, j, :], d_y[slot], reads=[hcur.b])
            for i in range(2):
                if d_y[i].cnt > 0:
                    P._wait(sp, (d_y[i], d_y[i].idx, d_y[i].cnt))
            print("final stats", P.stats())
    return nc


def _pack_params(inp):
    prm = np.zeros((128, NPRM), np.float32)

    def put(name, vec):
        o, w = PC[name]
        prm[:, o:o + w] = np.asarray(vec, np.float32).reshape(w, 128).T

    put("g1", inp["norm1_gain"][0])
    put("g2", inp["norm2_gain"][0])
    put("mu_r", inp["mu_r"][0])
    put("mu_k", inp["mu_k"][0])
    put("mu_v", inp["mu_v"][0])
    put("mu2", np.concatenate([inp["mu_w"][0], inp["mu_a"][0], inp["mu_g"][0]]))
    put("db", inp["decay_bias"][0])
    put("ab", inp["aaa_bias"][0])
    put("k_k", inp["k_k"][0])
    put("k_a", inp["k_a"][0])
    put("r_k", inp["r_k"][0])
    put("lng", inp["ln_x_gain"][0])
    put("lnb", inp["ln_x_bias"][0])
    put("qg", np.concatenate([inp["q_norm_gain"][0]] * 2))
    put("kg", np.concatenate([inp["k_norm_gain"][0]] * 2))
    return prm


_NC_CACHE = {}


def kernel(**inputs):
    inp = {k: np.asarray(v) for k, v in inputs.items()}
    x = np.ascontiguousarray(inp["x"], dtype=np.float32)
    prm = _pack_params(inp)
    shared = {
        "prm": prm,
        "attn_sinks": np.ascontiguousarray(inp["attn_sinks"], np.float32).reshape(1, 8),
        "w_in": np.ascontiguousarray(inp["w_in"][0], np.float32),
        "decay_up": np.ascontiguousarray(inp["decay_up"][0], np.float32),
        "aaa_up": np.ascontiguousarray(inp["aaa_up"][0], np.float32),
        "gate_up": np.ascontiguousarray(inp["gate_up"][0], np.float32),
        "w_branch_attn": np.ascontiguousarray(inp["w_branch_attn"][0], np.float32),
        "w_branch_rwkv": np.ascontiguousarray(inp["w_branch_rwkv"][0], np.float32),
        "w_out": np.ascontiguousarray(inp["w_out"][0], np.float32),
        "w_ff_in": np.ascontiguousarray(inp["w_ff_in"][0], np.float32),
        "w_ff_out": np.ascontiguousarray(inp["w_ff_out"][0], np.float32),
    }
    if "nc" not in _NC_CACHE:
        _NC_CACHE["nc"] = build_program()
    nc = _NC_CACHE["nc"]
    in_maps = []
    for c in range(NCORES):
        m = dict(shared)
        m["x"] = np.ascontiguousarray(x[c * NB:(c + 1) * NB])
        in_maps.append(m)
    res = run_bass_kernel_spmd(nc, in_maps, core_ids=list(range(NCORES)))
    out = np.concatenate([np.asarray(r["y"], np.float32).reshape(NB, SEQ, DM) for r in res.results], axis=0)
    return out.astype(np.float32)
```

```python
import numpy as np
from contextlib import ExitStack, suppress
import concourse.bass as bass
import concourse.mybir as mybir
from concourse.bass_utils import run_bass_kernel_spmd

F32 = mybir.dt.float32
BF16 = mybir.dt.bfloat16
ALU = mybir.AluOpType
AF = mybir.ActivationFunctionType

SEM_ROT = 6000
ATTACH_WAITS = True
NCORES = 8
SEQ = 2048
DM = 1024
NB = 2
IN_W = 4608
DFF = 4096
RMS_EPS = 1e-6
GN_EPS = 64e-5
DECAY_C = -0.6065306597126334

PC = {}
_o = 0
for _n, _w in [("g1", 8), ("g2", 8), ("mu_r", 4), ("mu_k", 4), ("mu_v", 4), ("mu2", 2),
               ("db", 4), ("ab", 4), ("k_k", 4), ("k_a", 4), ("r_k", 4), ("lng", 4), ("lnb", 4),
               ("qg", 1), ("kg", 1)]:
    PC[_n] = (_o, _w)
    _o += _w
NPRM = _o


class Buf:
    __slots__ = ("name", "w", "r")

    def __init__(self, name):
        self.name = name
        self.w = None
        self.r = {}


class Counter:
    def __init__(self, prog, name, step):
        self.prog = prog
        self.name = name
        self.step = step
        self.sems = []
        self.idx = -1
        self.cnt = 0
        self._new_sem()
        prog.counters.append(self)

    def _new_sem(self):
        s = self.prog.stack.enter_context(self.prog.nc.semaphore(f"{self.name}_{len(self.sems)}"))
        self.sems.append(s)
        self.idx += 1
        self.cnt = 0

    def next_token(self):
        if self.cnt >= SEM_ROT * self.step:
            self._new_sem()
        self.cnt += self.step
        return (self, self.idx, self.cnt)

    def peek_token(self):
        if self.cnt >= SEM_ROT * self.step:
            self._new_sem()
        return (self, self.idx, self.cnt + self.step)


class Eng:
    def __init__(self, prog, name, handle, compute=True):
        self.name = name
        self.h = handle
        self.ctr = Counter(prog, name, 1) if compute else None
        self.known = {}
        self.nwait = 0
        self.ninst = 0


class Prog:
    def __init__(self, nc, stack):
        self.nc = nc
        self.stack = stack
        self.counters = []
        self.pe = Eng(self, "pe", nc.tensor)
        self.act = Eng(self, "act", nc.scalar)
        self.dve = Eng(self, "dve", nc.vector)
        self.pool = Eng(self, "pool", nc.gpsimd)
        self.sp = Eng(self, "sp", nc.sync, compute=False)
        self.engs = [self.pe, self.act, self.dve, self.pool, self.sp]
        self.pe_pending = False

    def dsem(self, name):
        return Counter(self, name, 16)

    def _wait(self, eng, tok):
        ctr, idx, cnt = tok
        key = (id(ctr), idx)
        if eng.known.get(key, 0) >= cnt:
            return
        if ctr is self.pe.ctr and self.pe_pending and (idx, cnt) == (ctr.idx, ctr.cnt + 1):
            raise RuntimeError("waiting on a pending (unsignalled) PE token")
        eng.h.wait_ge(ctr.sems[idx], cnt)
        eng.known[key] = cnt
        eng.nwait += 1

    def _deps(self, eng, reads, writes, defer_last=False):
        toks = []
        for b in reads:
            if b.w is not None:
                toks.append(b.w)
        for b in writes:
            if b.w is not None:
                toks.append(b.w)
            toks.extend(b.r.values())
        need = {}
        for t in toks:
            if eng is self.pe and t[0] is self.pe.ctr:
                continue
            ctr, idx, cnt = t
            key = (id(ctr), idx)
            if eng.known.get(key, 0) >= cnt:
                continue
            if key not in need or need[key][2] < cnt:
                need[key] = t
        lst = list(need.values())
        last = None
        if defer_last and lst:
            last = lst.pop()
        for t in lst:
            self._wait(eng, t)
        return last

    def _attach(self, eng, ins, tok):
        ctr, idx, cnt = tok
        if ctr is self.pe.ctr and self.pe_pending and (idx, cnt) == (ctr.idx, ctr.cnt + 1):
            raise RuntimeError("waiting on a pending (unsignalled) PE token")
        ins._wait_ge(ctr.sems[idx], eng.h.lower_val(cnt))
        eng.known[(id(ctr), idx)] = cnt

    def op(self, eng, fn, reads=(), writes=(), sig=True):
        last = self._deps(eng, reads, writes, defer_last=ATTACH_WAITS)
        ins = fn(eng.h)
        if last is not None:
            self._attach(eng, ins, last)
        eng.ninst += 1
        if sig:
            tok = eng.ctr.next_token()
            ins.then_inc(tok[0].sems[tok[1]], 1)
            if eng is self.pe:
                self.pe_pending = False
        else:
            assert eng is self.pe
            tok = eng.ctr.peek_token()
            self.pe_pending = True
        for b in reads:
            b.r[eng.name] = tok
        for b in writes:
            b.w = tok
            b.r = {}
        return tok

    def dma(self, eng, out, in_, ds, reads=(), writes=(), **kw):
        self._deps(eng, reads, writes)
        tok = ds.next_token()
        eng.h.dma_start(out=out, in_=in_, **kw).then_inc(tok[0].sems[tok[1]], 16)
        eng.ninst += 1
        for b in reads:
            b.r[id(ds)] = tok
        for b in writes:
            b.w = tok
            b.r = {}
        return tok

    def mm(self, out, lhsT, rhs, start, stop, reads, writes, sig=None):
        if sig is None:
            sig = stop
        return self.op(self.pe, lambda h: h.matmul(out, lhsT, rhs, start=start, stop=stop),
                       reads, writes, sig=sig)

    def tr(self, out, in_, ident, reads, writes, sig=True):
        return self.op(self.pe, lambda h: h.transpose(out, in_, ident), reads, writes, sig=sig)

    def barrier(self):
        assert not self.pe_pending
        for e in self.engs:
            for c in self.counters:
                if e.ctr is c and e is self.pe:
                    continue
                if c.cnt > 0:
                    self._wait(e, (c, c.idx, c.cnt))

    def stats(self):
        return {e.name: (e.ninst, e.nwait) for e in self.engs}


class T:
    def __init__(self, h, name):
        self.h = h
        self.b = Buf(name)
        self.bs = [self.b]

    def __getitem__(self, k):
        return self.h[k]


def bc(ap, shape):
    return ap.broadcast_to(list(shape))


class _Stop(Exception):
    pass


def build_program(nsteps=SEQ // 128, dbg=False, upto=99):
    nc = bass.Bass("TRN2", target_bir_lowering=False)
    dt_in = lambda name, shape: nc.dram_tensor(name, list(shape), F32, kind="ExternalInput").ap()
    x_d = dt_in("x", [NB, SEQ, DM])
    prm_d = dt_in("prm", [128, NPRM])
    sinks_d = dt_in("attn_sinks", [1, 8])
    w_in_d = dt_in("w_in", [DM, IN_W])
    decay_up_d = dt_in("decay_up", [64, 512])
    aaa_up_d = dt_in("aaa_up", [64, 512])
    gate_up_d = dt_in("gate_up", [128, 512])
    w_ba_d = dt_in("w_branch_attn", [512, DM])
    w_br_d = dt_in("w_branch_rwkv", [512, DM])
    w_out_d = dt_in("w_out", [DM, DM])
    w1_d = dt_in("w_ff_in", [DM, DFF])
    w2_d = dt_in("w_ff_out", [DFF, DM])
    y_d = nc.dram_tensor("y", [NB, SEQ, DM], F32, kind="ExternalOutput").ap()
    h_d = nc.dram_tensor("hbuf", [NB, SEQ, DM], F32, kind="Internal").ap()
    dbg_d = {}

    with ExitStack() as st0:
        P = Prog(nc, st0)
        pe, act, dve, pool, sp = P.pe, P.act, P.dve, P.pool, P.sp

        def chk(stage):
            if stage > upto:
                raise _Stop()

        def mk(st, name, shape, dt=F32):
            return T(st.enter_context(nc.sbuf_tensor("s_" + name, list(shape), dt)), name)

        banks = [T(st0.enter_context(nc.psum_tensor(f"bank{i}", [128, 512], F32)), f"bank{i}") for i in range(8)]
        free_banks = list(banks)

        def bank():
            if not free_banks:
                raise RuntimeError("out of PSUM banks")
            return free_banks.pop(0)

        def rel(b):
            assert b not in free_banks
            free_banks.append(b)

        def bf(bk):
            return bk.h[:].bitcast(BF16)

        prm = mk(st0, "prm", [128, NPRM])
        der = mk(st0, "der", [128, 16])
        esink = mk(st0, "esink", [128, 8])
        ident = mk(st0, "ident", [128, 128], BF16)
        ones64 = mk(st0, "ones64", [128, 128], BF16)
        ones1 = mk(st0, "ones1", [128, 128], BF16)
        onesf = mk(st0, "onesf", [128, 128], F32)
        m_lt = mk(st0, "m_lt", [128, 128], BF16)
        m_le = mk(st0, "m_le", [128, 128], BF16)
        m_gt = mk(st0, "m_gt", [128, 128], BF16)
        mask4 = mk(st0, "mask4", [128, 4, 128], BF16)
        stage = [mk(st0, f"stage{i}", [128, 512]) for i in range(1)]
        d_prm = P.dsem("d_prm")
        d_stage = [P.dsem(f"d_stage{i}") for i in range(1)]
        d_x = [P.dsem(f"d_x{i}") for i in range(2)]
        d_st = [P.dsem(f"d_st{i}") for i in range(2)]
        sstate = {"i": 0}

        def pcol(name, j=0, n=1):
            o, w = PC[name]
            return prm[:, o + j:o + j + n]

        P.dma(sp, prm[:], prm_d[:, :], d_prm, writes=[prm.b])
        d_snk = P.dsem("d_snk")
        P.dma(sp, esink[:], bc(sinks_d[0:1, :], [128, 8]), d_snk, writes=[esink.b])
        V = P.op
        V(pool, lambda h: h.memset(onesf[:], 1.0), writes=[onesf.b])
        for t_, cmp_ in ((ident, ALU.is_equal), (m_le, ALU.is_ge), (m_lt, ALU.is_gt)):
            V(pool, lambda h, t_=t_: h.memset(t_[:], 1.0), writes=[t_.b])
            V(pool, lambda h, t_=t_, cmp_=cmp_: h.affine_select(out=t_[:], in_=t_[:], pattern=[[1, 128]],
                                                                compare_op=cmp_, fill=0.0, base=0,
                                                                channel_multiplier=-1),
              reads=[t_.b], writes=[t_.b])
        V(pool, lambda h: h.memset(m_gt[:], 1.0), writes=[m_gt.b])
        V(pool, lambda h: h.affine_select(out=m_gt[:], in_=m_gt[:], pattern=[[-1, 128]], compare_op=ALU.is_gt,
                                          fill=0.0, base=0, channel_multiplier=1),
          reads=[m_gt.b], writes=[m_gt.b])
        for t_, val in ((ones64, 1.0 / 64), (ones1, 1.0)):
            V(pool, lambda h, t_=t_: h.memset(t_[:], 0.0), writes=[t_.b])
            V(pool, lambda h, t_=t_, val=val: h.memset(t_[0:64, 0:64], val), reads=[t_.b], writes=[t_.b])
            V(pool, lambda h, t_=t_, val=val: h.memset(t_[64:128, 64:128], val), reads=[t_.b], writes=[t_.b])
        for j in range(4):
            src = m_lt if j % 2 == 0 else m_le
            V(pool, lambda h, j=j, src=src: h.tensor_copy(out=mask4[:, j, :], in_=src[:]),
              reads=[src.b], writes=[mask4.b])
        V(dve, lambda h: h.tensor_scalar(out=der[:, 0:4], in0=pcol("k_a", 0, 4), scalar1=-1.0, scalar2=1.0,
                                         op0=ALU.mult, op1=ALU.add), reads=[prm.b], writes=[der.b])
        V(dve, lambda h: h.scalar_tensor_tensor(out=der[:, 4:5], in0=pcol("qg"), scalar=0.125, in1=pcol("kg"),
                                                op0=ALU.mult, op1=ALU.mult), reads=[prm.b, der.b], writes=[der.b])
        V(act, lambda h: h.activation(out=esink[:], in_=esink[:], func=AF.Exp), reads=[esink.b], writes=[esink.b])

        cast_rr = {"i": 0}

        wsem = {}
        stg_state = {"i": 0, "e": 0}

        def load_cast_sp(dst_buf, dst_ap, src_ap, ncols, stgs, dsems):
            c0 = 0
            while c0 < ncols:
                i = stg_state["i"] % len(stgs)
                stg_state["i"] += 1
                sg = stgs[i]
                w = min(int(sg.h.shape[-1]), ncols - c0)
                P.dma(sp, sg[:, 0:w], src_ap[:, c0:c0 + w], dsems[i], writes=[sg.b])
                if stg_state["e"] % 2 == 0:
                    V(dve, lambda h, sg=sg, c0=c0, w=w: h.tensor_copy(out=dst_ap[:, c0:c0 + w], in_=sg[:, 0:w]),
                      reads=[sg.b], writes=[dst_buf])
                else:
                    V(act, lambda h, sg=sg, c0=c0, w=w: h.copy(out=dst_ap[:, c0:c0 + w], in_=sg[:, 0:w]),
                      reads=[sg.b], writes=[dst_buf])
                stg_state["e"] += 1
                c0 += w

        def load_cast(dst_t, dst_ap, src_ap, ncols):
            if dst_t.b.name not in wsem:
                wsem[dst_t.b.name] = P.dsem("dw_" + dst_t.b.name)
            P.dma(pool, dst_ap, src_ap, wsem[dst_t.b.name], writes=[dst_t.b])

        with ExitStack() as st1, suppress(_Stop):
            w_in_sb = mk(st1, "w_in_sb", [128, 8, IN_W], BF16)
            wkd = mk(st1, "wkd", [128, 8, 256], BF16)
            w_ba = mk(st1, "w_ba", [128, 4, DM], BF16)
            w_br = mk(st1, "w_br", [128, 4, DM], BF16)
            w_out_sb = mk(st1, "w_out_sb", [128, 8, DM], BF16)
            lora_wa = mk(st1, "lora_wa", [128, 512], BF16)
            lora_g = mk(st1, "lora_g", [128, 512], BF16)

            W_IN_SPLIT = True
            for kc in range(8):
                if not (W_IN_SPLIT and kc % 2 == 1):
                    load_cast(w_in_sb, w_in_sb[:, kc, :], w_in_d[kc * 128:(kc + 1) * 128, :], IN_W)
            for kc in range(4):
                load_cast(w_ba, w_ba[:, kc, :], w_ba_d[kc * 128:(kc + 1) * 128, :], DM)
                load_cast(w_br, w_br[:, kc, :], w_br_d[kc * 128:(kc + 1) * 128, :], DM)
            for kc in range(8):
                load_cast(w_out_sb, w_out_sb[:, kc, :], w_out_d[kc * 128:(kc + 1) * 128, :], DM)
            i = 0
            P.dma(sp, stage[i][0:64, 0:512], decay_up_d[:, :], d_stage[i], writes=[stage[i].b])
            P.dma(sp, stage[i][64:128, 0:512], aaa_up_d[:, :], d_stage[i], writes=[stage[i].b])
            V(dve, lambda h, i=i: h.tensor_copy(out=lora_wa[:], in_=stage[i][:, 0:512]),
              reads=[stage[i].b], writes=[lora_wa.b])
            load_cast(lora_g, lora_g[:], gate_up_d[:, :], 512)

            xt = [mk(st1, f"xt{i}", [128, DM]) for i in range(2)]
            xn = mk(st1, "xn", [128, DM], BF16)
            sm = mk(st1, "sm", [128, 32])
            uT = [mk(st1, f"uT{i}", [128, 8, 128], BF16) for i in range(2)]
            qsq = mk(st1, "qsq", [128, 6, 128], BF16)
            rq = mk(st1, "rq", [128, 6, 128])
            qn = mk(st1, "qn", [128, 4, 128], BF16)
            kn = [mk(st1, f"kn{i}", [128, 2, 2, 128], BF16) for i in range(2)]
            vext = [mk(st1, f"vext{i}", [128, 2, 65], BF16) for i in range(2)]
            pt = [mk(st1, f"pt{i}", [128, 4, 128], BF16) for i in range(2)]
            ao = mk(st1, "ao", [128, 8, 64], BF16)
            aoT = [mk(st1, f"aoT{i}", [128, 4, 128], BF16) for i in range(2)]
            rkvp = mk(st1, "rkvp", [128, 12, 129])
            lorap = mk(st1, "lorap", [128, 2, 129])
            FS = [mk(st1, f"fs{i}", [128, 4, 128]) for i in range(10)]
            r_, k_, v_, lw, a_, g_, kk, Lc, t1, t2 = FS
            kmod = k_
            e1 = t2
            sgt = mk(st1, "sgt", [128, 4, 128])
            zs = mk(st1, "zs", [128, 2, 128])
            zt = mk(st1, "zt", [128, 2, 128])
            twz = mk(st1, "twz", [128, 2, 128], BF16)
            sgb = mk(st1, "sgb", [128, 128], BF16)
            arz = mk(st1, "arz", [128, 4, 2, 2, 128], BF16)
            bk = mk(st1, "bk", [128, 4, 2, 128], BF16)
            hat = mk(st1, "hat", [128, 4, 2, 128], BF16)
            kbh = mk(st1, "kbh", [128, 4, 2, 128], BF16)
            vtok = mk(st1, "vtok", [128, 4, 128], BF16)
            hb16 = mk(st1, "hb16", [128, 4, 128], BF16)
            ssb = mk(st1, "ssb", [128, 8, 4, 128], BF16)
            Xs = mk(st1, "Xs", [128, 8, 128], BF16)
            Ys = mk(st1, "Ys", [128, 8, 128], BF16)
            TTs = mk(st1, "TTs", [128, 8, 128], BF16)
            gbf = mk(st1, "gbf", [128, 8, 64], BF16)
            ubf = mk(st1, "ubf", [128, 8, 64], BF16)
            Hs = mk(st1, "Hs", [128, 4, 64])
            Hb = mk(st1, "Hb", [128, 4, 64], BF16)
            wcs = mk(st1, "wcs", [128, 4, 1])
            ywT = mk(st1, "ywT", [128, 4, 128], BF16)
            mixT = mk(st1, "mixT", [128, 8, 128], BF16)
            print("sbuf remaining after phase1 alloc:", nc.sbuf_bytes_remaining)

            if W_IN_SPLIT:
                w_in_sb.b2 = Buf("w_in_odd")
                w_in_sb.bs = [w_in_sb.b, w_in_sb.b2]
                for kc in range(1, 8, 2):
                    load_cast_sp(w_in_sb.b2, w_in_sb[:, kc, :], w_in_d[kc * 128:(kc + 1) * 128, :], IN_W, xt, d_x)
            for kc in range(8):
                for g in range(2):
                    for dup in range(2):
                        V(pool, lambda h, kc=kc, g=g, dup=dup: h.tensor_copy(
                            out=wkd[:, kc, g * 128 + dup * 64:g * 128 + dup * 64 + 64],
                            in_=w_in_sb[:, kc, 512 + g * 64:512 + g * 64 + 64]),
                          reads=[*w_in_sb.bs], writes=[wkd.b])
            for i in range(2):
                V(pool, lambda h, i=i: h.memset(vext[i][:], 1.0), writes=[vext[i].b])
                V(pool, lambda h, i=i: h.memset(kn[i][:], 0.0), writes=[kn[i].b])
            V(pool, lambda h: h.memset(twz[:], 0.0), writes=[twz.b])
            V(pool, lambda h: h.memset(arz[:], 0.0), writes=[arz.b])

            def load_x(b, n, slot):
                P.dma(sp, xt[slot][:], x_d[b, n * 128:(n + 1) * 128, :], d_x[slot], writes=[xt[slot].b])

            steps = [(b, n) for b in range(NB) for n in range(nsteps)]

            def rsqrt_inplace(t, ap_fn, src_ap_fn, src_reads, scale, bias):
                V(act, lambda h: h.activation(out=ap_fn(), in_=src_ap_fn(), func=AF.Ln, bias=bias, scale=scale),
                  reads=src_reads, writes=[t.b])
                V(act, lambda h: h.activation(out=ap_fn(), in_=ap_fn(), func=AF.Exp, scale=-0.5),
                  reads=[t.b], writes=[t.b])

            def sigmoid_inplace(t, ap_fn, src_ap_fn, src_reads, scale=-1.0):
                V(act, lambda h: h.activation(out=ap_fn(), in_=src_ap_fn(), func=AF.Exp, scale=scale),
                  reads=src_reads, writes=[t.b])
                V(act, lambda h: h.activation(out=ap_fn(), in_=ap_fn(), func=AF.Ln, bias=1.0),
                  reads=[t.b], writes=[t.b])
                V(act, lambda h: h.activation(out=ap_fn(), in_=ap_fn(), func=AF.Exp, scale=-1.0),
                  reads=[t.b], writes=[t.b])

            def v4(bk_):
                return bk_.h[:].rearrange("p (c t) -> p c t", c=4)

            def mucol(name):
                o, w = PC[name]
                return bc(prm[:, o:o + w].unsqueeze(2), [128, w, 128])

            def gen_front(si):
                b, n = steps[si]
                slot = si % 2
                par = n % 2
                first = (n == 0)
                xcur = xt[slot]
                uTc = uT[slot]
                load_x(b, n, slot)
                yield
                V(dve, lambda h: h.memset(sm[:, 0:1], 0.0), writes=[sm.b])
                V(act, lambda h: h.activation(out=xn[:], in_=xcur[:], func=AF.Square, accum_out=sm[:, 0:1]),
                  reads=[xcur.b, sm.b], writes=[xn.b, sm.b])
                rsqrt_inplace(sm, lambda: sm[:, 0:1], lambda: sm[:, 0:1], [sm.b], 1.0 / DM, RMS_EPS)
                V(dve, lambda h: h.tensor_scalar(out=xn[:], in0=xcur[:], scalar1=sm[:, 0:1], scalar2=None,
                                                 op0=ALU.mult), reads=[xcur.b, sm.b], writes=[xn.b])
                yield
                yield
                bk_ = bank()
                for c in range(8):
                    P.tr(bf(bk_)[:, c * 128:(c + 1) * 128], xn[:, c * 128:(c + 1) * 128], ident[:],
                         reads=[xn.b, ident.b], writes=[bk_.b], sig=(c == 7))
                o, _ = PC["g1"]
                V(dve, lambda h: h.tensor_tensor(out=uTc[:], in0=bf(bk_).rearrange("p (c t) -> p c t", c=8),
                                                 in1=bc(prm[:, o:o + 8].unsqueeze(2), [128, 8, 128]), op=ALU.mult),
                  reads=[bk_.b, prm.b], writes=[uTc.b])
                rel(bk_)
                yield

                def proj(out_ap, bkx, wt, col0, sig):
                    for kc in range(8):
                        P.mm(out_ap, wt[:, kc, col0:col0 + 128], uTc[:, kc, :], kc == 0, kc == 7,
                             reads=[*wt.bs, uTc.b], writes=[bkx.b], sig=(sig and kc == 7))

                for qi in range(3):
                    br_ = bank()
                    for c in range(4):
                        proj(v4(br_)[:, c, :], br_, w_in_sb, 768 + (qi * 4 + c) * 128, c == 3)
                        if c == 1:
                            yield
                    V(act, lambda h, qi=qi, br_=br_: h.copy(out=rkvp[:, qi * 4:(qi + 1) * 4, 1:129], in_=v4(br_)),
                      reads=[br_.b], writes=[rkvp.b])
                    rel(br_)
                    yield
                bkk = bank()
                for g in range(2):
                    proj(v4(bkk)[:, g, :], bkk, wkd, g * 128, False)
                proj(v4(bkk)[:, 2, :], bkk, w_in_sb, 2304, False)
                proj(v4(bkk)[:, 3, :], bkk, w_in_sb, 2432, True)
                V(act, lambda h: h.copy(out=lorap[:, :, 1:129], in_=v4(bkk)[:, 2:4, :]), reads=[bkk.b],
                  writes=[lorap.b])
                V(act, lambda h: h.activation(out=qsq[:, 4:6, :], in_=v4(bkk)[:, 0:2, :], func=AF.Square),
                  reads=[bkk.b], writes=[qsq.b])
                yield
                bq = bank()
                for c in range(4):
                    proj(v4(bq)[:, c, :], bq, w_in_sb, c * 128, c == 3)
                    if c == 1:
                        yield
                V(act, lambda h: h.activation(out=qsq[:, 0:4, :], in_=v4(bq), func=AF.Square),
                  reads=[bq.b], writes=[qsq.b])
                yield
                bv = bank()
                for kc in range(8):
                    P.mm(bv[:, 0:128], uTc[:, kc, :], w_in_sb[:, kc, 640:768], kc == 0, kc == 7,
                         reads=[uTc.b, *w_in_sb.bs], writes=[bv.b])
                V(dve, lambda h: h.tensor_copy(out=vext[par][:, :, 0:64],
                                               in_=bv[:, 0:128].rearrange("p (g d) -> p g d", g=2)),
                  reads=[bv.b], writes=[vext[par].b])
                rel(bv)
                yield
                bs1 = bank()
                for c in range(4):
                    P.mm(v4(bs1)[:, c, :], ones64[:], qsq[:, c, :], True, True, reads=[ones64.b, qsq.b],
                         writes=[bs1.b], sig=(c == 3))
                bs2 = bank()
                for c in range(2):
                    P.mm(v4(bs2)[:, c, :], ones64[:], qsq[:, 4 + c, :], True, True, reads=[ones64.b, qsq.b],
                         writes=[bs2.b], sig=(c == 1))
                rsqrt_inplace(rq, lambda: rq[:, 0:4, :], lambda: v4(bs1), [bs1.b], 1.0, RMS_EPS)
                rsqrt_inplace(rq, lambda: rq[:, 4:6, :], lambda: v4(bs2)[:, 0:2, :], [bs2.b], 1.0, RMS_EPS)
                rel(bs1)
                rel(bs2)
                yield
                V(dve, lambda h: h.tensor_tensor(out=qn[:], in0=v4(bq), in1=rq[:, 0:4, :], op=ALU.mult),
                  reads=[bq.b, rq.b], writes=[qn.b])
                for hf in range(2):
                    ps_ = slice(hf * 64, (hf + 1) * 64)
                    V(dve, lambda h, hf=hf, ps_=ps_: h.scalar_tensor_tensor(
                        out=kn[par][ps_, :, hf, :], in0=v4(bkk)[ps_, 0:2, :], scalar=der[ps_, 4:5],
                        in1=rq[ps_, 4:6, :], op0=ALU.mult, op1=ALU.mult),
                      reads=[bkk.b, rq.b, der.b], writes=[kn[par].b])
                rel(bq)
                rel(bkk)
                yield
                for g in range(2):
                    blocks = ([] if first else [(1 - par, m_gt, pt[0])]) + [(par, m_le, pt[1])]
                    for (kp, msk, ptt) in blocks:
                        bs = bank()
                        for j in range(4):
                            hd = 4 * g + j
                            c, hf = hd // 2, hd % 2
                            P.mm(v4(bs)[:, j, :], kn[kp][:, g, hf, :], qn[:, c, :], True, True,
                                 reads=[kn[kp].b, qn.b], writes=[bs.b], sig=(j == 3))
                        V(act, lambda h, bs=bs, ptt=ptt: h.activation(out=ptt[:], in_=v4(bs), func=AF.Exp),
                          reads=[bs.b], writes=[ptt.b])
                        rel(bs)
                        V(pool, lambda h, ptt=ptt, msk=msk: h.tensor_tensor(
                            out=ptt[:], in0=ptt[:], in1=bc(msk[:].unsqueeze(1), [128, 4, 128]), op=ALU.mult),
                          reads=[ptt.b, msk.b], writes=[ptt.b])
                        yield
                    bo = bank()
                    bov = bo.h[:, 0:260].rearrange("p (j d) -> p j d", j=4)
                    for j in range(4):
                        for bi, (kp, msk, ptt) in enumerate(blocks):
                            P.mm(bov[:, j, :], ptt[:, j, :], vext[kp][:, g, :], bi == 0, bi == len(blocks) - 1,
                                 reads=[ptt.b, vext[kp].b], writes=[bo.b],
                                 sig=(j == 3 and bi == len(blocks) - 1))
                    V(dve, lambda h, g=g, bov=bov: h.tensor_tensor(
                        out=sm[:, 8 + 4 * g:12 + 4 * g].unsqueeze(2), in0=bov[:, :, 64:65],
                        in1=esink[:, 4 * g:4 * g + 4].unsqueeze(2), op=ALU.add),
                      reads=[bo.b, esink.b, sm.b], writes=[sm.b])
                    V(act, lambda h, g=g: h.activation(out=sm[:, 8 + 4 * g:12 + 4 * g],
                                                       in_=sm[:, 8 + 4 * g:12 + 4 * g], func=AF.Ln),
                      reads=[sm.b], writes=[sm.b])
                    V(act, lambda h, g=g: h.activation(out=sm[:, 8 + 4 * g:12 + 4 * g],
                                                       in_=sm[:, 8 + 4 * g:12 + 4 * g], func=AF.Exp, scale=-1.0),
                      reads=[sm.b], writes=[sm.b])
                    V(dve, lambda h, g=g, bov=bov: h.tensor_tensor(
                        out=ao[:, 4 * g:4 * g + 4, :], in0=bov[:, :, 0:64],
                        in1=bc(sm[:, 8 + 4 * g:12 + 4 * g].unsqueeze(2), [128, 4, 64]), op=ALU.mult),
                      reads=[bo.b, sm.b], writes=[ao.b])
                    rel(bo)
                    yield
                bt = bank()
                aov = ao[:].rearrange("p h d -> p (h d)")
                for c in range(4):
                    P.tr(bf(bt)[:, c * 128:(c + 1) * 128], aov[:, c * 128:(c + 1) * 128], ident[:],
                         reads=[ao.b, ident.b], writes=[bt.b], sig=(c == 3))
                V(act, lambda h: h.copy(out=aoT[slot][:], in_=bf(bt)[:, 0:512].rearrange("p (c t) -> p c t", c=4)),
                  reads=[bt.b], writes=[aoT[slot].b])
                rel(bt)
                yield

            def gen_rw(si):
                b, n = steps[si]
                first = (n == 0)
                if first:
                    V(pool, lambda h: h.memset(rkvp[:, :, 0:1], 0.0), reads=[rkvp.b], writes=[rkvp.b])
                    V(pool, lambda h: h.memset(lorap[:, :, 0:1], 0.0), reads=[lorap.b], writes=[lorap.b])
                    V(pool, lambda h: h.memset(Hs[:], 0.0), writes=[Hs.b])
                    V(pool, lambda h: h.memset(Hb[:], 0.0), writes=[Hb.b])
                V(dve, lambda h: h.tensor_tensor(out=zs[:], in0=lorap[:, :, 0:128], in1=lorap[:, :, 1:129],
                                                 op=ALU.subtract), reads=[lorap.b], writes=[zs.b])
                V(pool, lambda h: h.tensor_tensor(out=zs[:], in0=zs[:], in1=mucol("mu2"), op=ALU.mult),
                  reads=[zs.b, prm.b], writes=[zs.b])
                V(dve, lambda h: h.tensor_tensor(out=zs[:], in0=zs[:], in1=lorap[:, :, 1:129], op=ALU.add),
                  reads=[zs.b, lorap.b], writes=[zs.b])
                V(pool, lambda h: h.tensor_copy(out=lorap[:, :, 0:1], in_=lorap[:, :, 128:129]),
                  reads=[lorap.b], writes=[lorap.b])
                yield
                sigmoid_inplace(zt, lambda: zt[0:64, 0, :], lambda: zs[0:64, 0, :], [zs.b], scale=2.0)
                sigmoid_inplace(zt, lambda: zt[:, 1, :], lambda: zs[:, 1, :], [zs.b], scale=-1.0)
                V(dve, lambda h: h.tensor_scalar(out=twz[0:64, 0, :], in0=zt[0:64, 0, :], scalar1=-2.0, scalar2=1.0,
                                                 op0=ALU.mult, op1=ALU.add), reads=[zt.b], writes=[twz.b])
                V(dve, lambda h: h.tensor_copy(out=twz[64:128, 1, :], in_=zs[64:128, 0, :]), reads=[zs.b],
                  writes=[twz.b])
                V(dve, lambda h: h.tensor_copy(out=sgb[:], in_=zt[:, 1, :]), reads=[zt.b], writes=[sgb.b])
                yield
                for qi, (dst, mun) in enumerate(((r_, "mu_r"), (k_, "mu_k"), (v_, "mu_v"))):
                    V(dve, lambda h, qi=qi, dst=dst: h.tensor_tensor(out=dst[:], in0=rkvp[:, qi * 4:qi * 4 + 4, 0:128],
                                                                     in1=rkvp[:, qi * 4:qi * 4 + 4, 1:129],
                                                                     op=ALU.subtract),
                      reads=[rkvp.b], writes=[dst.b])
                    V(pool, lambda h, mun=mun, dst=dst: h.tensor_tensor(out=dst[:], in0=dst[:], in1=mucol(mun),
                                                                        op=ALU.mult),
                      reads=[dst.b, prm.b], writes=[dst.b])
                    V(pool, lambda h, qi=qi, dst=dst: h.tensor_tensor(out=dst[:], in0=dst[:],
                                                                      in1=rkvp[:, qi * 4:qi * 4 + 4, 1:129],
                                                                      op=ALU.add),
                      reads=[dst.b, rkvp.b], writes=[dst.b])
                    yield
                V(pool, lambda h: h.tensor_copy(out=rkvp[:, :, 0:1], in_=rkvp[:, :, 128:129]),
                  reads=[rkvp.b], writes=[rkvp.b])
                V(pool, lambda h: h.tensor_tensor(out=kk[:], in0=k_[:], in1=mucol("k_k"), op=ALU.mult),
                  reads=[k_.b, prm.b], writes=[kk.b])
                V(pool, lambda h: h.tensor_tensor(out=hb16[:], in0=kk[:], in1=kk[:], op=ALU.mult),
                  reads=[kk.b], writes=[hb16.b])
                bz1, bz2 = bank(), bank()
                for c in range(4):
                    P.mm(v4(bz1)[:, c, :], lora_wa[:, c * 128:(c + 1) * 128], twz[:, 0, :], True, True,
                         reads=[lora_wa.b, twz.b], writes=[bz1.b], sig=(c == 3))
                for c in range(4):
                    P.mm(v4(bz2)[:, c, :], lora_wa[:, c * 128:(c + 1) * 128], twz[:, 1, :], True, True,
                         reads=[lora_wa.b, twz.b], writes=[bz2.b], sig=(c == 3))
                V(dve, lambda h: h.tensor_tensor(out=lw[:], in0=v4(bz1), in1=mucol("db"), op=ALU.add),
                  reads=[bz1.b, prm.b], writes=[lw.b])
                V(dve, lambda h: h.tensor_tensor(out=a_[:], in0=v4(bz2), in1=mucol("ab"), op=ALU.add),
                  reads=[bz2.b, prm.b], writes=[a_.b])
                rel(bz1)
                rel(bz2)
                yield
                bs3 = bank()
                for c in range(4):
                    P.mm(v4(bs3)[:, c, :], ones1[:], hb16[:, c, :], True, True, reads=[ones1.b, hb16.b],
                         writes=[bs3.b], sig=(c == 3))
                sigmoid_inplace(lw, lambda: lw[:], lambda: lw[:], [lw.b])
                V(pool, lambda h: h.tensor_scalar(out=lw[:], in0=lw[:], scalar1=DECAY_C, scalar2=None, op0=ALU.mult),
                  reads=[lw.b], writes=[lw.b])
                yield
                sigmoid_inplace(a_, lambda: a_[:], lambda: a_[:], [a_.b])
                rsqrt_inplace(t1, lambda: t1[:], lambda: v4(bs3), [bs3.b], 1.0, 1e-30)
                rel(bs3)
                yield
                for c in range(4):
                    V(dve, lambda h, c=c: h.tensor_tensor_scan(out=Lc[:, c, :], data0=onesf[:], data1=lw[:, c, :],
                                                               initial=0.0, op0=ALU.mult, op1=ALU.add),
                      reads=[onesf.b, lw.b], writes=[Lc.b])
                V(dve, lambda h: h.tensor_tensor(out=kk[:], in0=kk[:], in1=t1[:], op=ALU.mult),
                  reads=[kk.b, t1.b], writes=[kk.b])
                yield
                V(pool, lambda h: h.tensor_tensor(out=t2[:], in0=a_[:], in1=mucol("k_a"), op=ALU.mult),
                  reads=[a_.b, prm.b], writes=[t2.b])
                V(pool, lambda h: h.tensor_tensor(out=t2[:], in0=t2[:], in1=bc(der[:, 0:4].unsqueeze(2), [128, 4, 128]),
                                                  op=ALU.add), reads=[t2.b, der.b], writes=[t2.b])
                V(dve, lambda h: h.tensor_tensor(out=kmod[:], in0=k_[:], in1=t2[:], op=ALU.mult),
                  reads=[k_.b, t2.b], writes=[kmod.b])
                yield
                V(act, lambda h: h.activation(out=e1[:], in_=Lc[:], func=AF.Exp), reads=[Lc.b], writes=[e1.b])
                V(pool, lambda h: h.tensor_tensor(out=t1[:], in0=Lc[:], in1=lw[:], op=ALU.subtract),
                  reads=[Lc.b, lw.b], writes=[t1.b])
                for hf in range(2):
                    ps_ = slice(hf * 64, (hf + 1) * 64)
                    V(dve, lambda h, hf=hf, ps_=ps_: h.tensor_tensor(out=arz[ps_, :, hf, 1, :], in0=r_[ps_, :, :],
                                                                     in1=e1[ps_, :, :], op=ALU.mult),
                      reads=[r_.b, e1.b], writes=[arz.b])
                V(pool, lambda h: h.tensor_copy(out=wcs[:], in_=e1[:, :, 127:128]), reads=[e1.b], writes=[wcs.b])
                V(act, lambda h: h.activation(out=t1[:], in_=t1[:], func=AF.Exp), reads=[t1.b], writes=[t1.b])
                yield
                for hf in range(2):
                    ps_ = slice(hf * 64, (hf + 1) * 64)
                    V(dve, lambda h, hf=hf, ps_=ps_: h.scalar_tensor_tensor(
                        out=arz[ps_, :, hf, 0, :], in0=kk[ps_, :, :], scalar=-1.0, in1=t1[ps_, :, :],
                        op0=ALU.mult, op1=ALU.mult), reads=[kk.b, t1.b], writes=[arz.b])
                V(act, lambda h: h.activation(out=t2[:], in_=Lc[:], func=AF.Exp, scale=-1.0),
                  reads=[Lc.b], writes=[t2.b])
                V(pool, lambda h: h.tensor_tensor(out=t1[:], in0=kk[:], in1=a_[:], op=ALU.mult),
                  reads=[kk.b, a_.b], writes=[t1.b])
                yield
                V(dve, lambda h: h.tensor_tensor(out=bk[:, :, 1, :], in0=kmod[:], in1=t2[:], op=ALU.mult),
                  reads=[kmod.b, t2.b], writes=[bk.b])
                V(dve, lambda h: h.tensor_tensor(out=bk[:, :, 0, :], in0=t1[:], in1=t2[:], op=ALU.mult),
                  reads=[t1.b, t2.b], writes=[bk.b])
                V(pool, lambda h: h.tensor_tensor(out=hat[:], in0=bk[:],
                                                  in1=bc(wcs[:].unsqueeze(2), [128, 4, 2, 128]), op=ALU.mult),
                  reads=[bk.b, wcs.b], writes=[hat.b])
                yield
                arv = arz[:].rearrange("p c h w t -> p c h (w t)")
                for hd in range(8):
                    c, hf = hd // 2, hd % 2
                    bsx = bank()
                    P.mm(bsx[:, 0:256], bk[:, c, 0, :], arv[:, c, hf, :], True, True,
                         reads=[bk.b, arz.b], writes=[bsx.b], sig=False)
                    P.mm(bsx[:, 256:512], bk[:, c, 1, :], arv[:, c, hf, :], True, True,
                         reads=[bk.b, arz.b], writes=[bsx.b], sig=True)
                    V(dve, lambda h, hd=hd, bsx=bsx: h.tensor_tensor(out=ssb[:, hd, :, :], in0=v4(bsx), in1=mask4[:],
                                                                     op=ALU.mult),
                      reads=[bsx.b, mask4.b], writes=[ssb.b])
                    rel(bsx)
                    if hd % 2 == 1:
                        yield
                for q in range(2):
                    by = bank()
                    for j in range(4):
                        hd = q * 4 + j
                        c, hf = hd // 2, hd % 2
                        P.mm(v4(by)[:, j, :], arz[:, c, hf, 0, :], bk[:, c, 0, :], True, True,
                             reads=[arz.b, bk.b], writes=[by.b], sig=(j == 3))
                    V(dve, lambda h, q=q, by=by: h.tensor_tensor(out=Ys[:, 4 * q:4 * q + 4, :], in0=v4(by),
                                                                 in1=bc(m_gt[:].unsqueeze(1), [128, 4, 128]),
                                                                 op=ALU.mult),
                      reads=[by.b, m_gt.b], writes=[Ys.b])
                    rel(by)
                yield
                bh = bank()
                for c in range(4):
                    for w in range(2):
                        o = (c * 2 + w) * 128
                        P.tr(bf(bh)[:, o:o + 128], hat[:, c, w, :], ident[:], reads=[hat.b, ident.b],
                             writes=[bh.b], sig=(c == 3 and w == 1))
                V(act, lambda h: h.copy(out=kbh[:].rearrange("p c w t -> p (c w t)"), in_=bf(bh)),
                  reads=[bh.b], writes=[kbh.b])
                rel(bh)
                V(pool, lambda h: h.tensor_copy(out=hb16[:], in_=v_[:]), reads=[v_.b], writes=[hb16.b])
                V(pool, lambda h: h.tensor_tensor(out=TTs[:], in0=ssb[:, :, 0, :],
                                                  in1=bc(ident[:].unsqueeze(1), [128, 8, 128]), op=ALU.add),
                  reads=[ssb.b, ident.b], writes=[TTs.b])
                yield
                bvt = bank()
                for c in range(4):
                    P.tr(bf(bvt)[:, c * 128:(c + 1) * 128], hb16[:, c, :], ident[:], reads=[hb16.b, ident.b],
                         writes=[bvt.b], sig=(c == 3))
                V(act, lambda h: h.copy(out=vtok[:].rearrange("p c t -> p (c t)"), in_=bf(bvt)[:, 0:512]),
                  reads=[bvt.b], writes=[vtok.b])
                rel(bvt)
                for kr in range(6):
                    Xin = (lambda hd: ssb[:, hd, 0, :]) if kr == 0 else (lambda hd: Xs[:, hd, :])
                    xb_ = ssb.b if kr == 0 else Xs.b
                    bxs = [bank(), bank()] if kr < 5 else None
                    bys = [bank(), bank()]
                    for q in range(2):
                        for j in range(4):
                            hd = 4 * q + j
                            if kr < 5:
                                P.mm(v4(bxs[q])[:, j, :], Ys[:, hd, :], Xin(hd), True, True,
                                     reads=[Ys.b, xb_], writes=[bxs[q].b], sig=(j == 3))
                        for j in range(4):
                            hd = 4 * q + j
                            P.mm(v4(bys[q])[:, j, :], Xin(hd), Ys[:, hd, :], True, True,
                                 reads=[Ys.b, xb_], writes=[bys[q].b], sig=(j == 3))
                    yield
                    for q in range(2):
                        if kr < 5:
                            V(act, lambda h, q=q, bxs=bxs: h.copy(out=Xs[:, 4 * q:4 * q + 4, :], in_=v4(bxs[q])),
                              reads=[bxs[q].b], writes=[Xs.b])
                            rel(bxs[q])
                        V(dve, lambda h, q=q, bys=bys: h.tensor_copy(out=Ys[:, 4 * q:4 * q + 4, :], in_=v4(bys[q])),
                          reads=[bys[q].b], writes=[Ys.b])
                        rel(bys[q])
                    yield
                    bts = [bank(), bank()]
                    for hd in range(8):
                        P.mm(v4(bts[hd // 4])[:, hd % 4, :], Ys[:, hd, :], TTs[:, hd, :], True, True,
                             reads=[Ys.b, TTs.b], writes=[bts[hd // 4].b], sig=(hd % 4 == 3))
                    for q in range(2):
                        V(dve, lambda h, q=q, bts=bts: h.tensor_tensor(out=TTs[:, 4 * q:4 * q + 4, :], in0=v4(bts[q]),
                                                                       in1=TTs[:, 4 * q:4 * q + 4, :], op=ALU.add),
                          reads=[bts[q].b, TTs.b], writes=[TTs.b])
                        rel(bts[q])
                    yield
                V(pool, lambda h: h.tensor_tensor(out=t1[:], in0=r_[:], in1=mucol("r_k"), op=ALU.mult),
                  reads=[r_.b, prm.b], writes=[t1.b])
                V(pool, lambda h: h.tensor_tensor(out=hb16[:], in0=t1[:], in1=kmod[:], op=ALU.mult),
                  reads=[t1.b, kmod.b], writes=[hb16.b])
                bg = bank()
                bgv = bg.h[:].rearrange("p (h d) -> p h d", h=8)
                for hd in range(8):
                    c, hf = hd // 2, hd % 2
                    p0 = hf * 64
                    if not first:
                        P.mm(bgv[:, hd, :], arz[:, c, hf, 0, :], Hb[:, c, :], True, False,
                             reads=[arz.b, Hb.b], writes=[bg.b])
                    P.mm(bgv[:, hd, :], ssb[:, hd, 2, :], vtok[:, c, p0:p0 + 64], first, True,
                         reads=[ssb.b, vtok.b], writes=[bg.b], sig=(hd == 7))
                V(act, lambda h: h.copy(out=gbf[:], in_=bgv), reads=[bg.b], writes=[gbf.b])
                rel(bg)
                yield
                bbs = bank()
                for c in range(4):
                    P.mm(v4(bbs)[:, c, :], ones1[:], hb16[:, c, :], True, True, reads=[ones1.b, hb16.b],
                         writes=[bbs.b], sig=(c == 3))
                bonus = kmod
                V(dve, lambda h: h.tensor_tensor(out=bonus[:], in0=v4(bbs), in1=v_[:], op=ALU.mult),
                  reads=[bbs.b, v_.b, kmod.b], writes=[bonus.b])
                rel(bbs)
                V(pool, lambda h: h.tensor_tensor(out=r_[:], in0=g_[:], in1=mucol("lng"), op=ALU.mult),
                  reads=[g_.b, prm.b], writes=[r_.b])
                V(pool, lambda h: h.tensor_tensor(out=v_[:], in0=bonus[:], in1=mucol("lnb"), op=ALU.add),
                  reads=[bonus.b, prm.b], writes=[v_.b])
                V(pool, lambda h: h.tensor_tensor(out=v_[:], in0=v_[:], in1=g_[:], op=ALU.mult),
                  reads=[v_.b, g_.b], writes=[v_.b])
                bu = bank()
                buv = bu.h[:].rearrange("p (h d) -> p h d", h=8)
                for hd in range(8):
                    P.mm(buv[:, hd, :], TTs[:, hd, :], gbf[:, hd, :], True, True, reads=[TTs.b, gbf.b],
                         writes=[bu.b], sig=(hd == 7))
                V(dve, lambda h: h.tensor_copy(out=ubf[:], in_=buv), reads=[bu.b], writes=[ubf.b])
                rel(bu)
                yield
                bhn = bank()
                bhv = bhn.h[:, 0:256].rearrange("p (c d) -> p c d", c=4)
                byp = bank()
                for hd in range(8):
                    c, hf = hd // 2, hd % 2
                    p0 = hf * 64
                    if not first:
                        P.mm(v4(byp)[p0:p0 + 64, c, :], Hb[:, c, :], arz[:, c, hf, 1, :], True, False,
                             reads=[Hb.b, arz.b], writes=[byp.b])
                    P.mm(v4(byp)[p0:p0 + 64, c, :], ubf[:, hd, :], ssb[:, hd, 1, :], first, False,
                         reads=[ubf.b, ssb.b], writes=[byp.b])
                    P.mm(v4(byp)[p0:p0 + 64, c, :], vtok[:, c, p0:p0 + 64], ssb[:, hd, 3, :], False, True,
                         reads=[vtok.b, ssb.b], writes=[byp.b], sig=(hd == 7))
                for hd in range(8):
                    c, hf = hd // 2, hd % 2
                    p0 = hf * 64
                    P.mm(bhv[p0:p0 + 64, c, :], kbh[:, c, 0, p0:p0 + 64], ubf[:, hd, :], True, False,
                         reads=[kbh.b, ubf.b], writes=[bhn.b])
                    P.mm(bhv[p0:p0 + 64, c, :], kbh[:, c, 1, p0:p0 + 64], vtok[:, c, p0:p0 + 64], False, True,
                         reads=[kbh.b, vtok.b], writes=[bhn.b], sig=(hd == 7))
                V(pool, lambda h: h.tensor_tensor(out=Hs[:], in0=Hs[:], in1=bc(wcs[:], [128, 4, 64]), op=ALU.mult),
                  reads=[Hs.b, wcs.b], writes=[Hs.b])
                V(dve, lambda h: h.tensor_tensor(out=Hs[:], in0=bhv, in1=Hs[:], op=ALU.add),
                  reads=[bhn.b, Hs.b], writes=[Hs.b])
                rel(bhn)
                V(pool, lambda h: h.tensor_copy(out=Hb[:], in_=Hs[:]), reads=[Hs.b], writes=[Hb.b])
                yield
                ysb, yc = lw, a_
                V(act, lambda h: h.copy(out=ysb[:], in_=v4(byp)), reads=[byp.b], writes=[ysb.b])
                rel(byp)
                V(pool, lambda h: h.tensor_copy(out=hb16[:], in_=ysb[:]), reads=[ysb.b], writes=[hb16.b])
                bm = bank()
                for c in range(4):
                    P.mm(v4(bm)[:, c, :], ones64[:], hb16[:, c, :], True, True, reads=[ones64.b, hb16.b],
                         writes=[bm.b], sig=(c == 3))
                V(dve, lambda h: h.tensor_tensor(out=yc[:], in0=ysb[:], in1=v4(bm), op=ALU.subtract),
                  reads=[ysb.b, bm.b], writes=[yc.b])
                rel(bm)
                yield
                V(pool, lambda h: h.tensor_tensor(out=hb16[:], in0=yc[:], in1=yc[:], op=ALU.mult),
                  reads=[yc.b], writes=[hb16.b])
                bvar = bank()
                for c in range(4):
                    P.mm(v4(bvar)[:, c, :], ones64[:], hb16[:, c, :], True, True, reads=[ones64.b, hb16.b],
                         writes=[bvar.b], sig=(c == 3))
                rsqrt_inplace(t1, lambda: t1[:], lambda: v4(bvar), [bvar.b], 1.0, GN_EPS)
                rel(bvar)
                yield
                V(dve, lambda h: h.tensor_tensor(out=yc[:], in0=yc[:], in1=t1[:], op=ALU.mult),
                  reads=[yc.b, t1.b], writes=[yc.b])
                V(dve, lambda h: h.tensor_tensor(out=yc[:], in0=yc[:], in1=r_[:], op=ALU.mult),
                  reads=[yc.b, r_.b], writes=[yc.b])
                yield

            def rw_gate_lora():
                bz3 = bank()
                for c in range(4):
                    P.mm(v4(bz3)[:, c, :], lora_g[:, c * 128:(c + 1) * 128], sgb[:], True, True,
                         reads=[lora_g.b, sgb.b], writes=[bz3.b], sig=(c == 3))
                V(act, lambda h: h.copy(out=g_[:], in_=v4(bz3)), reads=[bz3.b], writes=[g_.b])
                rel(bz3)

            def rw_finish():
                yc = a_
                V(dve, lambda h: h.tensor_tensor(out=ywT[:], in0=yc[:], in1=v_[:], op=ALU.add),
                  reads=[yc.b, v_.b], writes=[ywT.b])

            def gen_back(si):
                b, n = steps[si]
                slot = si % 2
                xcur = xt[slot]
                uTc = uT[slot]
                aoTc = aoT[slot]
                for fp in range(4):
                    bgt = bank()
                    for j in range(4):
                        col0 = (2560 if j < 2 else 3584) + (2 * fp + (j % 2)) * 128
                        for kc in range(8):
                            P.mm(v4(bgt)[:, j, :], w_in_sb[:, kc, col0:col0 + 128], uTc[:, kc, :], kc == 0, kc == 7,
                                 reads=[*w_in_sb.bs, uTc.b], writes=[bgt.b], sig=(j == 3 and kc == 7))
                        if j == 1:
                            yield
                    sigmoid_inplace(sgt, lambda: sgt[:], lambda bgt=bgt: v4(bgt), [bgt.b])
                    rel(bgt)
                    yield
                    byy = bank()
                    for j in range(2):
                        for kc in range(4):
                            P.mm(v4(byy)[:, j, :], w_ba[:, kc, (2 * fp + j) * 128:(2 * fp + j + 1) * 128], aoTc[:, kc, :],
                                 kc == 0, kc == 3, reads=[w_ba.b, aoTc.b], writes=[byy.b], sig=False)
                    for j in range(2):
                        for kc in range(4):
                            P.mm(v4(byy)[:, 2 + j, :], w_br[:, kc, (2 * fp + j) * 128:(2 * fp + j + 1) * 128],
                                 ywT[:, kc, :], kc == 0, kc == 3, reads=[w_br.b, ywT.b], writes=[byy.b],
                                 sig=(j == 1 and kc == 3))
                    V(dve, lambda h, byy=byy: h.tensor_tensor(out=sgt[:], in0=sgt[:], in1=v4(byy), op=ALU.mult),
                      reads=[sgt.b, byy.b], writes=[sgt.b])
                    rel(byy)
                    V(pool, lambda h, fp=fp: h.tensor_tensor(out=mixT[:, 2 * fp:2 * fp + 2, :], in0=sgt[:, 0:2, :],
                                                             in1=sgt[:, 2:4, :], op=ALU.add),
                      reads=[sgt.b], writes=[mixT.b])
                    yield
                for hh in range(2):
                    bw = bank()
                    for kc in range(8):
                        P.mm(bw[:], mixT[:, kc, :], w_out_sb[:, kc, hh * 512:(hh + 1) * 512], kc == 0, kc == 7,
                             reads=[mixT.b, w_out_sb.b], writes=[bw.b])
                    V(dve, lambda h, hh=hh, bw=bw: h.tensor_tensor(out=xcur[:, hh * 512:(hh + 1) * 512], in0=bw[:],
                                                                   in1=xcur[:, hh * 512:(hh + 1) * 512], op=ALU.add),
                      reads=[bw.b, xcur.b], writes=[xcur.b])
                    rel(bw)
                    yield
                P.dma(sp, h_d[b, n * 128:(n + 1) * 128, :], xcur[:], d_st[slot], reads=[xcur.b])
                yield

            def gen_rw_full(si):
                g = gen_rw(si)
                cnt = 0
                for _ in g:
                    cnt += 1
                    if cnt == 6:
                        rw_gate_lora()
                    yield
                rw_finish()
                yield

            def chain_gens(*gs):
                for g in gs:
                    if g is not None:
                        yield from g

            def interleave(g1, g2, r1=1, r2=1):
                a1 = a2 = True
                while a1 or a2:
                    for _ in range(r1):
                        if a1:
                            try:
                                next(g1)
                            except StopIteration:
                                a1 = False
                    for _ in range(r2):
                        if a2:
                            try:
                                next(g2)
                            except StopIteration:
                                a2 = False

            def run_all(g):
                for _ in g:
                    pass

            NS = len(steps)
            run_all(gen_front(0))
            for si in range(NS):
                g2 = chain_gens(gen_back(si - 1) if si > 0 else None,
                                gen_front(si + 1) if si + 1 < NS else None)
                interleave(gen_rw_full(si), g2)
            run_all(gen_back(NS - 1))
            P.barrier()
            print("phase1 stats", P.stats())

        with ExitStack() as st2, suppress(_Stop):
            w1_sb = mk(st2, "w1_sb", [128, 8, DFF], BF16)
            w2_sb = mk(st2, "w2_sb", [128, 32, DM], BF16)
            stg2 = [mk(st2, f"stg2_{i}", [128, 2048]) for i in range(2)]
            d_stg2 = [P.dsem(f"d_stg2_{i}") for i in range(2)]
            w1_sb.b2 = Buf("w1_odd")
            w1_sb.bs = [w1_sb.b, w1_sb.b2]
            w2_sb.b2 = Buf("w2_odd")
            w2_sb.bs = [w2_sb.b, w2_sb.b2]
            for kc in range(8):
                if kc % 2 == 0:
                    load_cast(w1_sb, w1_sb[:, kc, :], w1_d[kc * 128:(kc + 1) * 128, :], DFF)
                else:
                    load_cast_sp(w1_sb.b2, w1_sb[:, kc, :], w1_d[kc * 128:(kc + 1) * 128, :], DFF, stg2, d_stg2)
            for fc in range(32):
                if fc % 2 == 0:
                    load_cast(w2_sb, w2_sb[:, fc, :], w2_d[fc * 128:(fc + 1) * 128, :], DM)
                else:
                    load_cast_sp(w2_sb.b2, w2_sb[:, fc, :], w2_d[fc * 128:(fc + 1) * 128, :], DM, stg2, d_stg2)
            NT = 2
            ht = [mk(st2, f"ht{i}", [128, NT, DM]) for i in range(2)]
            xn2 = mk(st2, "xn2", [128, DM], BF16)
            sm2 = mk(st2, "sm2", [128, 4])
            u2T = mk(st2, "u2T", [128, 8, NT * 128], BF16)
            hT = mk(st2, "hT", [128, 32, NT * 128], BF16)
            print("sbuf remaining after phase2 alloc:", nc.sbuf_bytes_remaining)
            d_h = [P.dsem(f"d_h{i}") for i in range(2)]
            d_y = [P.dsem(f"d_y{i}") for i in range(2)]
            steps2 = [(b, n2) for b in range(NB) for n2 in range(nsteps // NT)]

            def load_h(b, n2, slot):
                for j in range(NT):
                    r0 = (n2 * NT + j) * 128
                    P.dma(sp, ht[slot][:, j, :], h_d[b, r0:r0 + 128, :], d_h[slot], writes=[ht[slot].b])

            load_h(steps2[0][0], steps2[0][1], 0)
            for si, (b, n2) in enumerate(steps2):
                slot = si % 2
                hcur = ht[slot]
                if si + 1 < len(steps2):
                    load_h(steps2[si + 1][0], steps2[si + 1][1], 1 - slot)
                for j in range(NT):
                    V(dve, lambda h: h.memset(sm2[:, 0:1], 0.0), writes=[sm2.b])
                    V(act, lambda h, j=j: h.activation(out=xn2[:], in_=hcur[:, j, :], func=AF.Square,
                                                       accum_out=sm2[:, 0:1]),
                      reads=[hcur.b, sm2.b], writes=[xn2.b, sm2.b])
                    V(act, lambda h: h.activation(out=sm2[:, 0:1], in_=sm2[:, 0:1], func=AF.Ln, bias=RMS_EPS,
                                                  scale=1.0 / DM), reads=[sm2.b], writes=[sm2.b])
                    V(act, lambda h: h.activation(out=sm2[:, 0:1], in_=sm2[:, 0:1], func=AF.Exp, scale=-0.5),
                      reads=[sm2.b], writes=[sm2.b])
                    V(dve, lambda h, j=j: h.tensor_scalar(out=xn2[:], in0=hcur[:, j, :], scalar1=sm2[:, 0:1],
                                                          scalar2=None, op0=ALU.mult),
                      reads=[hcur.b, sm2.b], writes=[xn2.b])
                    bk_ = bank()
                    for c in range(8):
                        P.tr(bf(bk_)[:, c * 128:(c + 1) * 128], xn2[:, c * 128:(c + 1) * 128], ident[:],
                             reads=[xn2.b, ident.b], writes=[bk_.b], sig=(c == 7))
                    o, _ = PC["g2"]
                    V(dve, lambda h, j=j, bk_=bk_: h.tensor_tensor(
                        out=u2T[:, :, j * 128:(j + 1) * 128], in0=bf(bk_).rearrange("p (c t) -> p c t", c=8),
                        in1=bc(prm[:, o:o + 8].unsqueeze(2), [128, 8, 128]), op=ALU.mult),
                      reads=[bk_.b, prm.b], writes=[u2T.b])
                    rel(bk_)
                W = NT * 128
                for fq in range(16):
                    bff = bank()
                    for j in range(2):
                        fc = 2 * fq + j
                        for kc in range(8):
                            P.mm(bff[:, j * W:(j + 1) * W], w1_sb[:, kc, fc * 128:(fc + 1) * 128], u2T[:, kc, :],
                                 kc == 0, kc == 7, reads=[*w1_sb.bs, u2T.b], writes=[bff.b],
                                 sig=(j == 1 and kc == 7))
                    V(act, lambda h, fq=fq, bff=bff: h.activation(
                        out=hT[:, 2 * fq:2 * fq + 2, :].rearrange("p a t -> p (a t)"), in_=bff[:, 0:2 * W],
                        func=AF.Relu), reads=[bff.b], writes=[hT.b])
                    rel(bff)
                    V(pool, lambda h, fq=fq: h.tensor_tensor(out=hT[:, 2 * fq:2 * fq + 2, :],
                                                             in0=hT[:, 2 * fq:2 * fq + 2, :],
                                                             in1=hT[:, 2 * fq:2 * fq + 2, :], op=ALU.mult),
                      reads=[hT.b], writes=[hT.b])
                for j in range(NT):
                    for hh in range(2):
                        bo2 = bank()
                        for fc in range(32):
                            P.mm(bo2[:], hT[:, fc, j * 128:(j + 1) * 128], w2_sb[:, fc, hh * 512:(hh + 1) * 512],
                                 fc == 0, fc == 31, reads=[hT.b, *w2_sb.bs], writes=[bo2.b])
                        V(dve, lambda h, j=j, hh=hh, bo2=bo2: h.tensor_tensor(
                            out=hcur[:, j, hh * 512:(hh + 1) * 512], in0=bo2[:],
                            in1=hcur[:, j, hh * 512:(hh + 1) * 512], op=ALU.add),
                          reads=[bo2.b, hcur.b], writes=[hcur.b])
                        rel(bo2)
                    r0 = (n2 * NT + j) * 128
                    P.dma(sp, y_d[b, r0:r0 + 128, :], hcur[:, j, :], d_y[slot], reads=[hcur.b])
            for i in range(2):
                if d_y[i].cnt > 0:
                    P._wait(sp, (d_y[i], d_y[i].idx, d_y[i].cnt))
            print("final stats", P.stats())
    return nc


def _pack_params(inp):
    prm = np.zeros((128, NPRM), np.float32)

    def put(name, vec):
        o, w = PC[name]
        prm[:, o:o + w] = np.asarray(vec, np.float32).reshape(w, 128).T

    put("g1", inp["norm1_gain"][0])
    put("g2", inp["norm2_gain"][0])
    put("mu_r", inp["mu_r"][0])
    put("mu_k", inp["mu_k"][0])
    put("mu_v", inp["mu_v"][0])
    put("mu2", np.concatenate([inp["mu_w"][0], inp["mu_a"][0], inp["mu_g"][0]]))
    put("db", inp["decay_bias"][0])
    put("ab", inp["aaa_bias"][0])
    put("k_k", inp["k_k"][0])
    put("k_a", inp["k_a"][0])
    put("r_k", inp["r_k"][0])
    put("lng", inp["ln_x_gain"][0])
    put("lnb", inp["ln_x_bias"][0])
    put("qg", np.concatenate([inp["q_norm_gain"][0]] * 2))
    put("kg", np.concatenate([inp["k_norm_gain"][0]] * 2))
    return prm


_NC_CACHE = {}


def kernel(**inputs):
    inp = {k: np.asarray(v) for k, v in inputs.items()}
    x = np.ascontiguousarray(inp["x"], dtype=np.float32)
    prm = _pack_params(inp)
    shared = {
        "prm": prm,
        "attn_sinks": np.ascontiguousarray(inp["attn_sinks"], np.float32).reshape(1, 8),
        "w_in": np.ascontiguousarray(inp["w_in"][0], np.float32),
        "decay_up": np.ascontiguousarray(inp["decay_up"][0], np.float32),
        "aaa_up": np.ascontiguousarray(inp["aaa_up"][0], np.float32),
        "gate_up": np.ascontiguousarray(inp["gate_up"][0], np.float32),
        "w_branch_attn": np.ascontiguousarray(inp["w_branch_attn"][0], np.float32),
        "w_branch_rwkv": np.ascontiguousarray(inp["w_branch_rwkv"][0], np.float32),
        "w_out": np.ascontiguousarray(inp["w_out"][0], np.float32),
        "w_ff_in": np.ascontiguousarray(inp["w_ff_in"][0], np.float32),
        "w_ff_out": np.ascontiguousarray(inp["w_ff_out"][0], np.float32),
    }
    if "nc" not in _NC_CACHE:
        _NC_CACHE["nc"] = build_program()
    nc = _NC_CACHE["nc"]
    in_maps = []
    for c in range(NCORES):
        m = dict(shared)
        m["x"] = np.ascontiguousarray(x[c * NB:(c + 1) * NB])
        in_maps.append(m)
    res = run_bass_kernel_spmd(nc, in_maps, core_ids=list(range(NCORES)))
    out = np.concatenate([np.asarray(r["y"], np.float32).reshape(NB, SEQ, DM) for r in res.results], axis=0)
    return out.astype(np.float32)
```

```python
import numpy as np
from contextlib import ExitStack, suppress
import concourse.bass as bass
import concourse.mybir as mybir
from concourse.bass_utils import run_bass_kernel_spmd

F32 = mybir.dt.float32
BF16 = mybir.dt.bfloat16
ALU = mybir.AluOpType
AF = mybir.ActivationFunctionType

SEM_ROT = 6000
ATTACH_WAITS = True
NCORES = 8
SEQ = 2048
DM = 1024
NB = 2
IN_W = 4608
DFF = 4096
RMS_EPS = 1e-6
GN_EPS = 64e-5
DECAY_C = -0.6065306597126334

PC = {}
_o = 0
for _n, _w in [("g1", 8), ("g2", 8), ("mu_r", 4), ("mu_k", 4), ("mu_v", 4), ("mu2", 2),
               ("db", 4), ("ab", 4), ("k_k", 4), ("k_a", 4), ("r_k", 4), ("lng", 4), ("lnb", 4),
               ("qg", 1), ("kg", 1)]:
    PC[_n] = (_o, _w)
    _o += _w
NPRM = _o


class Buf:
    __slots__ = ("name", "w", "r")

    def __init__(self, name):
        self.name = name
        self.w = None
        self.r = {}


class Counter:
    def __init__(self, prog, name, step):
        self.prog = prog
        self.name = name
        self.step = step
        self.sems = []
        self.idx = -1
        self.cnt = 0
        self._new_sem()
        prog.counters.append(self)

    def _new_sem(self):
        s = self.prog.stack.enter_context(self.prog.nc.semaphore(f"{self.name}_{len(self.sems)}"))
        self.sems.append(s)
        self.idx += 1
        self.cnt = 0

    def next_token(self):
        if self.cnt >= SEM_ROT * self.step:
            self._new_sem()
        self.cnt += self.step
        return (self, self.idx, self.cnt)

    def peek_token(self):
        if self.cnt >= SEM_ROT * self.step:
            self._new_sem()
        return (self, self.idx, self.cnt + self.step)


class Eng:
    def __init__(self, prog, name, handle, compute=True):
        self.name = name
        self.h = handle
        self.ctr = Counter(prog, name, 1) if compute else None
        self.known = {}
        self.nwait = 0
        self.ninst = 0


class Prog:
    def __init__(self, nc, stack):
        self.nc = nc
        self.stack = stack
        self.counters = []
        self.pe = Eng(self, "pe", nc.tensor)
        self.act = Eng(self, "act", nc.scalar)
        self.dve = Eng(self, "dve", nc.vector)
        self.pool = Eng(self, "pool", nc.gpsimd)
        self.sp = Eng(self, "sp", nc.sync, compute=False)
        self.engs = [self.pe, self.act, self.dve, self.pool, self.sp]
        self.pe_pending = False
        self.snap = {}

    def _learn(self, eng, tok):
        ctr, idx, cnt = tok
        eng.known[(id(ctr), idx)] = max(eng.known.get((id(ctr), idx), 0), cnt)
        sn = self.snap.get((id(ctr), idx, cnt))
        if sn:
            kn = eng.known
            for k, v in sn.items():
                if kn.get(k, 0) < v:
                    kn[k] = v

    def dsem(self, name):
        return Counter(self, name, 16)

    def _wait(self, eng, tok):
        ctr, idx, cnt = tok
        key = (id(ctr), idx)
        if eng.known.get(key, 0) >= cnt:
            return
        if ctr is self.pe.ctr and self.pe_pending and (idx, cnt) == (ctr.idx, ctr.cnt + 1):
            raise RuntimeError("waiting on a pending (unsignalled) PE token")
        eng.h.wait_ge(ctr.sems[idx], cnt)
        self._learn(eng, tok)
        eng.nwait += 1

    def _deps(self, eng, reads, writes, defer_last=False):
        toks = []
        for b in reads:
            if b.w is not None:
                toks.append(b.w)
        for b in writes:
            if b.w is not None:
                toks.append(b.w)
            toks.extend(b.r.values())
        need = {}
        for t in toks:
            if eng is self.pe and t[0] is self.pe.ctr:
                continue
            ctr, idx, cnt = t
            key = (id(ctr), idx)
            if eng.known.get(key, 0) >= cnt:
                continue
            if key not in need or need[key][2] < cnt:
                need[key] = t
        lst = sorted(need.values(), key=lambda t: -len(self.snap.get((id(t[0]), t[1], t[2]), ())))
        last = None
        while lst:
            t = lst.pop(0)
            if eng.known.get((id(t[0]), t[1]), 0) >= t[2]:
                continue
            if defer_last and not any(eng.known.get((id(u[0]), u[1]), 0) < u[2] for u in lst):
                last = t
                break
            self._wait(eng, t)
        return last

    def _attach(self, eng, ins, tok):
        ctr, idx, cnt = tok
        if ctr is self.pe.ctr and self.pe_pending and (idx, cnt) == (ctr.idx, ctr.cnt + 1):
            raise RuntimeError("waiting on a pending (unsignalled) PE token")
        ins._wait_ge(ctr.sems[idx], eng.h.lower_val(cnt))
        self._learn(eng, tok)

    def op(self, eng, fn, reads=(), writes=(), sig=True):
        last = self._deps(eng, reads, writes, defer_last=ATTACH_WAITS)
        ins = fn(eng.h)
        if last is not None:
            self._attach(eng, ins, last)
        eng.ninst += 1
        if sig:
            tok = eng.ctr.next_token()
            ins.then_inc(tok[0].sems[tok[1]], 1)
            self.snap[(id(tok[0]), tok[1], tok[2])] = dict(eng.known)
            if eng is self.pe:
                self.pe_pending = False
        else:
            assert eng is self.pe
            tok = eng.ctr.peek_token()
            self.pe_pending = True
        for b in reads:
            b.r[eng.name] = tok
        for b in writes:
            b.w = tok
            b.r = {}
        return tok

    def dma(self, eng, out, in_, ds, reads=(), writes=(), **kw):
        self._deps(eng, reads, writes)
        tok = ds.next_token()
        eng.h.dma_start(out=out, in_=in_, **kw).then_inc(tok[0].sems[tok[1]], 16)
        self.snap[(id(tok[0]), tok[1], tok[2])] = dict(eng.known)
        eng.ninst += 1
        for b in reads:
            b.r[id(ds)] = tok
        for b in writes:
            b.w = tok
            b.r = {}
        return tok

    def mm(self, out, lhsT, rhs, start, stop, reads, writes, sig=None):
        if sig is None:
            sig = stop
        return self.op(self.pe, lambda h: h.matmul(out, lhsT, rhs, start=start, stop=stop),
                       reads, writes, sig=sig)

    def tr(self, out, in_, ident, reads, writes, sig=True):
        return self.op(self.pe, lambda h: h.transpose(out, in_, ident), reads, writes, sig=sig)

    def barrier(self):
        assert not self.pe_pending
        for e in self.engs:
            for c in self.counters:
                if e.ctr is c and e is self.pe:
                    continue
                if c.cnt > 0:
                    self._wait(e, (c, c.idx, c.cnt))

    def stats(self):
        return {e.name: (e.ninst, e.nwait) for e in self.engs}


class T:
    def __init__(self, h, name):
        self.h = h
        self.b = Buf(name)
        self.bs = [self.b]

    def __getitem__(self, k):
        return self.h[k]


def bc(ap, shape):
    return ap.broadcast_to(list(shape))


class _Stop(Exception):
    pass


def build_program(nsteps=SEQ // 128, dbg=False, upto=99):
    nc = bass.Bass("TRN2", target_bir_lowering=False)
    dt_in = lambda name, shape: nc.dram_tensor(name, list(shape), F32, kind="ExternalInput").ap()
    x_d = dt_in("x", [NB, SEQ, DM])
    prm_d = dt_in("prm", [128, NPRM])
    sinks_d = dt_in("attn_sinks", [1, 8])
    w_in_d = dt_in("w_in", [DM, IN_W])
    decay_up_d = dt_in("decay_up", [64, 512])
    aaa_up_d = dt_in("aaa_up", [64, 512])
    gate_up_d = dt_in("gate_up", [128, 512])
    w_ba_d = dt_in("w_branch_attn", [512, DM])
    w_br_d = dt_in("w_branch_rwkv", [512, DM])
    w_out_d = dt_in("w_out", [DM, DM])
    w1_d = dt_in("w_ff_in", [DM, DFF])
    w2_d = dt_in("w_ff_out", [DFF, DM])
    y_d = nc.dram_tensor("y", [NB, SEQ, DM], F32, kind="ExternalOutput").ap()
    h_d = nc.dram_tensor("hbuf", [NB, SEQ, DM], F32, kind="Internal").ap()
    dbg_d = {}

    with ExitStack() as st0:
        P = Prog(nc, st0)
        pe, act, dve, pool, sp = P.pe, P.act, P.dve, P.pool, P.sp

        def chk(stage):
            if stage > upto:
                raise _Stop()

        def mk(st, name, shape, dt=F32):
            return T(st.enter_context(nc.sbuf_tensor("s_" + name, list(shape), dt)), name)

        banks = [T(st0.enter_context(nc.psum_tensor(f"bank{i}", [128, 512], F32)), f"bank{i}") for i in range(8)]
        free_banks = list(banks)

        def bank():
            if not free_banks:
                raise RuntimeError("out of PSUM banks")
            return free_banks.pop(0)

        def rel(b):
            assert b not in free_banks
            free_banks.append(b)

        def bf(bk):
            return bk.h[:].bitcast(BF16)

        prm = mk(st0, "prm", [128, NPRM])
        der = mk(st0, "der", [128, 16])
        esink = mk(st0, "esink", [128, 8])
        ident = mk(st0, "ident", [128, 128], BF16)
        ones64 = mk(st0, "ones64", [128, 128], BF16)
        ones1 = mk(st0, "ones1", [128, 128], BF16)
        onesf = mk(st0, "onesf", [128, 128], F32)
        m_lt = mk(st0, "m_lt", [128, 128], BF16)
        m_le = mk(st0, "m_le", [128, 128], BF16)
        m_gt = mk(st0, "m_gt", [128, 128], BF16)
        mask4 = mk(st0, "mask4", [128, 4, 128], BF16)
        stage = [mk(st0, f"stage{i}", [128, 512]) for i in range(1)]
        d_prm = P.dsem("d_prm")
        d_stage = [P.dsem(f"d_stage{i}") for i in range(1)]
        d_x = [P.dsem(f"d_x{i}") for i in range(2)]
        d_st = [P.dsem(f"d_st{i}") for i in range(2)]
        sstate = {"i": 0}

        def pcol(name, j=0, n=1):
            o, w = PC[name]
            return prm[:, o + j:o + j + n]

        P.dma(sp, prm[:], prm_d[:, :], d_prm, writes=[prm.b])
        d_snk = P.dsem("d_snk")
        P.dma(sp, esink[:], bc(sinks_d[0:1, :], [128, 8]), d_snk, writes=[esink.b])
        V = P.op
        V(pool, lambda h: h.memset(onesf[:], 1.0), writes=[onesf.b])
        for t_, cmp_ in ((ident, ALU.is_equal), (m_le, ALU.is_ge), (m_lt, ALU.is_gt)):
            V(pool, lambda h, t_=t_: h.memset(t_[:], 1.0), writes=[t_.b])
            V(pool, lambda h, t_=t_, cmp_=cmp_: h.affine_select(out=t_[:], in_=t_[:], pattern=[[1, 128]],
                                                                compare_op=cmp_, fill=0.0, base=0,
                                                                channel_multiplier=-1),
              reads=[t_.b], writes=[t_.b])
        V(pool, lambda h: h.memset(m_gt[:], 1.0), writes=[m_gt.b])
        V(pool, lambda h: h.affine_select(out=m_gt[:], in_=m_gt[:], pattern=[[-1, 128]], compare_op=ALU.is_gt,
                                          fill=0.0, base=0, channel_multiplier=1),
          reads=[m_gt.b], writes=[m_gt.b])
        for t_, val in ((ones64, 1.0 / 64), (ones1, 1.0)):
            V(pool, lambda h, t_=t_: h.memset(t_[:], 0.0), writes=[t_.b])
            V(pool, lambda h, t_=t_, val=val: h.memset(t_[0:64, 0:64], val), reads=[t_.b], writes=[t_.b])
            V(pool, lambda h, t_=t_, val=val: h.memset(t_[64:128, 64:128], val), reads=[t_.b], writes=[t_.b])
        for j in range(4):
            src = m_lt if j % 2 == 0 else m_le
            V(pool, lambda h, j=j, src=src: h.tensor_copy(out=mask4[:, j, :], in_=src[:]),
              reads=[src.b], writes=[mask4.b])
        V(dve, lambda h: h.tensor_scalar(out=der[:, 0:4], in0=pcol("k_a", 0, 4), scalar1=-1.0, scalar2=1.0,
                                         op0=ALU.mult, op1=ALU.add), reads=[prm.b], writes=[der.b])
        V(dve, lambda h: h.scalar_tensor_tensor(out=der[:, 4:5], in0=pcol("qg"), scalar=0.125, in1=pcol("kg"),
                                                op0=ALU.mult, op1=ALU.mult), reads=[prm.b, der.b], writes=[der.b])
        V(act, lambda h: h.activation(out=esink[:], in_=esink[:], func=AF.Exp), reads=[esink.b], writes=[esink.b])

        cast_rr = {"i": 0}

        wsem = {}
        stg_state = {"i": 0, "e": 0}

        def load_cast_sp(dst_buf, dst_ap, src_ap, ncols, stgs, dsems):
            c0 = 0
            while c0 < ncols:
                i = stg_state["i"] % len(stgs)
                stg_state["i"] += 1
                sg = stgs[i]
                w = min(int(sg.h.shape[-1]), ncols - c0)
                P.dma(sp, sg[:, 0:w], src_ap[:, c0:c0 + w], dsems[i], writes=[sg.b])
                if stg_state["e"] % 2 == 0:
                    V(dve, lambda h, sg=sg, c0=c0, w=w: h.tensor_copy(out=dst_ap[:, c0:c0 + w], in_=sg[:, 0:w]),
                      reads=[sg.b], writes=[dst_buf])
                else:
                    V(act, lambda h, sg=sg, c0=c0, w=w: h.copy(out=dst_ap[:, c0:c0 + w], in_=sg[:, 0:w]),
                      reads=[sg.b], writes=[dst_buf])
                stg_state["e"] += 1
                c0 += w

        def load_cast(dst_t, dst_ap, src_ap, ncols):
            if dst_t.b.name not in wsem:
                wsem[dst_t.b.name] = P.dsem("dw_" + dst_t.b.name)
            P.dma(pool, dst_ap, src_ap, wsem[dst_t.b.name], writes=[dst_t.b])

        with ExitStack() as st1, suppress(_Stop):
            w_in_sb = mk(st1, "w_in_sb", [128, 8, IN_W], BF16)
            wkd = mk(st1, "wkd", [128, 8, 256], BF16)
            w_ba = mk(st1, "w_ba", [128, 4, DM], BF16)
            w_br = mk(st1, "w_br", [128, 4, DM], BF16)
            w_out_sb = mk(st1, "w_out_sb", [128, 8, DM], BF16)
            lora_wa = mk(st1, "lora_wa", [128, 512], BF16)
            lora_g = mk(st1, "lora_g", [128, 512], BF16)

            W_IN_SPLIT = True
            for kc in range(8):
                if not (W_IN_SPLIT and kc % 2 == 1):
                    load_cast(w_in_sb, w_in_sb[:, kc, :], w_in_d[kc * 128:(kc + 1) * 128, :], IN_W)
            for kc in range(4):
                load_cast(w_ba, w_ba[:, kc, :], w_ba_d[kc * 128:(kc + 1) * 128, :], DM)
                load_cast(w_br, w_br[:, kc, :], w_br_d[kc * 128:(kc + 1) * 128, :], DM)
            for kc in range(8):
                load_cast(w_out_sb, w_out_sb[:, kc, :], w_out_d[kc * 128:(kc + 1) * 128, :], DM)
            i = 0
            P.dma(sp, stage[i][0:64, 0:512], decay_up_d[:, :], d_stage[i], writes=[stage[i].b])
            P.dma(sp, stage[i][64:128, 0:512], aaa_up_d[:, :], d_stage[i], writes=[stage[i].b])
            V(dve, lambda h, i=i: h.tensor_copy(out=lora_wa[:], in_=stage[i][:, 0:512]),
              reads=[stage[i].b], writes=[lora_wa.b])
            load_cast(lora_g, lora_g[:], gate_up_d[:, :], 512)

            xt = [mk(st1, f"xt{i}", [128, DM]) for i in range(2)]
            xn = mk(st1, "xn", [128, DM], BF16)
            sm = mk(st1, "sm", [128, 32])
            uT = [mk(st1, f"uT{i}", [128, 8, 128], BF16) for i in range(2)]
            qsq = mk(st1, "qsq", [128, 6, 128], BF16)
            rq = mk(st1, "rq", [128, 6, 128])
            qn = mk(st1, "qn", [128, 4, 128], BF16)
            kn = [mk(st1, f"kn{i}", [128, 2, 2, 128], BF16) for i in range(2)]
            vext = [mk(st1, f"vext{i}", [128, 2, 65], BF16) for i in range(2)]
            pt = [mk(st1, f"pt{i}", [128, 4, 128], BF16) for i in range(2)]
            ao = mk(st1, "ao", [128, 8, 64], BF16)
            aoT = [mk(st1, f"aoT{i}", [128, 4, 128], BF16) for i in range(2)]
            rkvp = mk(st1, "rkvp", [128, 12, 129])
            lorap = mk(st1, "lorap", [128, 2, 129])
            FS = [mk(st1, f"fs{i}", [128, 4, 128]) for i in range(10)]
            r_, k_, v_, lw, a_, g_, kk, Lc, t1, t2 = FS
            kmod = k_
            e1 = t2
            sgt = mk(st1, "sgt", [128, 4, 128])
            zs = mk(st1, "zs", [128, 2, 128])
            zt = mk(st1, "zt", [128, 2, 128])
            twz = mk(st1, "twz", [128, 2, 128], BF16)
            sgb = mk(st1, "sgb", [128, 128], BF16)
            arz = mk(st1, "arz", [128, 4, 2, 2, 128], BF16)
            bk = mk(st1, "bk", [128, 4, 2, 128], BF16)
            hat = mk(st1, "hat", [128, 4, 2, 128], BF16)
            kbh = mk(st1, "kbh", [128, 4, 2, 128], BF16)
            vtok = mk(st1, "vtok", [128, 4, 128], BF16)
            hb16 = mk(st1, "hb16", [128, 4, 128], BF16)
            ssb = mk(st1, "ssb", [128, 8, 4, 128], BF16)
            Xs = mk(st1, "Xs", [128, 8, 128], BF16)
            Ys = mk(st1, "Ys", [128, 8, 128], BF16)
            TTs = mk(st1, "TTs", [128, 8, 128], BF16)
            gbf = mk(st1, "gbf", [128, 8, 64], BF16)
            ubf = mk(st1, "ubf", [128, 8, 64], BF16)
            Hs = mk(st1, "Hs", [128, 4, 64])
            Hb = mk(st1, "Hb", [128, 4, 64], BF16)
            wcs = mk(st1, "wcs", [128, 4, 1])
            ywT = mk(st1, "ywT", [128, 4, 128], BF16)
            mixT = mk(st1, "mixT", [128, 8, 128], BF16)
            print("sbuf remaining after phase1 alloc:", nc.sbuf_bytes_remaining)

            if W_IN_SPLIT:
                w_in_sb.b2 = Buf("w_in_odd")
                w_in_sb.bs = [w_in_sb.b, w_in_sb.b2]
                for kc in range(1, 8, 2):
                    load_cast_sp(w_in_sb.b2, w_in_sb[:, kc, :], w_in_d[kc * 128:(kc + 1) * 128, :], IN_W, xt, d_x)
            for kc in range(8):
                for g in range(2):
                    for dup in range(2):
                        V(pool, lambda h, kc=kc, g=g, dup=dup: h.tensor_copy(
                            out=wkd[:, kc, g * 128 + dup * 64:g * 128 + dup * 64 + 64],
                            in_=w_in_sb[:, kc, 512 + g * 64:512 + g * 64 + 64]),
                          reads=[*w_in_sb.bs], writes=[wkd.b])
            for i in range(2):
                V(pool, lambda h, i=i: h.memset(vext[i][:], 1.0), writes=[vext[i].b])
                V(pool, lambda h, i=i: h.memset(kn[i][:], 0.0), writes=[kn[i].b])
            V(pool, lambda h: h.memset(twz[:], 0.0), writes=[twz.b])
            V(pool, lambda h: h.memset(arz[:], 0.0), writes=[arz.b])

            def load_x(b, n, slot):
                P.dma(sp, xt[slot][:], x_d[b, n * 128:(n + 1) * 128, :], d_x[slot], writes=[xt[slot].b])

            steps = [(b, n) for b in range(NB) for n in range(nsteps)]

            def rsqrt_inplace(t, ap_fn, src_ap_fn, src_reads, scale, bias):
                V(act, lambda h: h.activation(out=ap_fn(), in_=src_ap_fn(), func=AF.Ln, bias=bias, scale=scale),
                  reads=src_reads, writes=[t.b])
                V(act, lambda h: h.activation(out=ap_fn(), in_=ap_fn(), func=AF.Exp, scale=-0.5),
                  reads=[t.b], writes=[t.b])

            def sigmoid_inplace(t, ap_fn, src_ap_fn, src_reads, scale=-1.0):
                V(act, lambda h: h.activation(out=ap_fn(), in_=src_ap_fn(), func=AF.Exp, scale=scale),
                  reads=src_reads, writes=[t.b])
                V(act, lambda h: h.activation(out=ap_fn(), in_=ap_fn(), func=AF.Ln, bias=1.0),
                  reads=[t.b], writes=[t.b])
                V(act, lambda h: h.activation(out=ap_fn(), in_=ap_fn(), func=AF.Exp, scale=-1.0),
                  reads=[t.b], writes=[t.b])

            def v4(bk_):
                return bk_.h[:].rearrange("p (c t) -> p c t", c=4)

            def mucol(name):
                o, w = PC[name]
                return bc(prm[:, o:o + w].unsqueeze(2), [128, w, 128])

            def gen_front(si):
                b, n = steps[si]
                slot = si % 2
                par = n % 2
                first = (n == 0)
                xcur = xt[slot]
                uTc = uT[slot]
                load_x(b, n, slot)
                yield
                V(dve, lambda h: h.memset(sm[:, 0:1], 0.0), writes=[sm.b])
                V(act, lambda h: h.activation(out=xn[:], in_=xcur[:], func=AF.Square, accum_out=sm[:, 0:1]),
                  reads=[xcur.b, sm.b], writes=[xn.b, sm.b])
                rsqrt_inplace(sm, lambda: sm[:, 0:1], lambda: sm[:, 0:1], [sm.b], 1.0 / DM, RMS_EPS)
                V(dve, lambda h: h.tensor_scalar(out=xn[:], in0=xcur[:], scalar1=sm[:, 0:1], scalar2=None,
                                                 op0=ALU.mult), reads=[xcur.b, sm.b], writes=[xn.b])
                yield
                yield
                bk_ = bank()
                for c in range(8):
                    P.tr(bf(bk_)[:, c * 128:(c + 1) * 128], xn[:, c * 128:(c + 1) * 128], ident[:],
                         reads=[xn.b, ident.b], writes=[bk_.b], sig=(c == 7))
                o, _ = PC["g1"]
                V(dve, lambda h: h.tensor_tensor(out=uTc[:], in0=bf(bk_).rearrange("p (c t) -> p c t", c=8),
                                                 in1=bc(prm[:, o:o + 8].unsqueeze(2), [128, 8, 128]), op=ALU.mult),
                  reads=[bk_.b, prm.b], writes=[uTc.b])
                rel(bk_)
                yield

                def proj(out_ap, bkx, wt, col0, sig):
                    for kc in range(8):
                        P.mm(out_ap, wt[:, kc, col0:col0 + 128], uTc[:, kc, :], kc == 0, kc == 7,
                             reads=[*wt.bs, uTc.b], writes=[bkx.b], sig=(sig and kc == 7))

                for qi in range(3):
                    br_ = bank()
                    for c in range(4):
                        proj(v4(br_)[:, c, :], br_, w_in_sb, 768 + (qi * 4 + c) * 128, c == 3)
                        if c == 1:
                            yield
                    V(act, lambda h, qi=qi, br_=br_: h.copy(out=rkvp[:, qi * 4:(qi + 1) * 4, 1:129], in_=v4(br_)),
                      reads=[br_.b], writes=[rkvp.b])
                    rel(br_)
                    yield
                bkk = bank()
                for g in range(2):
                    proj(v4(bkk)[:, g, :], bkk, wkd, g * 128, False)
                proj(v4(bkk)[:, 2, :], bkk, w_in_sb, 2304, False)
                proj(v4(bkk)[:, 3, :], bkk, w_in_sb, 2432, True)
                V(act, lambda h: h.copy(out=lorap[:, :, 1:129], in_=v4(bkk)[:, 2:4, :]), reads=[bkk.b],
                  writes=[lorap.b])
                V(act, lambda h: h.activation(out=qsq[:, 4:6, :], in_=v4(bkk)[:, 0:2, :], func=AF.Square),
                  reads=[bkk.b], writes=[qsq.b])
                yield
                bq = bank()
                for c in range(4):
                    proj(v4(bq)[:, c, :], bq, w_in_sb, c * 128, c == 3)
                    if c == 1:
                        yield
                V(act, lambda h: h.activation(out=qsq[:, 0:4, :], in_=v4(bq), func=AF.Square),
                  reads=[bq.b], writes=[qsq.b])
                yield
                bv = bank()
                for kc in range(8):
                    P.mm(bv[:, 0:128], uTc[:, kc, :], w_in_sb[:, kc, 640:768], kc == 0, kc == 7,
                         reads=[uTc.b, *w_in_sb.bs], writes=[bv.b])
                V(dve, lambda h: h.tensor_copy(out=vext[par][:, :, 0:64],
                                               in_=bv[:, 0:128].rearrange("p (g d) -> p g d", g=2)),
                  reads=[bv.b], writes=[vext[par].b])
                rel(bv)
                yield
                bs1 = bank()
                for c in range(4):
                    P.mm(v4(bs1)[:, c, :], ones64[:], qsq[:, c, :], True, True, reads=[ones64.b, qsq.b],
                         writes=[bs1.b], sig=(c == 3))
                bs2 = bank()
                for c in range(2):
                    P.mm(v4(bs2)[:, c, :], ones64[:], qsq[:, 4 + c, :], True, True, reads=[ones64.b, qsq.b],
                         writes=[bs2.b], sig=(c == 1))
                rsqrt_inplace(rq, lambda: rq[:, 0:4, :], lambda: v4(bs1), [bs1.b], 1.0, RMS_EPS)
                rsqrt_inplace(rq, lambda: rq[:, 4:6, :], lambda: v4(bs2)[:, 0:2, :], [bs2.b], 1.0, RMS_EPS)
                rel(bs1)
                rel(bs2)
                yield
                V(dve, lambda h: h.tensor_tensor(out=qn[:], in0=v4(bq), in1=rq[:, 0:4, :], op=ALU.mult),
                  reads=[bq.b, rq.b], writes=[qn.b])
                for hf in range(2):
                    ps_ = slice(hf * 64, (hf + 1) * 64)
                    V(dve, lambda h, hf=hf, ps_=ps_: h.scalar_tensor_tensor(
                        out=kn[par][ps_, :, hf, :], in0=v4(bkk)[ps_, 0:2, :], scalar=der[ps_, 4:5],
                        in1=rq[ps_, 4:6, :], op0=ALU.mult, op1=ALU.mult),
                      reads=[bkk.b, rq.b, der.b], writes=[kn[par].b])
                rel(bq)
                rel(bkk)
                yield
                for g in range(2):
                    blocks = ([] if first else [(1 - par, m_gt, pt[0])]) + [(par, m_le, pt[1])]
                    for (kp, msk, ptt) in blocks:
                        bs = bank()
                        for j in range(4):
                            hd = 4 * g + j
                            c, hf = hd // 2, hd % 2
                            P.mm(v4(bs)[:, j, :], kn[kp][:, g, hf, :], qn[:, c, :], True, True,
                                 reads=[kn[kp].b, qn.b], writes=[bs.b], sig=(j == 3))
                        V(act, lambda h, bs=bs, ptt=ptt: h.activation(out=ptt[:], in_=v4(bs), func=AF.Exp),
                          reads=[bs.b], writes=[ptt.b])
                        rel(bs)
                        V(pool, lambda h, ptt=ptt, msk=msk: h.tensor_tensor(
                            out=ptt[:], in0=ptt[:], in1=bc(msk[:].unsqueeze(1), [128, 4, 128]), op=ALU.mult),
                          reads=[ptt.b, msk.b], writes=[ptt.b])
                        yield
                    bo = bank()
                    bov = bo.h[:, 0:260].rearrange("p (j d) -> p j d", j=4)
                    for j in range(4):
                        for bi, (kp, msk, ptt) in enumerate(blocks):
                            P.mm(bov[:, j, :], ptt[:, j, :], vext[kp][:, g, :], bi == 0, bi == len(blocks) - 1,
                                 reads=[ptt.b, vext[kp].b], writes=[bo.b],
                                 sig=(j == 3 and bi == len(blocks) - 1))
                    V(dve, lambda h, g=g, bov=bov: h.tensor_tensor(
                        out=sm[:, 8 + 4 * g:12 + 4 * g].unsqueeze(2), in0=bov[:, :, 64:65],
                        in1=esink[:, 4 * g:4 * g + 4].unsqueeze(2), op=ALU.add),
                      reads=[bo.b, esink.b, sm.b], writes=[sm.b])
                    V(act, lambda h, g=g: h.activation(out=sm[:, 8 + 4 * g:12 + 4 * g],
                                                       in_=sm[:, 8 + 4 * g:12 + 4 * g], func=AF.Ln),
                      reads=[sm.b], writes=[sm.b])
                    V(act, lambda h, g=g: h.activation(out=sm[:, 8 + 4 * g:12 + 4 * g],
                                                       in_=sm[:, 8 + 4 * g:12 + 4 * g], func=AF.Exp, scale=-1.0),
                      reads=[sm.b], writes=[sm.b])
                    V(dve, lambda h, g=g, bov=bov: h.tensor_tensor(
                        out=ao[:, 4 * g:4 * g + 4, :], in0=bov[:, :, 0:64],
                        in1=bc(sm[:, 8 + 4 * g:12 + 4 * g].unsqueeze(2), [128, 4, 64]), op=ALU.mult),
                      reads=[bo.b, sm.b], writes=[ao.b])
                    rel(bo)
                    yield
                bt = bank()
                aov = ao[:].rearrange("p h d -> p (h d)")
                for c in range(4):
                    P.tr(bf(bt)[:, c * 128:(c + 1) * 128], aov[:, c * 128:(c + 1) * 128], ident[:],
                         reads=[ao.b, ident.b], writes=[bt.b], sig=(c == 3))
                V(act, lambda h: h.copy(out=aoT[slot][:], in_=bf(bt)[:, 0:512].rearrange("p (c t) -> p c t", c=4)),
                  reads=[bt.b], writes=[aoT[slot].b])
                rel(bt)
                yield

            def gen_rw(si):
                b, n = steps[si]
                first = (n == 0)
                if first:
                    V(pool, lambda h: h.memset(rkvp[:, :, 0:1], 0.0), reads=[rkvp.b], writes=[rkvp.b])
                    V(pool, lambda h: h.memset(lorap[:, :, 0:1], 0.0), reads=[lorap.b], writes=[lorap.b])
                    V(pool, lambda h: h.memset(Hs[:], 0.0), writes=[Hs.b])
                    V(pool, lambda h: h.memset(Hb[:], 0.0), writes=[Hb.b])
                V(dve, lambda h: h.tensor_tensor(out=zs[:], in0=lorap[:, :, 0:128], in1=lorap[:, :, 1:129],
                                                 op=ALU.subtract), reads=[lorap.b], writes=[zs.b])
                V(pool, lambda h: h.tensor_tensor(out=zs[:], in0=zs[:], in1=mucol("mu2"), op=ALU.mult),
                  reads=[zs.b, prm.b], writes=[zs.b])
                V(dve, lambda h: h.tensor_tensor(out=zs[:], in0=zs[:], in1=lorap[:, :, 1:129], op=ALU.add),
                  reads=[zs.b, lorap.b], writes=[zs.b])
                V(pool, lambda h: h.tensor_copy(out=lorap[:, :, 0:1], in_=lorap[:, :, 128:129]),
                  reads=[lorap.b], writes=[lorap.b])
                yield
                sigmoid_inplace(zt, lambda: zt[0:64, 0, :], lambda: zs[0:64, 0, :], [zs.b], scale=2.0)
                sigmoid_inplace(zt, lambda: zt[:, 1, :], lambda: zs[:, 1, :], [zs.b], scale=-1.0)
                V(dve, lambda h: h.tensor_scalar(out=twz[0:64, 0, :], in0=zt[0:64, 0, :], scalar1=-2.0, scalar2=1.0,
                                                 op0=ALU.mult, op1=ALU.add), reads=[zt.b], writes=[twz.b])
                V(dve, lambda h: h.tensor_copy(out=twz[64:128, 1, :], in_=zs[64:128, 0, :]), reads=[zs.b],
                  writes=[twz.b])
                V(dve, lambda h: h.tensor_copy(out=sgb[:], in_=zt[:, 1, :]), reads=[zt.b], writes=[sgb.b])
                yield
                for qi, (dst, mun) in enumerate(((r_, "mu_r"), (k_, "mu_k"), (v_, "mu_v"))):
                    V(dve, lambda h, qi=qi, dst=dst: h.tensor_tensor(out=dst[:], in0=rkvp[:, qi * 4:qi * 4 + 4, 0:128],
                                                                     in1=rkvp[:, qi * 4:qi * 4 + 4, 1:129],
                                                                     op=ALU.subtract),
                      reads=[rkvp.b], writes=[dst.b])
                    V(pool, lambda h, mun=mun, dst=dst: h.tensor_tensor(out=dst[:], in0=dst[:], in1=mucol(mun),
                                                                        op=ALU.mult),
                      reads=[dst.b, prm.b], writes=[dst.b])
                    V(pool, lambda h, qi=qi, dst=dst: h.tensor_tensor(out=dst[:], in0=dst[:],
                                                                      in1=rkvp[:, qi * 4:qi * 4 + 4, 1:129],
                                                                      op=ALU.add),
                      reads=[dst.b, rkvp.b], writes=[dst.b])
                    yield
                V(pool, lambda h: h.tensor_copy(out=rkvp[:, :, 0:1], in_=rkvp[:, :, 128:129]),
                  reads=[rkvp.b], writes=[rkvp.b])
                V(pool, lambda h: h.tensor_tensor(out=kk[:], in0=k_[:], in1=mucol("k_k"), op=ALU.mult),
                  reads=[k_.b, prm.b], writes=[kk.b])
                V(pool, lambda h: h.tensor_tensor(out=hb16[:], in0=kk[:], in1=kk[:], op=ALU.mult),
                  reads=[kk.b], writes=[hb16.b])
                bz1, bz2 = bank(), bank()
                for c in range(4):
                    P.mm(v4(bz1)[:, c, :], lora_wa[:, c * 128:(c + 1) * 128], twz[:, 0, :], True, True,
                         reads=[lora_wa.b, twz.b], writes=[bz1.b], sig=(c == 3))
                for c in range(4):
                    P.mm(v4(bz2)[:, c, :], lora_wa[:, c * 128:(c + 1) * 128], twz[:, 1, :], True, True,
                         reads=[lora_wa.b, twz.b], writes=[bz2.b], sig=(c == 3))
                V(dve, lambda h: h.tensor_tensor(out=lw[:], in0=v4(bz1), in1=mucol("db"), op=ALU.add),
                  reads=[bz1.b, prm.b], writes=[lw.b])
                V(dve, lambda h: h.tensor_tensor(out=a_[:], in0=v4(bz2), in1=mucol("ab"), op=ALU.add),
                  reads=[bz2.b, prm.b], writes=[a_.b])
                rel(bz1)
                rel(bz2)
                yield
                bs3 = bank()
                for c in range(4):
                    P.mm(v4(bs3)[:, c, :], ones1[:], hb16[:, c, :], True, True, reads=[ones1.b, hb16.b],
                         writes=[bs3.b], sig=(c == 3))
                sigmoid_inplace(lw, lambda: lw[:], lambda: lw[:], [lw.b])
                V(pool, lambda h: h.tensor_scalar(out=lw[:], in0=lw[:], scalar1=DECAY_C, scalar2=None, op0=ALU.mult),
                  reads=[lw.b], writes=[lw.b])
                yield
                sigmoid_inplace(a_, lambda: a_[:], lambda: a_[:], [a_.b])
                rsqrt_inplace(t1, lambda: t1[:], lambda: v4(bs3), [bs3.b], 1.0, 1e-30)
                rel(bs3)
                yield
                for c in range(4):
                    V(dve, lambda h, c=c: h.tensor_tensor_scan(out=Lc[:, c, :], data0=onesf[:], data1=lw[:, c, :],
                                                               initial=0.0, op0=ALU.mult, op1=ALU.add),
                      reads=[onesf.b, lw.b], writes=[Lc.b])
                V(dve, lambda h: h.tensor_tensor(out=kk[:], in0=kk[:], in1=t1[:], op=ALU.mult),
                  reads=[kk.b, t1.b], writes=[kk.b])
                yield
                V(pool, lambda h: h.tensor_tensor(out=t2[:], in0=a_[:], in1=mucol("k_a"), op=ALU.mult),
                  reads=[a_.b, prm.b], writes=[t2.b])
                V(pool, lambda h: h.tensor_tensor(out=t2[:], in0=t2[:], in1=bc(der[:, 0:4].unsqueeze(2), [128, 4, 128]),
                                                  op=ALU.add), reads=[t2.b, der.b], writes=[t2.b])
                V(dve, lambda h: h.tensor_tensor(out=kmod[:], in0=k_[:], in1=t2[:], op=ALU.mult),
                  reads=[k_.b, t2.b], writes=[kmod.b])
                yield
                V(act, lambda h: h.activation(out=e1[:], in_=Lc[:], func=AF.Exp), reads=[Lc.b], writes=[e1.b])
                V(pool, lambda h: h.tensor_tensor(out=t1[:], in0=Lc[:], in1=lw[:], op=ALU.subtract),
                  reads=[Lc.b, lw.b], writes=[t1.b])
                for hf in range(2):
                    ps_ = slice(hf * 64, (hf + 1) * 64)
                    V(dve, lambda h, hf=hf, ps_=ps_: h.tensor_tensor(out=arz[ps_, :, hf, 1, :], in0=r_[ps_, :, :],
                                                                     in1=e1[ps_, :, :], op=ALU.mult),
                      reads=[r_.b, e1.b], writes=[arz.b])
                V(pool, lambda h: h.tensor_copy(out=wcs[:], in_=e1[:, :, 127:128]), reads=[e1.b], writes=[wcs.b])
                V(act, lambda h: h.activation(out=t1[:], in_=t1[:], func=AF.Exp), reads=[t1.b], writes=[t1.b])
                yield
                for hf in range(2):
                    ps_ = slice(hf * 64, (hf + 1) * 64)
                    V(dve, lambda h, hf=hf, ps_=ps_: h.scalar_tensor_tensor(
                        out=arz[ps_, :, hf, 0, :], in0=kk[ps_, :, :], scalar=-1.0, in1=t1[ps_, :, :],
                        op0=ALU.mult, op1=ALU.mult), reads=[kk.b, t1.b], writes=[arz.b])
                V(act, lambda h: h.activation(out=t2[:], in_=Lc[:], func=AF.Exp, scale=-1.0),
                  reads=[Lc.b], writes=[t2.b])
                V(pool, lambda h: h.tensor_tensor(out=t1[:], in0=kk[:], in1=a_[:], op=ALU.mult),
                  reads=[kk.b, a_.b], writes=[t1.b])
                yield
                V(dve, lambda h: h.tensor_tensor(out=bk[:, :, 1, :], in0=kmod[:], in1=t2[:], op=ALU.mult),
                  reads=[kmod.b, t2.b], writes=[bk.b])
                V(dve, lambda h: h.tensor_tensor(out=bk[:, :, 0, :], in0=t1[:], in1=t2[:], op=ALU.mult),
                  reads=[t1.b, t2.b], writes=[bk.b])
                V(pool, lambda h: h.tensor_tensor(out=hat[:], in0=bk[:],
                                                  in1=bc(wcs[:].unsqueeze(2), [128, 4, 2, 128]), op=ALU.mult),
                  reads=[bk.b, wcs.b], writes=[hat.b])
                yield
                arv = arz[:].rearrange("p c h w t -> p c h (w t)")
                for hd in range(8):
                    c, hf = hd // 2, hd % 2
                    bsx = bank()
                    P.mm(bsx[:, 0:256], bk[:, c, 0, :], arv[:, c, hf, :], True, True,
                         reads=[bk.b, arz.b], writes=[bsx.b], sig=False)
                    P.mm(bsx[:, 256:512], bk[:, c, 1, :], arv[:, c, hf, :], True, True,
                         reads=[bk.b, arz.b], writes=[bsx.b], sig=True)
                    V(dve, lambda h, hd=hd, bsx=bsx: h.tensor_tensor(out=ssb[:, hd, :, :], in0=v4(bsx), in1=mask4[:],
                                                                     op=ALU.mult),
                      reads=[bsx.b, mask4.b], writes=[ssb.b])
                    rel(bsx)
                    if hd % 2 == 1:
                        yield
                for q in range(2):
                    by = bank()
                    for j in range(4):
                        hd = q * 4 + j
                        c, hf = hd // 2, hd % 2
                        P.mm(v4(by)[:, j, :], arz[:, c, hf, 0, :], bk[:, c, 0, :], True, True,
                             reads=[arz.b, bk.b], writes=[by.b], sig=(j == 3))
                    V(dve, lambda h, q=q, by=by: h.tensor_tensor(out=Ys[:, 4 * q:4 * q + 4, :], in0=v4(by),
                                                                 in1=bc(m_gt[:].unsqueeze(1), [128, 4, 128]),
                                                                 op=ALU.mult),
                      reads=[by.b, m_gt.b], writes=[Ys.b])
                    rel(by)
                yield
                bh = bank()
                for c in range(4):
                    for w in range(2):
                        o = (c * 2 + w) * 128
                        P.tr(bf(bh)[:, o:o + 128], hat[:, c, w, :], ident[:], reads=[hat.b, ident.b],
                             writes=[bh.b], sig=(c == 3 and w == 1))
                V(act, lambda h: h.copy(out=kbh[:].rearrange("p c w t -> p (c w t)"), in_=bf(bh)),
                  reads=[bh.b], writes=[kbh.b])
                rel(bh)
                V(pool, lambda h: h.tensor_copy(out=hb16[:], in_=v_[:]), reads=[v_.b], writes=[hb16.b])
                V(pool, lambda h: h.tensor_tensor(out=TTs[:], in0=ssb[:, :, 0, :],
                                                  in1=bc(ident[:].unsqueeze(1), [128, 8, 128]), op=ALU.add),
                  reads=[ssb.b, ident.b], writes=[TTs.b])
                yield
                bvt = bank()
                for c in range(4):
                    P.tr(bf(bvt)[:, c * 128:(c + 1) * 128], hb16[:, c, :], ident[:], reads=[hb16.b, ident.b],
                         writes=[bvt.b], sig=(c == 3))
                V(act, lambda h: h.copy(out=vtok[:].rearrange("p c t -> p (c t)"), in_=bf(bvt)[:, 0:512]),
                  reads=[bvt.b], writes=[vtok.b])
                rel(bvt)
                for kr in range(6):
                    Xin = (lambda hd: ssb[:, hd, 0, :]) if kr == 0 else (lambda hd: Xs[:, hd, :])
                    xb_ = ssb.b if kr == 0 else Xs.b
                    bxs = [bank(), bank()] if kr < 5 else None
                    bys = [bank(), bank()]
                    for q in range(2):
                        for j in range(4):
                            hd = 4 * q + j
                            if kr < 5:
                                P.mm(v4(bxs[q])[:, j, :], Ys[:, hd, :], Xin(hd), True, True,
                                     reads=[Ys.b, xb_], writes=[bxs[q].b], sig=(j == 3))
                        for j in range(4):
                            hd = 4 * q + j
                            P.mm(v4(bys[q])[:, j, :], Xin(hd), Ys[:, hd, :], True, True,
                                 reads=[Ys.b, xb_], writes=[bys[q].b], sig=(j == 3))
                    yield
                    for q in range(2):
                        if kr < 5:
                            V(act, lambda h, q=q, bxs=bxs: h.copy(out=Xs[:, 4 * q:4 * q + 4, :], in_=v4(bxs[q])),
                              reads=[bxs[q].b], writes=[Xs.b])
                            rel(bxs[q])
                        V(dve, lambda h, q=q, bys=bys: h.tensor_copy(out=Ys[:, 4 * q:4 * q + 4, :], in_=v4(bys[q])),
                          reads=[bys[q].b], writes=[Ys.b])
                        rel(bys[q])
                    yield
                    bts = [bank(), bank()]
                    for hd in range(8):
                        P.mm(v4(bts[hd // 4])[:, hd % 4, :], Ys[:, hd, :], TTs[:, hd, :], True, True,
                             reads=[Ys.b, TTs.b], writes=[bts[hd // 4].b], sig=(hd % 4 == 3))
                    for q in range(2):
                        V(dve, lambda h, q=q, bts=bts: h.tensor_tensor(out=TTs[:, 4 * q:4 * q + 4, :], in0=v4(bts[q]),
                                                                       in1=TTs[:, 4 * q:4 * q + 4, :], op=ALU.add),
                          reads=[bts[q].b, TTs.b], writes=[TTs.b])
                        rel(bts[q])
                    yield
                V(pool, lambda h: h.tensor_tensor(out=t1[:], in0=r_[:], in1=mucol("r_k"), op=ALU.mult),
                  reads=[r_.b, prm.b], writes=[t1.b])
                V(pool, lambda h: h.tensor_tensor(out=hb16[:], in0=t1[:], in1=kmod[:], op=ALU.mult),
                  reads=[t1.b, kmod.b], writes=[hb16.b])
                bg = bank()
                bgv = bg.h[:].rearrange("p (h d) -> p h d", h=8)
                for hd in range(8):
                    c, hf = hd // 2, hd % 2
                    p0 = hf * 64
                    if not first:
                        P.mm(bgv[:, hd, :], arz[:, c, hf, 0, :], Hb[:, c, :], True, False,
                             reads=[arz.b, Hb.b], writes=[bg.b])
                    P.mm(bgv[:, hd, :], ssb[:, hd, 2, :], vtok[:, c, p0:p0 + 64], first, True,
                         reads=[ssb.b, vtok.b], writes=[bg.b], sig=(hd == 7))
                V(act, lambda h: h.copy(out=gbf[:], in_=bgv), reads=[bg.b], writes=[gbf.b])
                rel(bg)
                yield
                bbs = bank()
                for c in range(4):
                    P.mm(v4(bbs)[:, c, :], ones1[:], hb16[:, c, :], True, True, reads=[ones1.b, hb16.b],
                         writes=[bbs.b], sig=(c == 3))
                bonus = kmod
                V(dve, lambda h: h.tensor_tensor(out=bonus[:], in0=v4(bbs), in1=v_[:], op=ALU.mult),
                  reads=[bbs.b, v_.b, kmod.b], writes=[bonus.b])
                rel(bbs)
                V(pool, lambda h: h.tensor_tensor(out=r_[:], in0=g_[:], in1=mucol("lng"), op=ALU.mult),
                  reads=[g_.b, prm.b], writes=[r_.b])
                V(pool, lambda h: h.tensor_tensor(out=v_[:], in0=bonus[:], in1=mucol("lnb"), op=ALU.add),
                  reads=[bonus.b, prm.b], writes=[v_.b])
                V(pool, lambda h: h.tensor_tensor(out=v_[:], in0=v_[:], in1=g_[:], op=ALU.mult),
                  reads=[v_.b, g_.b], writes=[v_.b])
                bu = bank()
                buv = bu.h[:].rearrange("p (h d) -> p h d", h=8)
                for hd in range(8):
                    P.mm(buv[:, hd, :], TTs[:, hd, :], gbf[:, hd, :], True, True, reads=[TTs.b, gbf.b],
                         writes=[bu.b], sig=(hd == 7))
                V(dve, lambda h: h.tensor_copy(out=ubf[:], in_=buv), reads=[bu.b], writes=[ubf.b])
                rel(bu)
                yield
                bhn = bank()
                bhv = bhn.h[:, 0:256].rearrange("p (c d) -> p c d", c=4)
                byp = bank()
                for hd in range(8):
                    c, hf = hd // 2, hd % 2
                    p0 = hf * 64
                    if not first:
                        P.mm(v4(byp)[p0:p0 + 64, c, :], Hb[:, c, :], arz[:, c, hf, 1, :], True, False,
                             reads=[Hb.b, arz.b], writes=[byp.b])
                    P.mm(v4(byp)[p0:p0 + 64, c, :], ubf[:, hd, :], ssb[:, hd, 1, :], first, False,
                         reads=[ubf.b, ssb.b], writes=[byp.b])
                    P.mm(v4(byp)[p0:p0 + 64, c, :], vtok[:, c, p0:p0 + 64], ssb[:, hd, 3, :], False, True,
                         reads=[vtok.b, ssb.b], writes=[byp.b], sig=(hd == 7))
                for hd in range(8):
                    c, hf = hd // 2, hd % 2
                    p0 = hf * 64
                    P.mm(bhv[p0:p0 + 64, c, :], kbh[:, c, 0, p0:p0 + 64], ubf[:, hd, :], True, False,
                         reads=[kbh.b, ubf.b], writes=[bhn.b])
                    P.mm(bhv[p0:p0 + 64, c, :], kbh[:, c, 1, p0:p0 + 64], vtok[:, c, p0:p0 + 64], False, True,
                         reads=[kbh.b, vtok.b], writes=[bhn.b], sig=(hd == 7))
                V(pool, lambda h: h.tensor_tensor(out=Hs[:], in0=Hs[:], in1=bc(wcs[:], [128, 4, 64]), op=ALU.mult),
                  reads=[Hs.b, wcs.b], writes=[Hs.b])
                V(dve, lambda h: h.tensor_tensor(out=Hs[:], in0=bhv, in1=Hs[:], op=ALU.add),
                  reads=[bhn.b, Hs.b], writes=[Hs.b])
                rel(bhn)
                V(pool, lambda h: h.tensor_copy(out=Hb[:], in_=Hs[:]), reads=[Hs.b], writes=[Hb.b])
                yield
                ysb, yc = lw, a_
                V(act, lambda h: h.copy(out=ysb[:], in_=v4(byp)), reads=[byp.b], writes=[ysb.b])
                rel(byp)
                V(pool, lambda h: h.tensor_copy(out=hb16[:], in_=ysb[:]), reads=[ysb.b], writes=[hb16.b])
                bm = bank()
                for c in range(4):
                    P.mm(v4(bm)[:, c, :], ones64[:], hb16[:, c, :], True, True, reads=[ones64.b, hb16.b],
                         writes=[bm.b], sig=(c == 3))
                V(dve, lambda h: h.tensor_tensor(out=yc[:], in0=ysb[:], in1=v4(bm), op=ALU.subtract),
                  reads=[ysb.b, bm.b], writes=[yc.b])
                rel(bm)
                yield
                V(pool, lambda h: h.tensor_tensor(out=hb16[:], in0=yc[:], in1=yc[:], op=ALU.mult),
                  reads=[yc.b], writes=[hb16.b])
                bvar = bank()
                for c in range(4):
                    P.mm(v4(bvar)[:, c, :], ones64[:], hb16[:, c, :], True, True, reads=[ones64.b, hb16.b],
                         writes=[bvar.b], sig=(c == 3))
                rsqrt_inplace(t1, lambda: t1[:], lambda: v4(bvar), [bvar.b], 1.0, GN_EPS)
                rel(bvar)
                yield
                V(dve, lambda h: h.tensor_tensor(out=yc[:], in0=yc[:], in1=t1[:], op=ALU.mult),
                  reads=[yc.b, t1.b], writes=[yc.b])
                V(dve, lambda h: h.tensor_tensor(out=yc[:], in0=yc[:], in1=r_[:], op=ALU.mult),
                  reads=[yc.b, r_.b], writes=[yc.b])
                yield

            def rw_gate_lora():
                bz3 = bank()
                for c in range(4):
                    P.mm(v4(bz3)[:, c, :], lora_g[:, c * 128:(c + 1) * 128], sgb[:], True, True,
                         reads=[lora_g.b, sgb.b], writes=[bz3.b], sig=(c == 3))
                V(act, lambda h: h.copy(out=g_[:], in_=v4(bz3)), reads=[bz3.b], writes=[g_.b])
                rel(bz3)

            def rw_finish():
                yc = a_
                V(dve, lambda h: h.tensor_tensor(out=ywT[:], in0=yc[:], in1=v_[:], op=ALU.add),
                  reads=[yc.b, v_.b], writes=[ywT.b])

            def gen_back(si):
                b, n = steps[si]
                slot = si % 2
                xcur = xt[slot]
                uTc = uT[slot]
                aoTc = aoT[slot]
                for fp in range(4):
                    bgt = bank()
                    for j in range(4):
                        col0 = (2560 if j < 2 else 3584) + (2 * fp + (j % 2)) * 128
                        for kc in range(8):
                            P.mm(v4(bgt)[:, j, :], w_in_sb[:, kc, col0:col0 + 128], uTc[:, kc, :], kc == 0, kc == 7,
                                 reads=[*w_in_sb.bs, uTc.b], writes=[bgt.b], sig=(j == 3 and kc == 7))
                        if j == 1:
                            yield
                    sigmoid_inplace(sgt, lambda: sgt[:], lambda bgt=bgt: v4(bgt), [bgt.b])
                    rel(bgt)
                    yield
                    byy = bank()
                    for j in range(2):
                        for kc in range(4):
                            P.mm(v4(byy)[:, j, :], w_ba[:, kc, (2 * fp + j) * 128:(2 * fp + j + 1) * 128], aoTc[:, kc, :],
                                 kc == 0, kc == 3, reads=[w_ba.b, aoTc.b], writes=[byy.b], sig=False)
                    for j in range(2):
                        for kc in range(4):
                            P.mm(v4(byy)[:, 2 + j, :], w_br[:, kc, (2 * fp + j) * 128:(2 * fp + j + 1) * 128],
                                 ywT[:, kc, :], kc == 0, kc == 3, reads=[w_br.b, ywT.b], writes=[byy.b],
                                 sig=(j == 1 and kc == 3))
                    V(dve, lambda h, byy=byy: h.tensor_tensor(out=sgt[:], in0=sgt[:], in1=v4(byy), op=ALU.mult),
                      reads=[sgt.b, byy.b], writes=[sgt.b])
                    rel(byy)
                    V(pool, lambda h, fp=fp: h.tensor_tensor(out=mixT[:, 2 * fp:2 * fp + 2, :], in0=sgt[:, 0:2, :],
                                                             in1=sgt[:, 2:4, :], op=ALU.add),
                      reads=[sgt.b], writes=[mixT.b])
                    yield
                for hh in range(2):
                    bw = bank()
                    for kc in range(8):
                        P.mm(bw[:], mixT[:, kc, :], w_out_sb[:, kc, hh * 512:(hh + 1) * 512], kc == 0, kc == 7,
                             reads=[mixT.b, w_out_sb.b], writes=[bw.b])
                    V(dve, lambda h, hh=hh, bw=bw: h.tensor_tensor(out=xcur[:, hh * 512:(hh + 1) * 512], in0=bw[:],
                                                                   in1=xcur[:, hh * 512:(hh + 1) * 512], op=ALU.add),
                      reads=[bw.b, xcur.b], writes=[xcur.b])
                    rel(bw)
                    yield
                P.dma(sp, h_d[b, n * 128:(n + 1) * 128, :], xcur[:], d_st[slot], reads=[xcur.b])
                yield

            def gen_rw_full(si):
                g = gen_rw(si)
                cnt = 0
                for _ in g:
                    cnt += 1
                    if cnt == 6:
                        rw_gate_lora()
                    yield
                rw_finish()
                yield

            def chain_gens(*gs):
                for g in gs:
                    if g is not None:
                        yield from g

            def interleave(g1, g2, r1=1, r2=1):
                a1 = a2 = True
                while a1 or a2:
                    for _ in range(r1):
                        if a1:
                            try:
                                next(g1)
                            except StopIteration:
                                a1 = False
                    for _ in range(r2):
                        if a2:
                            try:
                                next(g2)
                            except StopIteration:
                                a2 = False

            def run_all(g):
                for _ in g:
                    pass

            NS = len(steps)
            run_all(gen_front(0))
            for si in range(NS):
                g2 = chain_gens(gen_back(si - 1) if si > 0 else None,
                                gen_front(si + 1) if si + 1 < NS else None)
                interleave(gen_rw_full(si), g2)
            run_all(gen_back(NS - 1))
            P.barrier()
            print("phase1 stats", P.stats())

        with ExitStack() as st2, suppress(_Stop):
            w1_sb = mk(st2, "w1_sb", [128, 8, DFF], BF16)
            w2_sb = mk(st2, "w2_sb", [128, 32, DM], BF16)
            stg2 = [mk(st2, f"stg2_{i}", [128, 2048]) for i in range(2)]
            d_stg2 = [P.dsem(f"d_stg2_{i}") for i in range(2)]
            w1_sb.b2 = Buf("w1_odd")
            w1_sb.bs = [w1_sb.b, w1_sb.b2]
            w2_sb.b2 = Buf("w2_odd")
            w2_sb.bs = [w2_sb.b, w2_sb.b2]
            for kc in range(8):
                if kc % 2 == 0:
                    load_cast(w1_sb, w1_sb[:, kc, :], w1_d[kc * 128:(kc + 1) * 128, :], DFF)
                else:
                    load_cast_sp(w1_sb.b2, w1_sb[:, kc, :], w1_d[kc * 128:(kc + 1) * 128, :], DFF, stg2, d_stg2)
            for fc in range(32):
                if fc % 2 == 0:
                    load_cast(w2_sb, w2_sb[:, fc, :], w2_d[fc * 128:(fc + 1) * 128, :], DM)
                else:
                    load_cast_sp(w2_sb.b2, w2_sb[:, fc, :], w2_d[fc * 128:(fc + 1) * 128, :], DM, stg2, d_stg2)
            NT = 2
            ht = [mk(st2, f"ht{i}", [128, NT, DM]) for i in range(2)]
            xn2 = mk(st2, "xn2", [128, DM], BF16)
            sm2 = mk(st2, "sm2", [128, 4])
            u2T = mk(st2, "u2T", [128, 8, NT * 128], BF16)
            hT = mk(st2, "hT", [128, 32, NT * 128], BF16)
            print("sbuf remaining after phase2 alloc:", nc.sbuf_bytes_remaining)
            d_h = [P.dsem(f"d_h{i}") for i in range(2)]
            d_y = [P.dsem(f"d_y{i}") for i in range(2)]
            steps2 = [(b, n2) for b in range(NB) for n2 in range(nsteps // NT)]

            def load_h(b, n2, slot):
                for j in range(NT):
                    r0 = (n2 * NT + j) * 128
                    P.dma(sp, ht[slot][:, j, :], h_d[b, r0:r0 + 128, :], d_h[slot], writes=[ht[slot].b])

            load_h(steps2[0][0], steps2[0][1], 0)
            for si, (b, n2) in enumerate(steps2):
                slot = si % 2
                hcur = ht[slot]
                if si + 1 < len(steps2):
                    load_h(steps2[si + 1][0], steps2[si + 1][1], 1 - slot)
                for j in range(NT):
                    V(dve, lambda h: h.memset(sm2[:, 0:1], 0.0), writes=[sm2.b])
                    V(act, lambda h, j=j: h.activation(out=xn2[:], in_=hcur[:, j, :], func=AF.Square,
                                                       accum_out=sm2[:, 0:1]),
                      reads=[hcur.b, sm2.b], writes=[xn2.b, sm2.b])
                    V(act, lambda h: h.activation(out=sm2[:, 0:1], in_=sm2[:, 0:1], func=AF.Ln, bias=RMS_EPS,
                                                  scale=1.0 / DM), reads=[sm2.b], writes=[sm2.b])
                    V(act, lambda h: h.activation(out=sm2[:, 0:1], in_=sm2[:, 0:1], func=AF.Exp, scale=-0.5),
                      reads=[sm2.b], writes=[sm2.b])
                    V(dve, lambda h, j=j: h.tensor_scalar(out=xn2[:], in0=hcur[:, j, :], scalar1=sm2[:, 0:1],
                                                          scalar2=None, op0=ALU.mult),
                      reads=[hcur.b, sm2.b], writes=[xn2.b])
                    bk_ = bank()
                    for c in range(8):
                        P.tr(bf(bk_)[:, c * 128:(c + 1) * 128], xn2[:, c * 128:(c + 1) * 128], ident[:],
                             reads=[xn2.b, ident.b], writes=[bk_.b], sig=(c == 7))
                    o, _ = PC["g2"]
                    V(dve, lambda h, j=j, bk_=bk_: h.tensor_tensor(
                        out=u2T[:, :, j * 128:(j + 1) * 128], in0=bf(bk_).rearrange("p (c t) -> p c t", c=8),
                        in1=bc(prm[:, o:o + 8].unsqueeze(2), [128, 8, 128]), op=ALU.mult),
                      reads=[bk_.b, prm.b], writes=[u2T.b])
                    rel(bk_)
                W = NT * 128
                for fq in range(16):
                    bff = bank()
                    for j in range(2):
                        fc = 2 * fq + j
                        for kc in range(8):
                            P.mm(bff[:, j * W:(j + 1) * W], w1_sb[:, kc, fc * 128:(fc + 1) * 128], u2T[:, kc, :],
                                 kc == 0, kc == 7, reads=[*w1_sb.bs, u2T.b], writes=[bff.b],
                                 sig=(j == 1 and kc == 7))
                    V(act, lambda h, fq=fq, bff=bff: h.activation(
                        out=hT[:, 2 * fq:2 * fq + 2, :].rearrange("p a t -> p (a t)"), in_=bff[:, 0:2 * W],
                        func=AF.Relu), reads=[bff.b], writes=[hT.b])
                    rel(bff)
                    V(pool, lambda h, fq=fq: h.tensor_tensor(out=hT[:, 2 * fq:2 * fq + 2, :],
                                                             in0=hT[:, 2 * fq:2 * fq + 2, :],
                                                             in1=hT[:, 2 * fq:2 * fq + 2, :], op=ALU.mult),
                      reads=[hT.b], writes=[hT.b])
                for j in range(NT):
                    for hh in range(2):
                        bo2 = bank()
                        for fc in range(32):
                            P.mm(bo2[:], hT[:, fc, j * 128:(j + 1) * 128], w2_sb[:, fc, hh * 512:(hh + 1) * 512],
                                 fc == 0, fc == 31, reads=[hT.b, *w2_sb.bs], writes=[bo2.b])
                        V(dve, lambda h, j=j, hh=hh, bo2=bo2: h.tensor_tensor(
                            out=hcur[:, j, hh * 512:(hh + 1) * 512], in0=bo2[:],
                            in1=hcur[:, j, hh * 512:(hh + 1) * 512], op=ALU.add),
                          reads=[bo2.b, hcur.b], writes=[hcur.b])
                        rel(bo2)
                    r0 = (n2 * NT + j) * 128
                    P.dma(sp, y_d[b, r0:r0 + 128, :], hcur[:, j, :], d_y[slot], reads=[hcur.b])
            for i in range(2):
                if d_y[i].cnt > 0:
                    P._wait(sp, (d_y[i], d_y[i].idx, d_y[i].cnt))
            print("final stats", P.stats())
    return nc


def _pack_params(inp):
    prm = np.zeros((128, NPRM), np.float32)

    def put(name, vec):
        o, w = PC[name]
        prm[:, o:o + w] = np.asarray(vec, np.float32).reshape(w, 128).T

    put("g1", inp["norm1_gain"][0])
    put("g2", inp["norm2_gain"][0])
    put("mu_r", inp["mu_r"][0])
    put("mu_k", inp["mu_k"][0])
    put("mu_v", inp["mu_v"][0])
    put("mu2", np.concatenate([inp["mu_w"][0], inp["mu_a"][0], inp["mu_g"][0]]))
    put("db", inp["decay_bias"][0])
    put("ab", inp["aaa_bias"][0])
    put("k_k", inp["k_k"][0])
    put("k_a", inp["k_a"][0])
    put("r_k", inp["r_k"][0])
    put("lng", inp["ln_x_gain"][0])
    put("lnb", inp["ln_x_bias"][0])
    put("qg", np.concatenate([inp["q_norm_gain"][0]] * 2))
    put("kg", np.concatenate([inp["k_norm_gain"][0]] * 2))
    return prm


_NC_CACHE = {}


def kernel(**inputs):
    inp = {k: np.asarray(v) for k, v in inputs.items()}
    x = np.ascontiguousarray(inp["x"], dtype=np.float32)
    prm = _pack_params(inp)
    shared = {
        "prm": prm,
        "attn_sinks": np.ascontiguousarray(inp["attn_sinks"], np.float32).reshape(1, 8),
        "w_in": np.ascontiguousarray(inp["w_in"][0], np.float32),
        "decay_up": np.ascontiguousarray(inp["decay_up"][0], np.float32),
        "aaa_up": np.ascontiguousarray(inp["aaa_up"][0], np.float32),
        "gate_up": np.ascontiguousarray(inp["gate_up"][0], np.float32),
        "w_branch_attn": np.ascontiguousarray(inp["w_branch_attn"][0], np.float32),
        "w_branch_rwkv": np.ascontiguousarray(inp["w_branch_rwkv"][0], np.float32),
        "w_out": np.ascontiguousarray(inp["w_out"][0], np.float32),
        "w_ff_in": np.ascontiguousarray(inp["w_ff_in"][0], np.float32),
        "w_ff_out": np.ascontiguousarray(inp["w_ff_out"][0], np.float32),
    }
    if "nc" not in _NC_CACHE:
        _NC_CACHE["nc"] = build_program()
    nc = _NC_CACHE["nc"]
    in_maps = []
    for c in range(NCORES):
        m = dict(shared)
        m["x"] = np.ascontiguousarray(x[c * NB:(c + 1) * NB])
        in_maps.append(m)
    res = run_bass_kernel_spmd(nc, in_maps, core_ids=list(range(NCORES)))
    out = np.concatenate([np.asarray(r["y"], np.float32).reshape(NB, SEQ, DM) for r in res.results], axis=0)
    return out.astype(np.float32)
```
